# Optimizing a Trainium2 kernel written in Bass

```python
import jax
import jax.numpy as jnp
from jax import lax
import numpy as np

D_MODEL = 1024
BATCH = 8
SEQ = 4096
DEPTH = 4

GRID_W = 64
CTX_LEN = 256
N_Q_HEADS = 8
N_KV_HEADS = 2
HEAD_DIM = 64
Q_GROUP = N_Q_HEADS // N_KV_HEADS
WINDOW = 128
BLOCK = 128
ROPE_THETA = 10000.0
POOL_SIZES = (2, 4, 8, 16)
N_POOL_GROUPS = len(POOL_SIZES)
POOL_GROUP_DIM = D_MODEL // 8
POOL_WIDTH = N_POOL_GROUPS * POOL_GROUP_DIM
Q_WIDTH = N_Q_HEADS * HEAD_DIM
KV_WIDTH = N_KV_HEADS * HEAD_DIM
N_BRANCHES = 2
IN_WIDTH = Q_WIDTH + 2 * KV_WIDTH + POOL_WIDTH + N_BRANCHES * D_MODEL
D_FF = 2816
CONV_WIDTH = 3
N_MOD = 6
EPS = 1e-6
NEG_INF = -1e30

kernel_name = 'hybrid_dit_window_gqa_pool_convffn'


def rms_norm(x, g):
    xf = x.astype(jnp.float32)
    y = xf * lax.rsqrt(jnp.mean(xf * xf, axis=-1, keepdims=True) + EPS)
    return (y * g.astype(jnp.float32)).astype(x.dtype)


def modulate(x, g, shift, scale):
    return rms_norm(x, g) * (1 + scale) + shift


def adaln(cond, w_mod_l, b_mod_l):
    m = jax.nn.silu(cond) @ w_mod_l + b_mod_l
    return jnp.split(m, N_MOD, axis=-1)


def head_rms_norm(t, g):
    tf = t.astype(jnp.float32)
    y = tf * lax.rsqrt(jnp.mean(tf * tf, axis=-1, keepdims=True) + EPS)
    return (y * g.astype(jnp.float32)).astype(t.dtype)


def axial_rope_tables(rows):
    row = jnp.repeat(jnp.arange(rows, dtype=jnp.int32), GRID_W).astype(jnp.float32)
    col = jnp.tile(jnp.arange(GRID_W, dtype=jnp.int32), rows).astype(jnp.float32)
    n_freq = HEAD_DIM // 4
    inv = ROPE_THETA ** (-jnp.arange(n_freq, dtype=jnp.float32) / n_freq)
    ang = jnp.concatenate([row[:, None] * inv, col[:, None] * inv], axis=-1)
    return jnp.cos(ang), jnp.sin(ang)


def apply_axial_rope(t, cos, sin):
    B, N, H, _ = t.shape
    n_freq = HEAD_DIM // 4
    tf = t.astype(jnp.float32).reshape(B, N, H, 2, 2, n_freq)
    cb = cos.reshape(N, 2, n_freq)[None, :, None]
    sb = sin.reshape(N, 2, n_freq)[None, :, None]
    t1 = tf[..., 0, :]
    t2 = tf[..., 1, :]
    out = jnp.stack([t1 * cb - t2 * sb, t2 * cb + t1 * sb], axis=-2)
    return out.reshape(B, N, H, HEAD_DIM).astype(t.dtype)


def split_projection(z):
    B, N, _ = z.shape
    cuts = [Q_WIDTH, Q_WIDTH + KV_WIDTH, Q_WIDTH + 2 * KV_WIDTH, Q_WIDTH + 2 * KV_WIDTH + POOL_WIDTH]
    q, k, v, p, gate = jnp.split(z, cuts, axis=-1)
    q = q.reshape(B, N, N_Q_HEADS, HEAD_DIM)
    k = k.reshape(B, N, N_KV_HEADS, HEAD_DIM)
    v = v.reshape(B, N, N_KV_HEADS, HEAD_DIM)
    return q, k, v, p, gate


def windowed_attention(q, k, v, kc, vc, sink):
    B, N = q.shape[:2]
    nb = N // BLOCK
    scale = HEAD_DIM ** -0.5
    qb = q.reshape(B, nb, BLOCK, N_KV_HEADS, Q_GROUP, HEAD_DIM)
    pad = ((0, 0), (BLOCK, BLOCK), (0, 0), (0, 0))
    kp = jnp.pad(k, pad).reshape(B, nb + 2, BLOCK, N_KV_HEADS, HEAD_DIM)
    vp = jnp.pad(v, pad).reshape(B, nb + 2, BLOCK, N_KV_HEADS, HEAD_DIM)
    kb = jnp.concatenate([kp[:, :-2], kp[:, 1:-1], kp[:, 2:]], axis=2)
    vb = jnp.concatenate([vp[:, :-2], vp[:, 1:-1], vp[:, 2:]], axis=2)
    s_loc = jnp.einsum('bnqhgd,bnkhd->bnhgqk', qb, kb, preferred_element_type=jnp.float32) * scale
    s_ctx = jnp.einsum('bnqhgd,bchd->bnhgqc', qb, kc, preferred_element_type=jnp.float32) * scale
    blk = jnp.arange(nb)[:, None, None]
    qi = jnp.arange(BLOCK)[None, :, None]
    kj = jnp.arange(3 * BLOCK)[None, None, :]
    q_pos = blk * BLOCK + qi
    k_pos = (blk - 1) * BLOCK + kj
    valid = (jnp.abs(k_pos - q_pos) <= WINDOW) & (k_pos >= 0) & (k_pos < N)
    s_loc = jnp.where(valid[None, :, None, None], s_loc, NEG_INF)
    sink_b = sink.astype(jnp.float32).reshape(N_KV_HEADS, Q_GROUP)[None, None, :, :, None, None]
    m = jnp.maximum(jnp.maximum(jnp.max(s_loc, axis=-1, keepdims=True),
                                jnp.max(s_ctx, axis=-1, keepdims=True)), sink_b)
    e_loc = jnp.exp(s_loc - m)
    e_ctx = jnp.exp(s_ctx - m)
    denom = jnp.sum(e_loc, axis=-1, keepdims=True) + jnp.sum(e_ctx, axis=-1, keepdims=True) + jnp.exp(sink_b - m)
    o = (jnp.einsum('bnhgqk,bnkhd->bnhgqd', e_loc, vb.astype(jnp.float32))
         + jnp.einsum('bnhgqc,bchd->bnhgqd', e_ctx, vc.astype(jnp.float32))) / denom
    o = jnp.transpose(o, (0, 1, 4, 2, 3, 5))
    return o.reshape(B, N, Q_WIDTH).astype(q.dtype)


def context_attention(qc, kc, vc, sink):
    B, C = qc.shape[:2]
    scale = HEAD_DIM ** -0.5
    qg = qc.reshape(B, C, N_KV_HEADS, Q_GROUP, HEAD_DIM)
    s = jnp.einsum('bqhgd,bkhd->bhgqk', qg, kc, preferred_element_type=jnp.float32) * scale
    sink_b = jnp.broadcast_to(sink.astype(jnp.float32).reshape(N_KV_HEADS, Q_GROUP)[None, :, :, None, None],
                              s.shape[:-1] + (1,))
    p = jax.nn.softmax(jnp.concatenate([s, sink_b], axis=-1), axis=-1)[..., :-1]
    o = jnp.einsum('bhgqk,bkhd->bqhgd', p, vc.astype(jnp.float32))
    return o.reshape(B, C, Q_WIDTH).astype(qc.dtype)


def multiscale_pool(p):
    B, N, _ = p.shape
    pf = p.astype(jnp.float32)
    cs = jnp.concatenate([jnp.zeros((B, 1, POOL_WIDTH), jnp.float32), jnp.cumsum(pf, axis=1)], axis=1)
    pos = jnp.arange(N)
    outs = []
    for gi, w in enumerate(POOL_SIZES):
        cs_g = cs[..., gi * POOL_GROUP_DIM:(gi + 1) * POOL_GROUP_DIM]
        lo = jnp.clip(pos - w // 2, 0, N)
        hi = jnp.clip(pos + (w - w // 2), 0, N)
        s = jnp.take(cs_g, hi, axis=1) - jnp.take(cs_g, lo, axis=1)
        outs.append(s / (hi - lo).astype(jnp.float32)[None, :, None])
    pooled = jnp.concatenate(outs, axis=-1)
    return (pooled - pf).astype(p.dtype)


def pool_branch(p, w_pool_l, pool_scale_l):
    B, N, _ = p.shape
    d = multiscale_pool(p).reshape(B, N, N_POOL_GROUPS, POOL_GROUP_DIM)
    y = jnp.einsum('bngc,gcd->bngd', d, w_pool_l).reshape(B, N, POOL_WIDTH)
    return y * pool_scale_l


def merge_branches(attn, pool_out, gate, w_br_attn_l, w_br_pool_l, w_out_l):
    g = jax.nn.sigmoid(gate.astype(jnp.float32)).astype(attn.dtype)
    g_attn, g_pool = jnp.split(g, N_BRANCHES, axis=-1)
    y = g_attn * (attn @ w_br_attn_l) + g_pool * (pool_out @ w_br_pool_l)
    return y @ w_out_l


def conv_ffn(h, w_up_l, conv_w_l, conv_b_l, w_down_l):
    N = h.shape[1]
    u = h @ w_up_l
    half = CONV_WIDTH // 2
    up = jnp.pad(u, ((0, 0), (half, half), (0, 0)))
    uc = conv_b_l + up[:, 0:N] * conv_w_l[0]
    for j in range(1, CONV_WIDTH):
        uc = uc + up[:, j:j + N] * conv_w_l[j]
    a, b = jnp.split(uc, 2, axis=-1)
    return (jax.nn.silu(a) * b) @ w_down_l


def setup_inputs(seed: int = 0) -> dict:
    key = jax.random.key(seed)
    ks = jax.random.split(key, 24)
    f32 = jnp.float32
    D = D_MODEL

    def nrm(k, shape, s):
        return jax.random.normal(k, shape, f32) * s

    return {
        'x': nrm(ks[0], (BATCH, SEQ, D), 1.0),
        'c': nrm(ks[1], (BATCH, D), 1.0),
        'ctx': nrm(ks[2], (BATCH, CTX_LEN, D), 1.0),
        'c_ctx': nrm(ks[3], (D,), 1.0),
        'w_mod': nrm(ks[4], (DEPTH, D, N_MOD * D), 0.5 * D ** -0.5),
        'b_mod': nrm(ks[5], (DEPTH, N_MOD * D), 0.01),
        'norm1_g': 1.0 + nrm(ks[6], (DEPTH, D), 0.02),
        'norm2_g': 1.0 + nrm(ks[7], (DEPTH, D), 0.02),
        'w_in': nrm(ks[8], (DEPTH, D, IN_WIDTH), D ** -0.5),
        'q_gain': 1.0 + nrm(ks[9], (DEPTH, HEAD_DIM), 0.02),
        'k_gain': 1.0 + nrm(ks[10], (DEPTH, HEAD_DIM), 0.02),
        'sink': nrm(ks[11], (DEPTH, N_Q_HEADS), 0.5),
        'w_pool': nrm(ks[12], (DEPTH, N_POOL_GROUPS, POOL_GROUP_DIM, POOL_GROUP_DIM), POOL_GROUP_DIM ** -0.5),
        'pool_scale': 1.0 + nrm(ks[13], (DEPTH, POOL_WIDTH), 0.1),
        'w_br_attn': nrm(ks[14], (DEPTH, Q_WIDTH, D), Q_WIDTH ** -0.5),
        'w_br_pool': nrm(ks[15], (DEPTH, POOL_WIDTH, D), POOL_WIDTH ** -0.5),
        'w_out': nrm(ks[16], (DEPTH, D, D), D ** -0.5),
        'w_up': nrm(ks[17], (DEPTH, D, 2 * D_FF), D ** -0.5),
        'conv_w': nrm(ks[18], (DEPTH, CONV_WIDTH, 2 * D_FF), CONV_WIDTH ** -0.5),
        'conv_b': nrm(ks[19], (DEPTH, 2 * D_FF), 0.01),
        'w_down': nrm(ks[20], (DEPTH, D_FF, D), D_FF ** -0.5),
    }


def reference(x, c, ctx, c_ctx, w_mod, b_mod, norm1_g, norm2_g, w_in, q_gain, k_gain, sink,
              w_pool, pool_scale, w_br_attn, w_br_pool, w_out, w_up, conv_w, conv_b, w_down):
    B, n_tok = x.shape[0], x.shape[1]
    C = ctx.shape[1]
    rows = n_tok // GRID_W
    cos, sin = axial_rope_tables(rows)
    xc = ctx
    for l in range(DEPTH):
        last = l == DEPTH - 1
        sh1, sc1, g1, sh2, sc2, g2 = [t[:, None, :] for t in adaln(c, w_mod[l], b_mod[l])]
        csh1, csc1, cg1, csh2, csc2, cg2 = adaln(c_ctx, w_mod[l], b_mod[l])

        hc = modulate(xc, norm1_g[l], csh1, csc1)
        if last:
            kc, vc = jnp.split(hc @ w_in[l][:, Q_WIDTH:Q_WIDTH + 2 * KV_WIDTH], 2, axis=-1)
            kc = head_rms_norm(kc.reshape(B, C, N_KV_HEADS, HEAD_DIM), k_gain[l])
            vc = vc.reshape(B, C, N_KV_HEADS, HEAD_DIM)
        else:
            qc, kc, vc, pc, gatec = split_projection(hc @ w_in[l])
            qc = head_rms_norm(qc, q_gain[l])
            kc = head_rms_norm(kc, k_gain[l])
            mix_c = merge_branches(context_attention(qc, kc, vc, sink[l]),
                                   pool_branch(pc, w_pool[l], pool_scale[l]),
                                   gatec, w_br_attn[l], w_br_pool[l], w_out[l])
            xc_mid = xc + cg1 * mix_c
            xc_next = xc_mid + cg2 * conv_ffn(modulate(xc_mid, norm2_g[l], csh2, csc2),
                                              w_up[l], conv_w[l], conv_b[l], w_down[l])

        h = modulate(x, norm1_g[l], sh1, sc1)
        q, k, v, p, gate = split_projection(h @ w_in[l])
        q = apply_axial_rope(head_rms_norm(q, q_gain[l]), cos, sin)
        k = apply_axial_rope(head_rms_norm(k, k_gain[l]), cos, sin)
        attn = windowed_attention(q, k, v, kc, vc, sink[l])
        pool_out = pool_branch(p, w_pool[l], pool_scale[l])
        x = x + g1 * merge_branches(attn, pool_out, gate, w_br_attn[l], w_br_pool[l], w_out[l])

        x = x + g2 * conv_ffn(modulate(x, norm2_g[l], sh2, sc2), w_up[l], conv_w[l], conv_b[l], w_down[l])

        if not last:
            xc = xc_next
    return x
```

```python
import numpy as np
from contextlib import ExitStack
import concourse.bass as bass
import concourse.mybir as mybir
from concourse.bass_utils import run_bass_kernel_spmd

F32 = mybir.dt.float32
BF16 = mybir.dt.bfloat16
AF = mybir.ActivationFunctionType
ALU = mybir.AluOpType

D = 1024
SEQ = 4096
CTX = 256
DEPTH = 4
NCORES = 8
GRID_W = 64
HD = 64
DFF = 2816
NFC = DFF // 128
EPS = 1e-6
G = 512
NG = SEQ // G
RING = 12
NSLOT = 7
SLOT_E = 2048
NPL = 256

PIECES = []
for _i in range(13):
    PIECES.append((f"in{_i}", 2048))
PIECES.append(("pool", 512))
PIECES += [("bra0", 2048), ("bra1", 2048), ("brp0", 2048), ("brp1", 2048)]
PIECES += [(f"out{_i}", 2048) for _i in range(4)]
PIECES += [(f"up{_i}", 2048) for _i in range(NFC)]
for _f in range(4):
    PIECES += [(f"dn{_f}_0", 2048), (f"dn{_f}_1", 2048), (f"dn{_f}_2", 1536)]
POFF = {}
_o = 0
for _n, _e in PIECES:
    POFF[_n] = (_o, _e)
    _o += 128 * _e
WSTREAM_LEN = _o


def _piece(W, kchunks, cols):
    K = W.shape[0] // 128
    Wr = W.reshape(K, 128, W.shape[1])[kchunks][:, :, cols]
    return np.ascontiguousarray(Wr.transpose(1, 0, 2)).reshape(128, -1)


def _qperm():
    idx = []
    for cq in range(4):
        for half in range(2):
            h = cq + 4 * half
            idx += list(range(h * 64, (h + 1) * 64))
    return np.array(idx)


def build_wstream(l, w_in, w_pool, w_br_attn, w_br_pool, w_out, w_up, w_down):
    out = np.empty(WSTREAM_LEN, np.float32)

    def put(name, arr):
        o, e = POFF[name]
        assert arr.shape == (128, e), (name, arr.shape, e)
        out[o:o + 128 * e] = arr.reshape(-1)

    qp = _qperm()
    cols = np.concatenate([np.arange(512, 640), np.arange(640, 768), qp, np.arange(768, 1280), np.arange(1280, 3328)])
    Wp = w_in[l][:, cols]
    k8 = list(range(8))
    for i in range(13):
        put(f"in{i}", _piece(Wp, k8, np.arange(i * 256, (i + 1) * 256)))
    put("pool", np.ascontiguousarray(w_pool[l].transpose(1, 0, 2)).reshape(128, 512))
    bra = w_br_attn[l][qp, :]
    brp = w_br_pool[l]
    for hf in range(2):
        put(f"bra{hf}", _piece(bra, [0, 1, 2, 3], np.arange(hf * 512, (hf + 1) * 512)))
        put(f"brp{hf}", _piece(brp, [0, 1, 2, 3], np.arange(hf * 512, (hf + 1) * 512)))
    for i in range(4):
        put(f"out{i}", _piece(w_out[l], k8, np.arange(i * 256, (i + 1) * 256)))
    for i in range(NFC):
        cc = np.concatenate([np.arange(i * 128, (i + 1) * 128), np.arange(DFF + i * 128, DFF + (i + 1) * 128)])
        put(f"up{i}", _piece(w_up[l], k8, cc))
    for f in range(4):
        cc = np.arange(f * 256, (f + 1) * 256)
        put(f"dn{f}_0", _piece(w_down[l], list(range(0, 8)), cc))
        put(f"dn{f}_1", _piece(w_down[l], list(range(8, 16)), cc))
        put(f"dn{f}_2", _piece(w_down[l], list(range(16, 22)), cc))
    return out


def build_wmod_stream(l, w_mod):
    W = w_mod[l].reshape(8, 128, 24, 256)
    return np.ascontiguousarray(W.transpose(2, 1, 0, 3)).reshape(24, 128, 2048)


def build_params(b_mod, norm1_g, norm2_g, pool_scale, conv_w, conv_b, q_gain, k_gain, sink):
    P = np.zeros((128, DEPTH * NPL), np.float32)
    for l in range(DEPTH):
        o = l * NPL
        P[:, o:o + 48] = b_mod[l].reshape(48, 128).T
        P[:, o + 48:o + 56] = norm1_g[l].reshape(8, 128).T
        P[:, o + 56:o + 64] = norm2_g[l].reshape(8, 128).T
        P[:, o + 64:o + 68] = pool_scale[l].reshape(4, 128).T
        for j in range(3):
            P[:, o + 68 + j * 44:o + 68 + (j + 1) * 44] = conv_w[l, j].reshape(44, 128).T
        P[:, o + 200:o + 244] = conv_b[l].reshape(44, 128).T
        P[:, o + 244] = np.tile(q_gain[l], 2)
        P[:, o + 245] = np.tile(k_gain[l], 2)
        P[:, o + 246:o + 254] = sink[l][None, :]
    return P


def build_consts():
    cm = np.zeros((128, 6 * 128), np.float32)
    cm[:, 0:128] = 1.0 / 1024.0
    for hb in (0, 64):
        cm[hb:hb + 64, 128 + hb:128 + hb + 64] = 1.0 / 64.0
    for m in range(128):
        d = m % 64
        half = (d % 32) // 16
        partner = m + 16 if half == 0 else m - 16
        cm[partner, 256 + m] = 1.0
    kj = np.arange(128)[:, None]
    qi = np.arange(128)[None, :]
    cm[:, 384:512] = ((kj <= qi).astype(np.float32) - 1.0) * 30000.0
    cm[:, 512:640] = ((kj >= qi).astype(np.float32) - 1.0) * 30000.0
    cm[:, 640:768] = np.eye(128, dtype=np.float32)
    n_freq = HD // 4
    inv = (np.float32(10000.0) ** (-(np.arange(n_freq, dtype=np.float32)) / np.float32(n_freq))).astype(np.float32)
    t = np.arange(SEQ)
    row = (t // GRID_W).astype(np.float32)
    col = (t % GRID_W).astype(np.float32)
    cosT = np.zeros((128, SEQ), np.float32)
    sinT = np.zeros((128, SEQ), np.float32)
    for p in range(128):
        d = p % 64
        axis = d // 32
        half = (d % 32) // 16
        f = d % 16
        ang = ((row if axis == 0 else col) * inv[f]).astype(np.float32)
        cosT[p] = np.cos(ang).astype(np.float32)
        s = np.sin(ang).astype(np.float32)
        sinT[p] = -s if half == 0 else s
    pc = np.zeros((128, 64), np.float32)
    for c, w in enumerate((2, 4, 8, 16)):
        for i in range(8):
            cntl = (i + w // 2) - max(i - w // 2, 0)
            pc[:, c * 8 + i] = 1.0 / cntl
            tt = -8 + i
            hi = min(tt + w // 2, 0)
            lo = tt - w // 2
            pc[:, 32 + c * 8 + i] = 1.0 / (hi - lo)
    return cm, cosT, sinT, pc


class Tok:
    __slots__ = ("eng", "sem", "val")

    def __init__(self, eng):
        self.eng = eng
        self.sem = None
        self.val = None


ENGS = ("pe", "act", "dve", "pool", "sp")
SEM_ROLL = 30000


class Sched:
    def __init__(self, nc, es):
        self.nc = nc
        self.es = es
        self.prog = {e: [] for e in ENGS}
        self.esem = {}
        self.ecount = {}
        self.nroll = {}
        for e in ("pe", "act", "dve", "pool"):
            self.esem[e] = es.enter_context(nc.semaphore(f"c_{e}_0"))
            self.ecount[e] = 0
            self.nroll[e] = 0
        self.pending = {e: [] for e in ENGS}
        self.waited = {e: {} for e in ENGS}
        self.lastw = {}
        self.readers = {}
        self.dsem = {q: [es.enter_context(nc.semaphore(f"d_{q}_{i}")) for i in range(12)] for q in ("sp", "pool")}
        self.dcount = {q: [0] * 12 for q in ("sp", "pool")}
        self.drr = {"sp": 0, "pool": 0}
        self.ninstr = {e: 0 for e in ENGS}

    def _deps(self, reads, writes):
        toks = []
        for k in reads:
            t = self.lastw.get(k)
            if t is not None:
                toks.append(t)
        for k in writes:
            t = self.lastw.get(k)
            if t is not None:
                toks.append(t)
            toks.extend(self.readers.get(k, ()))
        return toks

    def _waits(self, eng, toks):
        waits = []
        for t in toks:
            assert t.val is not None, "dependency on an unresolved (unsignalled) op"
            if t.eng == "pe" and eng == "pe":
                continue
            sid = id(t.sem)
            if t.val > self.waited[eng].get(sid, 0):
                self.waited[eng][sid] = t.val
                waits.append((t.sem, t.val))
        return waits

    def _commit(self, tok, reads, writes):
        for k in reads:
            self.readers.setdefault(k, []).append(tok)
        for k in writes:
            self.lastw[k] = tok
            self.readers[k] = []

    def op(self, eng, fn, reads=(), writes=(), signal=True):
        waits = self._waits(eng, self._deps(reads, writes))
        tok = Tok(eng)
        sem = None
        if signal:
            if self.ecount[eng] >= SEM_ROLL:
                self.nroll[eng] += 1
                self.esem[eng] = self.es.enter_context(self.nc.semaphore(f"c_{eng}_{self.nroll[eng]}"))
                self.ecount[eng] = 0
            self.ecount[eng] += 1
            sem = self.esem[eng]
            tok.sem, tok.val = sem, self.ecount[eng]
            for p in self.pending[eng]:
                p.sem, p.val = tok.sem, tok.val
            self.pending[eng] = []
        else:
            self.pending[eng].append(tok)

        def run(e, waits=waits, fn=fn, sem=sem):
            for s, v in waits:
                e.wait_ge(s, v)
            ins = fn(e)
            if sem is not None:
                ins.then_inc(sem, 1)

        self.prog[eng].append(run)
        self.ninstr[eng] += 1
        self._commit(tok, reads, writes)
        return tok

    def dma(self, q, out_ap, in_ap, reads=(), writes=(), slow=False):
        waits = self._waits(q, self._deps(reads, writes))
        i = self.drr[q]
        self.drr[q] = (i + 1) % len(self.dsem[q])
        sem = self.dsem[q][i]
        c = self.dcount[q][i]
        if c > self.waited[q].get(id(sem), 0):
            self.waited[q][id(sem)] = c
            waits.append((sem, c))
        self.dcount[q][i] = c + 16
        tok = Tok("dma")
        tok.sem, tok.val = sem, c + 16

        def run(e, waits=waits, sem=sem, out_ap=out_ap, in_ap=in_ap, slow=slow):
            for s, v in waits:
                e.wait_ge(s, v)
            if slow:
                e.dma_start(out=out_ap, in_=in_ap, allow_slow_non_contiguous=True).then_inc(sem, 16)
            else:
                e.dma_start(out=out_ap, in_=in_ap).then_inc(sem, 16)

        self.prog[q].append(run)
        self.ninstr[q] += 1
        self._commit(tok, reads, writes)
        return tok

    def finish(self):
        finals = [(self.dsem["sp"][i], self.dcount["sp"][i]) for i in range(12) if self.dcount["sp"][i] > 0]

        def run(e, finals=finals):
            for s, v in finals:
                e.wait_ge(s, v)

        self.prog["sp"].append(run)


class DrySched:
    def __init__(self):
        self.lastw = {}
        self.prog = {e: [] for e in ENGS}
        self.ninstr = {e: 0 for e in ENGS}

    def op(self, eng, fn, reads=(), writes=(), signal=True):
        t = Tok(eng)
        t.val = 1
        return t

    def dma(self, q, out_ap, in_ap, reads=(), writes=(), slow=False):
        t = Tok("dma")
        t.val = 1
        return t

    def finish(self):
        pass


AHEAD = 4
assert AHEAD + 3 <= NSLOT

class Grp:
    def __init__(self, is_ctx, g):
        self.is_ctx = is_ctx
        self.g = g
        self.n = CTX if is_ctx else G
        self.nb = self.n // 128
        self.slot = 1 if is_ctx else g % 2
        self.xs = 2 if is_ctx else g % 3
        self.t0 = 0 if is_ctx else g * G
        self.si = 1 if is_ctx else 0
        self.first = is_ctx or g == 0
        self.last = is_ctx or g == NG - 1
        if is_ctx:
            self.kslots = [0, 1]
        else:
            self.kslots = [2 + ((4 * g + i) % RING) for i in range(4)]
        self.name = "c" if is_ctx else str(g)


def mslot(b):
    return 2 + (b % RING)


class Builder:
    def __init__(self, n_layers=DEPTH, dbg=False, plan=None):
        self.n_layers = n_layers
        self.dbg = dbg
        self.dry = plan is None
        self.plan = [] if plan is None else plan
        self.pidx = 0
        self.issued = 0
        self.pslots = {}
        self.nc = bass.Bass("TRN2", target_bir_lowering=False)
        self.es = ExitStack()

    def sb(self, name, shape, dt):
        return self.es.enter_context(self.nc.sbuf_tensor(name, shape, dt))

    def dram_in(self, name, shape, dt=F32):
        return self.nc.dram_tensor(name, shape, dt, kind="ExternalInput").ap()

    def build(self):
        nc, es = self.nc, self.es
        L = self.n_layers
        self.xT = self.dram_in("xT", [D, SEQ])
        self.cxT = self.dram_in("cxT", [D, CTX])
        self.cond = self.dram_in("cond", [128, 16])
        self.params_d = self.dram_in("params", [128, DEPTH * NPL])
        self.cmat_d = self.dram_in("cmat", [128, 768])
        self.cos_d = self.dram_in("cosT", [128, SEQ])
        self.sin_d = self.dram_in("sinT", [128, SEQ])
        self.poolc_d = self.dram_in("poolc", [128, 64])
        self.wst = [self.dram_in(f"wst{l}", [WSTREAM_LEN]) for l in range(L)]
        self.wmod = [self.dram_in(f"wmod{l}", [24, 128, 2048]) for l in range(L)]
        self.outT = nc.dram_tensor("outT", [D, SEQ], F32, kind="ExternalOutput").ap()
        self.S = [nc.dram_tensor(f"S{i}", [D, SEQ], F32, kind="ExternalOutput" if self.dbg else "Internal").ap() for i in range(2)]
        self.C = [nc.dram_tensor(f"C{i}", [D, CTX], F32, kind="ExternalOutput" if self.dbg else "Internal").ap() for i in range(2)]

        self.sc = DrySched() if self.dry else Sched(nc, es)
        sb = self.sb
        self.wslot = [sb(f"wslot{i}", [128, SLOT_E], BF16) for i in range(NSLOT)]
        self.wrr = 0
        self.xb = [sb(f"xb{i}", [128, 8, G], F32) for i in range(3)]
        self.hb = [sb(f"hb{i}", [128, 8, G], BF16) for i in range(2)]
        self.ntmp = [sb(f"ntmp{i}", [128, G], F32) for i in range(2)]
        self.rln = sb("rln", [128, G], F32)
        self.rstd = sb("rstd", [128, G], F32)
        self.kT = [sb(f"kT{i}", [128, (2 + RING) * 128], BF16) for i in range(2)]
        self.V = sb("V", [128, 2 + RING, 2, 128], BF16)
        self.qb = sb("qb", [128, 4, G], BF16)
        self.big = sb("big", [128, 24 * G], BF16)
        self.gates = self.big[:, 0:16 * G].rearrange("p (c t) -> p c t", t=G)
        self.yb = self.big[:, 16 * G:24 * G].rearrange("p (c t) -> p c t", t=G)
        self.actT = self.big[:, 0:NFC * G].rearrange("p (c t) -> p c t", t=G)
        self.pT = sb("pT", [128, 4, G + 16], F32)
        self.attn = sb("attn", [128, 4, G], BF16)
        self.dT = sb("dT", [128, 4, G], BF16)
        self.po = sb("po", [128, 4, G], BF16)
        self.f4 = [sb(f"f4_{i}", [128, G + 16], F32) for i in range(4)]
        self.PT = [sb(f"PT{i}", [128, G], BF16) for i in range(3)]
        self.ptr = 0
        self.qsq2 = [sb(f"qsq{i}", [128, G], BF16) for i in range(2)]
        self.qln2 = [sb("qln0", [128, G], F32)] * 2
        self.qrs2 = [sb(f"qrs{i}", [128, G], F32) for i in range(2)]
        self.qn2 = [sb(f"qn{i}", [128, G], BF16) for i in range(2)]
        self.qkpar = 0
        self.deferred = []
        self.cosb = sb("cosb", [128, G], F32)
        self.sinb = sb("sinb", [128, G], F32)
        self.t1b = [sb("t1_0", [128, G], F32)] * 2
        self.t2b = [sb("t2_0", [128, G], F32)] * 2
        self.t1, self.t2 = self.t1b[0], self.t2b[0]
        self.lnden = sb("lnden", [128, G], F32)
        self.rden = sb("rden", [128, G], F32)
        self.sil2 = [sb(f"sil{i}", [128, G], F32) for i in range(2)]
        self.corr = sb("corr", [128, 2 * NFC, 2], F32)
        self.saved = sb("saved", [128, 2 * NFC, 2], F32)
        self.xl = [sb(f"xl{i}", [128, 8], F32) for i in range(2)]
        self.cmat = sb("cmat_s", [128, 768], BF16)
        self.poolc = sb("poolc_s", [128, 64], F32)
        self.par = sb("par_s", [128, DEPTH * NPL], F32)
        self.condb = sb("condb", [128, 16], F32)
        self.scond = sb("scond", [128, 16], BF16)
        self.modL = [sb(f"mod{i}", [128, 2, 48], F32) for i in range(2)]
        self.gs1L = [sb(f"gs1_{i}", [128, 2, 8], F32) for i in range(2)]
        self.gs2L = [sb(f"gs2_{i}", [128, 2, 8], F32) for i in range(2)]
        self.esinkL = [sb(f"esink{i}", [128, 8], F32) for i in range(2)]
        self.tail_a = sb("tail_a", [128, 2 * NFC], F32)
        self.tail_s = sb("tail_s", [128, NFC], F32)
        self.tail_act = sb("tail_act", [128, NFC], BF16)
        self.epsb = sb("epsb", [128, 1], F32)
        self.sbuf_left = nc.sbuf_bytes_remaining
        self.ps = [es.enter_context(nc.psum_tensor(f"ps{i}", [128, 512], F32)) for i in range(8)]
        self.pools = {"mm": [0, 1, 2, 3], "st": [4, 5, 0], "o": [6, 1], "n": [7], "aux": [4, 5, 6, 7], "qk": [4, 5], "pp": [2, 3]}
        self.prr = {k: 0 for k in self.pools}

        sc = self.sc
        sc.dma("pool", self.cmat[:, :], self.cmat_d[:, :], writes=["cmat"])
        sc.dma("sp", self.poolc[:, :], self.poolc_d[:, :], writes=["poolc"])
        sc.dma("sp", self.par[:, :], self.params_d[:, :], writes=["par"])
        sc.dma("sp", self.condb[:, :], self.cond[:, :], writes=["condb"])
        sc.op("dve", lambda e: e.memset(self.kT[0][:, :], 0.0), writes=[("kT", s) for s in range(2 + RING)])
        sc.op("dve", lambda e: e.memset(self.kT[1][:, :], 0.0), writes=[("kT", s) for s in range(2 + RING)])
        sc.op("dve", lambda e: e.memset(self.V[:, :, 0, 64:128], 1.0), writes=[("V", s) for s in range(2 + RING)])
        sc.op("dve", lambda e: e.memset(self.V[:, :, 1, 0:64], 1.0), writes=[("V", s) for s in range(2 + RING)])
        sc.op("act", lambda e: e.activation(out=self.scond[:, :], in_=self.condb[:, :], func=AF.Silu),
              reads=["condb"], writes=["scond"])
        sc.op("dve", lambda e: e.memset(self.epsb[:, :], EPS), writes=["epsb"])

        self.ones_mean = self.cmat[:, 0:128]
        self.bd_mean = self.cmat[:, 128:256]
        self.perm = self.cmat[:, 256:384]
        self.mask_next = self.cmat[:, 384:512]
        self.mask_prev = self.cmat[:, 512:640]
        self.ident = self.cmat[:, 640:768]

        self.l = 0
        for k in range(8):
            self.adaln_part(0, k)
        self.adaln_finish(0)
        for l in range(L):
            self.set_layer(l)
            last = (l == DEPTH - 1)
            self.src_x = self.xT if l == 0 else self.S[(l - 1) % 2]
            self.src_c = self.cxT if l == 0 else self.C[(l - 1) % 2]
            self.dst_x = self.outT if l == L - 1 else self.S[l % 2]
            self.dst_c = self.C[l % 2]
            self.xkey_src = ("X", "in" if l == 0 else (l - 1) % 2)
            self.xkey_dst = ("X", "out" if l == L - 1 else l % 2)
            gc = Grp(True, 0)
            grps = [Grp(False, g) for g in range(NG)]
            self.front_load(gc)
            self.front_load(grps[0])
            self.front_a(gc)
            self.front_b(gc)
            if not last:
                self.back_proj_a(gc, 0)
                self.back_proj_a(gc, 1)
                self.back_rest(gc, None)
                self.ffn(gc)
            self.front_load(grps[1])
            self.front_a(grps[0])
            self.front_b(grps[0])
            for g in range(NG):
                Gr = grps[g]
                nxt = grps[g + 1] if g + 1 < NG else None
                if g + 2 < NG:
                    self.front_load(grps[g + 2])
                self.back_proj_a(Gr, 0)
                if nxt is not None:
                    self.front_a(nxt)
                self.back_proj_a(Gr, 1)
                if nxt is not None:
                    self.front_b(nxt)
                self.back_rest(Gr, nxt)
                if l + 1 < L:
                    self.adaln_part(l + 1, g)
                self.ffn(Gr)
            if l + 1 < L:
                self.adaln_finish(l + 1)
        sc.finish()

        prog = sc.prog
        if self.dry:
            es.close()
            return None
        with nc.Block() as block:
            @block.tensor
            def _(e):
                for f in prog["pe"]:
                    f(e)

            @block.scalar
            def _(e):
                for f in prog["act"]:
                    f(e)

            @block.vector
            def _(e):
                for f in prog["dve"]:
                    f(e)

            @block.gpsimd
            def _(e):
                for f in prog["pool"]:
                    f(e)

            @block.sync
            def _(e):
                for f in prog["sp"]:
                    f(e)
        es.close()
        return nc

    def psum(self, pool):
        banks = self.pools[pool]
        i = self.prr[pool]
        self.prr[pool] = (i + 1) % len(banks)
        b = banks[i]
        return self.ps[b], ("ps", b)

    def _issue_upto(self, k):
        k = min(k, len(self.plan) - 1)
        while self.issued <= k:
            idx = self.issued
            kind, l, nm = self.plan[idx]
            i = idx % NSLOT
            slot = self.wslot[i]
            key = ("w", i)
            if kind == "w":
                o, e = POFF[nm]
                src = self.wst[l][o:o + 128 * e].rearrange("(p e) -> p e", e=e)
                self.sc.dma("pool", slot[:, 0:e], src, writes=[key])
            else:
                self.sc.dma("pool", slot[:, :], self.wmod[l][nm, :, :], writes=[key])
            self.issued += 1

    def wget(self, name, kind="w", layer=None):
        ent = (kind, self.l if layer is None else layer, name)
        if self.dry:
            self.plan.append(ent)
            return self.wslot[0], ("w", 0)
        idx = self.pidx
        assert self.plan[idx] == ent, (self.plan[idx], ent)
        self.pidx += 1
        self._issue_upto(idx + AHEAD)
        i = idx % NSLOT
        return self.wslot[i], ("w", i)

    def pcol(self, off, n=1):
        o = self.l * NPL + off
        return self.par[:, o:o + n]

    def mm_group(self, out_ap, pskey, terms, extra_reads=()):
        sc = self.sc
        nt = len(terms)
        tok = None
        for i, (lh, rh, rk) in enumerate(terms):
            st, sp_ = (i == 0), (i == nt - 1)
            tok = sc.op("pe", lambda e, lh=lh, rh=rh, st=st, sp_=sp_: e.matmul(out_ap, lhsT=lh, rhs=rh, start=st, stop=sp_),
                        reads=list(rk) + (list(extra_reads) if i == 0 else []),
                        writes=[pskey] if i == 0 else [], signal=sp_)
        self.sc.lastw[pskey] = tok
        return tok

    def pcol_l(self, l, off, n=1):
        o = l * NPL + off
        return self.par[:, o:o + n]

    def adaln_part(self, l, k):
        sc = self.sc
        p2 = l % 2
        mod, kmod = self.modL[p2], ("mod", p2)
        ps, pk = self.psum("mm")
        tok = None
        rhs_all = self.scond[:, :].rearrange("p (s k) -> p k s", s=2)
        first = True
        for i2 in range(3 * k, 3 * k + 3):
            slot, key = self.wget(i2, kind="m", layer=l)
            wv = slot[:, :].rearrange("p (k f) -> p k f", f=256)
            for cc in range(2):
                jj = 2 * (i2 - 3 * k) + cc
                for kc in range(8):
                    st, sp_ = (kc == 0), (kc == 7)
                    tok = sc.op("pe", lambda e, wv=wv, kc=kc, jj=jj, cc=cc, st=st, sp_=sp_: e.matmul(
                        ps[:, 2 * jj:2 * jj + 2], lhsT=wv[:, kc, cc * 128:(cc + 1) * 128], rhs=rhs_all[:, kc, :], start=st, stop=sp_),
                        reads=[key, "scond"], writes=[pk] if first else [], signal=sp_)
                    first = False
        sc.lastw[pk] = tok
        bm = self.pcol_l(l, 6 * k, 6)
        for s_ in range(2):
            sc.op("dve", lambda e, s_=s_: e.tensor_tensor(out=mod[:, s_, 6 * k:6 * k + 6], in0=ps[:, s_:12:2], in1=bm, op=ALU.add),
                  reads=["par"], writes=[pk, kmod])

    def adaln_finish(self, l):
        sc = self.sc
        p2 = l % 2
        mod, kmod = self.modL[p2], ("mod", p2)
        for (gsb, sco, ngo, nm) in ((self.gs1L[p2], 8, 48, ("gs1", p2)), (self.gs2L[p2], 32, 56, ("gs2", p2))):
            ng = self.pcol_l(l, ngo, 8)
            for s_ in range(2):
                sc.op("dve", lambda e, s_=s_, gsb=gsb, sco=sco, ng=ng: e.scalar_tensor_tensor(
                    out=gsb[:, s_, :], in0=mod[:, s_, sco:sco + 8], scalar=1.0, in1=ng, op0=ALU.add, op1=ALU.mult),
                    reads=[kmod, "par"], writes=[nm])
        esink = self.esinkL[p2]
        sk = self.pcol_l(l, 246, 8)
        sc.op("act", lambda e: e.activation(out=esink[:, :], in_=sk, func=AF.Exp), reads=["par"], writes=[("esink", p2)])

    def set_layer(self, l):
        p2 = l % 2
        self.l = l
        self.mod, self.gs1, self.gs2, self.esink = self.modL[p2], self.gs1L[p2], self.gs2L[p2], self.esinkL[p2]
        self.kmod, self.kgs1, self.kgs2, self.kesink = ("mod", p2), ("gs1", p2), ("gs2", p2), ("esink", p2)

    def norm_mod(self, Gr, gsb, gsname, sh_off, stats_done=False, interleave=False):
        sc = self.sc
        s, n, si = Gr.slot, Gr.n, Gr.si
        xb, hb = self.xb[Gr.xs], self.hb[s]
        mod, kmod = self.mod, self.kmod
        xks = [("xb", Gr.xs, kc) for kc in range(8)]
        hks = [("hb", s, kc) for kc in range(8)]
        if not stats_done:
            sc.op("act", lambda e: e.activation(out=hb[:, :, 0:n], in_=xb[:, :, 0:n], func=AF.Square), reads=xks, writes=hks)
            ps, pk = self.psum("n")
            self.mm_group(ps[:, 0:n], pk, [(self.ones_mean, hb[:, kc, 0:n], [hks[kc], "cmat"]) for kc in range(8)])
        else:
            ps, pk = self.nstat
        sc.op("act", lambda e: e.activation(out=self.rln[:, 0:n], in_=ps[:, 0:n], func=AF.Ln, bias=self.epsb[:, 0:1], scale=1.0),
              reads=["epsb"], writes=[pk, "rln"])
        sc.op("act", lambda e: e.activation(out=self.rstd[:, 0:n], in_=self.rln[:, 0:n], func=AF.Exp, scale=-0.5),
              reads=["rln"], writes=["rstd"])
        def chunk(kc):
            nt = self.ntmp[kc % 2]
            nk = ("ntmp", kc % 2)
            sc.op("dve", lambda e: e.tensor_tensor(out=nt[:, 0:n], in0=xb[:, kc, 0:n], in1=self.rstd[:, 0:n], op=ALU.mult),
                  reads=[xks[kc], "rstd"], writes=[nk])
            sc.op("act", lambda e: e.activation(out=hb[:, kc, 0:n], in_=nt[:, 0:n], func=AF.Identity,
                                                bias=mod[:, si, sh_off + kc:sh_off + kc + 1], scale=gsb[:, si, kc:kc + 1]),
                  reads=[nk, kmod, gsname], writes=[hks[kc]])

        for kc in range(8):
            if interleave:
                self.defer(kc + 1, lambda kc=kc: chunk(kc))
            else:
                chunk(kc)

    def defer(self, delay, fn):
        self.deferred.append([delay, fn])

    def tick(self):
        due = [d for d in self.deferred if d[0] <= 1]
        self.deferred = [[d[0] - 1, d[1]] for d in self.deferred if d[0] > 1]
        for d in due:
            d[1]()

    def flush(self):
        while self.deferred:
            self.tick()

    def qk_post(self, ps, pk, n, gain_off, rope, dst_ap, dst_keys):
        sc = self.sc
        par = self.qkpar
        self.qkpar ^= 1
        qsq, qln, qrs, qn, t1, t2 = self.qsq2[par], self.qln2[par], self.qrs2[par], self.qn2[par], self.t1b[par], self.t2b[par]
        kq, kr, kn = [(nm, par) for nm in ("qsq", "qrs", "qn")]
        kl, k1, k2 = ("qln", 0), ("t1", 0), ("t2", 0)
        gain = self.pcol(gain_off, 1)
        if not isinstance(dst_ap, list):
            dst_ap = [(0, 128, dst_ap)]
        sc.op("act", lambda e: e.activation(out=qsq[:, 0:n], in_=ps[:, 0:n], func=AF.Square), reads=[], writes=[pk, kq])

        def step1():
            pn, pnk = self.psum("n")
            self.mm_group(pn[:, 0:n], pnk, [(self.bd_mean, qsq[:, 0:n], [kq, "cmat"])])
            sc.op("act", lambda e: e.activation(out=qln[:, 0:n], in_=pn[:, 0:n], func=AF.Ln, bias=self.epsb[:, 0:1], scale=1.0),
                  reads=["epsb"], writes=[pnk, kl])
            sc.op("act", lambda e: e.activation(out=qrs[:, 0:n], in_=qln[:, 0:n], func=AF.Exp, scale=-0.5), reads=[kl], writes=[kr])
            if not rope:
                for (p0, p1, dap) in dst_ap:
                    sc.op("dve", lambda e, p0=p0, p1=p1, dap=dap: e.scalar_tensor_tensor(out=dap, in0=ps[p0:p1, 0:n], scalar=gain[p0:p1, :], in1=qrs[p0:p1, 0:n],
                                                                                   op0=ALU.mult, op1=ALU.mult),
                          reads=[kr, "par"], writes=[pk] + dst_keys)
            else:
                sc.op("dve", lambda e: e.scalar_tensor_tensor(out=qn[:, 0:n], in0=ps[:, 0:n], scalar=gain, in1=qrs[:, 0:n], op0=ALU.mult, op1=ALU.mult),
                      reads=[kr, "par"], writes=[pk, kn])

        def step2():
            pr, prk = self.psum("qk")
            self.mm_group(pr[:, 0:n], prk, [(self.perm, qn[:, 0:n], [kn, "cmat"])])
            sc.op("dve", lambda e: e.tensor_tensor(out=t1[:, 0:n], in0=qn[:, 0:n], in1=self.cosb[:, 0:n], op=ALU.mult), reads=[kn, "cosb"], writes=[k1])
            sc.op("dve", lambda e: e.tensor_tensor(out=t2[:, 0:n], in0=pr[:, 0:n], in1=self.sinb[:, 0:n], op=ALU.mult), reads=["sinb"], writes=[prk, k2])
            for (p0, p1, dap) in dst_ap:
                sc.op("dve", lambda e, p0=p0, p1=p1, dap=dap: e.tensor_tensor(out=dap, in0=t1[p0:p1, 0:n], in1=t2[p0:p1, 0:n], op=ALU.add),
                      reads=[k1, k2], writes=dst_keys)

        self.defer(1, step1)
        if rope:
            self.defer(3, step2)

    def front_load(self, Gr):
        sc = self.sc
        n, t0 = Gr.n, Gr.t0
        src = (self.src_c if Gr.is_ctx else self.src_x).rearrange("(k p) t -> p k t", p=128)
        sc.dma("sp", self.xb[Gr.xs][:, :, 0:n], src[:, :, t0:t0 + n],
               reads=[(self.xkey_src, Gr.name, pt_) for pt_ in ("m0", "m1", "m2", "m3", "m4", "m5", "e", "t")], writes=[("xb", Gr.xs, kc) for kc in range(8)])

    def front_a(self, Gr):
        sc = self.sc
        s, n, t0 = Gr.slot, Gr.n, Gr.t0
        if not Gr.is_ctx:
            sc.dma("sp", self.cosb[:, :], self.cos_d[:, t0:t0 + n], writes=["cosb"])
            sc.dma("sp", self.sinb[:, :], self.sin_d[:, t0:t0 + n], writes=["sinb"])
        self.norm_mod(Gr, self.gs1, self.kgs1, 0, interleave=True)

    def front_b(self, Gr):
        self.flush()
        sc = self.sc
        s, n, t0 = Gr.slot, Gr.n, Gr.t0
        hb = self.hb[s]
        hk = lambda kc: ("hb", s, kc)
        w, wk = self.wget("in0")
        wv = w[:, :].rearrange("p (k f) -> p k f", f=256)
        ps, pk = self.psum("mm")
        self.mm_group(ps[:, 0:n], pk, [(wv[:, kc, 0:128], hb[:, kc, 0:n], [wk, hk(kc)]) for kc in range(8)])
        s0 = Gr.kslots[0]
        kdst = [(0, 64, self.kT[0][0:64, s0 * 128:s0 * 128 + n]), (64, 128, self.kT[1][64:128, s0 * 128:s0 * 128 + n])]
        self.qk_post(ps, pk, n, 245, not Gr.is_ctx, kdst, [("kT", sl) for sl in Gr.kslots])
        ps2, pk2 = self.psum("mm")
        tok = None
        for b in range(Gr.nb):
            for kc in range(8):
                st, sp_ = (kc == 0), (kc == 7)
                tok = sc.op("pe", lambda e, b=b, kc=kc, st=st, sp_=sp_: e.matmul(
                    ps2[:, b * 128:(b + 1) * 128], lhsT=hb[:, kc, b * 128:(b + 1) * 128], rhs=wv[:, kc, 128:256], start=st, stop=sp_),
                    reads=[wk, hk(kc)], writes=[pk2] if (b == 0 and kc == 0) else [], signal=sp_)
            self.tick()
        self.flush()
        sc.lastw[pk2] = tok
        psv = ps2[:, 0:n].rearrange("p (b f) -> p b f", f=128)
        vk = [("V", sl) for sl in Gr.kslots]
        sc.op("act", lambda e: e.activation(out=self.V[:, s0:s0 + Gr.nb, 0, 0:64], in_=psv[:, :, 0:64], func=AF.Identity),
              writes=[pk2] + vk)
        sc.op("act", lambda e: e.activation(out=self.V[:, s0:s0 + Gr.nb, 1, 64:128], in_=psv[:, :, 64:128], func=AF.Identity),
              writes=[pk2] + vk)

    def back_proj_a(self, Gr, part):
        sc = self.sc
        s, n = Gr.slot, Gr.n
        hb = self.hb[s]
        hk = lambda kc: ("hb", s, kc)
        for pi in (range(2) if part == 0 else []):
            w, wk = self.wget(f"in{1 + pi}")
            wv = w[:, :].rearrange("p (k f) -> p k f", f=256)
            for cc in range(2):
                cq = pi * 2 + cc
                ps, pk = self.psum("mm")
                self.mm_group(ps[:, 0:n], pk, [(wv[:, kc, cc * 128:(cc + 1) * 128], hb[:, kc, 0:n], [wk, hk(kc)]) for kc in range(8)])
                self.tick()
                self.qk_post(ps, pk, n, 244, not Gr.is_ctx, self.qb[:, cq, 0:n], [("qb", cq)])
        for pi in (range(0, 4) if part == 0 else range(4, 8)):
            w, wk = self.wget(f"in{5 + pi}")
            wv = w[:, :].rearrange("p (k f) -> p k f", f=256)
            for cc in range(2):
                j = pi * 2 + cc
                ps, pk = self.psum("mm")
                self.mm_group(ps[:, 0:n], pk, [(wv[:, kc, cc * 128:(cc + 1) * 128], hb[:, kc, 0:n], [wk, hk(kc)]) for kc in range(8)])
                sc.op("act", lambda e, ps=ps, j=j: e.activation(out=self.gates[:, j, 0:n], in_=ps[:, 0:n], func=AF.Sigmoid),
                      writes=[pk, ("big", j)])
                self.tick()

    def back_proj_b_steps(self, Gr, nxt):
        sc = self.sc
        s, n = Gr.slot, Gr.n
        hb = self.hb[s]
        hk = lambda kc: ("hb", s, kc)
        pT = self.pT
        st8 = {}

        def init():
            if Gr.first:
                sc.op("dve", lambda e: e.memset(pT[:, :, 0:8], 0.0), writes=["pT"])
            else:
                sc.op("dve", lambda e: e.tensor_copy(out=pT[:, :, 0:8], in_=pT[:, :, n:n + 8]), reads=[], writes=["pT"])
            if nxt is None:
                sc.op("dve", lambda e: e.memset(pT[:, :, 8 + n:16 + n], 0.0), writes=["pT"])
            else:
                st8["psh"] = self.psum("n")

        def chunk(c):
            pi, cc = c // 2, c % 2
            if cc == 0:
                st8["w"] = self.wget(f"in{3 + pi}")
            w, wk = st8["w"]
            wv = w[:, :].rearrange("p (k f) -> p k f", f=256)
            ps, pk = self.psum("pp")
            self.mm_group(ps[:, 0:n], pk, [(wv[:, kc, cc * 128:(cc + 1) * 128], hb[:, kc, 0:n], [wk, hk(kc)]) for kc in range(8)])
            sc.op("dve", lambda e: e.tensor_copy(out=pT[:, c, 8:8 + n], in_=ps[:, 0:n]), writes=[pk, "pT"])
            if nxt is not None:
                psh, pkh = st8["psh"]
                hbn = self.hb[nxt.slot]
                tok = None
                for kc in range(8):
                    st, sp_ = (kc == 0), (kc == 7)
                    tok = sc.op("pe", lambda e, kc=kc, st=st, sp_=sp_: e.matmul(
                        psh[:, c * 8:(c + 1) * 8], lhsT=wv[:, kc, cc * 128:(cc + 1) * 128], rhs=hbn[:, kc, 0:8], start=st, stop=sp_),
                        reads=[wk, ("hb", nxt.slot, kc)], writes=[pkh] if (c == 0 and kc == 0) else [], signal=sp_)
                sc.lastw[pkh] = tok

        def fin():
            if nxt is not None:
                psh, pkh = st8["psh"]
                sc.op("dve", lambda e: e.tensor_copy(out=pT[:, :, 8 + n:16 + n], in_=psh[:, 0:32].rearrange("p (c f) -> p c f", f=8)),
                      writes=[pkh, "pT"])

        return [init] + [(lambda c=c: chunk(c)) for c in range(4)] + [fin]

    def back_rest(self, Gr, nxt):
        self.flush()
        steps = self.back_proj_b_steps(Gr, nxt) + [lambda: self.poolmix(Gr)]
        spacing = 1 if Gr.is_ctx else 4
        for i, st in enumerate(steps):
            self.defer(1 + i * spacing, st)
        self.attention(Gr)
        self.flush()
        self.poolmix_pe(Gr)
        self.merge_out(Gr)

    def attention(self, Gr):
        sc = self.sc
        n, g = Gr.n, Gr.g
        tiles = [(0, 0, n, []), (1, 0, n, [])]
        if not Gr.is_ctx:
            for j in range(4 * g - 1, 4 * g + 5):
                if j < 0 or j >= SEQ // 128:
                    continue
                lo = max(j - 1, 4 * g)
                hi = min(j + 1, 4 * g + 3)
                masks = []
                for i in range(lo, hi + 1):
                    if i == j - 1:
                        masks.append(((i - lo) * 128, self.mask_next))
                    elif i == j + 1:
                        masks.append(((i - lo) * 128, self.mask_prev))
                tiles.append((mslot(j), (lo - 4 * g) * 128, (hi + 1 - 4 * g) * 128, masks))
        units = [(cq, half) for cq in range(4) for half in range(2)]
        esink, kesink = self.esink, self.kesink
        nt = len(tiles)
        seq = [(u, ti) for u in range(len(units)) for ti in range(nt)]
        psS_of = {}

        def emit_S(idx):
            u, ti = seq[idx]
            cq, half = units[u]
            slot, c0, c1, masks = tiles[ti]
            N = c1 - c0
            psS, pkS = self.psum("st")
            nmm = 1 + len(masks)
            tok = sc.op("pe", lambda e: e.matmul(psS[:, 0:N], lhsT=self.kT[half][:, slot * 128:(slot + 1) * 128], rhs=self.qb[:, cq, c0:c1],
                                                 start=True, stop=(nmm == 1)),
                        reads=[("kT", slot), ("qb", cq)], writes=[pkS], signal=(nmm == 1))
            for mi, (co, mk) in enumerate(masks):
                lastm = (mi == len(masks) - 1)
                tok = sc.op("pe", lambda e, co=co, mk=mk, lastm=lastm: e.matmul(psS[:, co:co + 128], lhsT=self.ident, rhs=mk, start=False, stop=lastm),
                            reads=["cmat"], writes=[], signal=lastm)
            sc.lastw[pkS] = tok
            psS_of[idx] = (psS, pkS)

        def emit_norm(u, psO, pkO):
            cq, half = units[u]
            hbp = half * 64
            ob = 64 - hbp
            h = cq + 4 * half
            sc.op("act", lambda e: e.activation(out=self.lnden[ob:ob + 64, 0:n], in_=psO[ob:ob + 64, 0:n], func=AF.Ln,
                                                bias=esink[ob:ob + 64, h:h + 1], scale=1.0),
                  reads=[kesink], writes=[pkO, "lnden"])
            sc.op("act", lambda e: e.activation(out=self.rden[hbp:hbp + 64, 0:n], in_=self.lnden[ob:ob + 64, 0:n], func=AF.Exp, scale=-1.0),
                  reads=["lnden"], writes=["rden"])
            sc.op("dve", lambda e: e.tensor_tensor(out=self.attn[hbp:hbp + 64, cq, 0:n], in0=psO[hbp:hbp + 64, 0:n],
                                                   in1=self.rden[hbp:hbp + 64, 0:n], op=ALU.mult),
                  reads=["rden"], writes=[pkO, ("attn", cq)])

        pending = None
        cur = None
        emit_S(0)
        emit_S(1)
        for idx in range(len(seq)):
            u, ti = seq[idx]
            cq, half = units[u]
            slot, c0, c1, masks = tiles[ti]
            N = c1 - c0
            if idx + 2 < len(seq):
                emit_S(idx + 2)
            if ti == 0:
                cur = self.psum("o")
            psO, pkO = cur
            psS, pkS = psS_of.pop(idx)
            pt = self.PT[self.ptr]
            ptk = ("PT", self.ptr)
            self.ptr = (self.ptr + 1) % len(self.PT)
            sc.op("act", lambda e, pt=pt, psS=psS, N=N: e.activation(out=pt[:, 0:N], in_=psS[:, 0:N], func=AF.Exp, scale=0.125),
                  writes=[pkS, ptk])
            st, sp_ = (ti == 0), (ti == nt - 1)
            tokO = sc.op("pe", lambda e, psO=psO, slot=slot, half=half, pt=pt, N=N, c0=c0, c1=c1, st=st, sp_=sp_: e.matmul(
                psO[:, c0:c1], lhsT=self.V[:, slot, half, :], rhs=pt[:, 0:N], start=st, stop=sp_),
                reads=[("V", slot), ptk], writes=[pkO] if ti == 0 else [], signal=sp_)
            self.tick()
            if ti == 1 and pending is not None:
                emit_norm(*pending)
                pending = None
            if ti == nt - 1:
                sc.lastw[pkO] = tokO
                pending = (u, psO, pkO)
        emit_norm(*pending)

    def poolmix(self, Gr):
        sc = self.sc
        n = Gr.n
        pT = self.pT
        A, B_, C8, S_ = self.f4
        fk = ["f4_0", "f4_1", "f4_2", "f4_3"]

        def add(out_ap, a, b, reads, writes):
            sc.op("dve", lambda e: e.tensor_tensor(out=out_ap, in0=a, in1=b, op=ALU.add), reads=reads, writes=writes)

        for c, w in enumerate((2, 4, 8, 16)):
            P = pT[:, c, :]
            if c == 0:
                add(S_[:, 0:n], P[:, 7:7 + n], P[:, 8:8 + n], ["pT"], [fk[3]])
            elif c == 1:
                add(A[:, 0:n + 2], P[:, 6:8 + n], P[:, 7:9 + n], ["pT"], [fk[0]])
                add(S_[:, 0:n], A[:, 0:n], A[:, 2:n + 2], [fk[0]], [fk[3]])
            elif c == 2:
                add(A[:, 0:n + 6], P[:, 4:10 + n], P[:, 5:11 + n], ["pT"], [fk[0]])
                add(B_[:, 0:n + 4], A[:, 0:n + 4], A[:, 2:n + 6], [fk[0]], [fk[1]])
                add(S_[:, 0:n], B_[:, 0:n], B_[:, 4:n + 4], [fk[1]], [fk[3]])
            else:
                add(A[:, 0:n + 14], P[:, 0:14 + n], P[:, 1:15 + n], ["pT"], [fk[0]])
                add(B_[:, 0:n + 12], A[:, 0:n + 12], A[:, 2:n + 14], [fk[0]], [fk[1]])
                add(C8[:, 0:n + 8], B_[:, 0:n + 8], B_[:, 4:n + 12], [fk[1]], [fk[2]])
                add(S_[:, 0:n], C8[:, 0:n], C8[:, 8:n + 8], [fk[2]], [fk[3]])
            sc.op("dve", lambda e, c=c, w=w, P=P: e.scalar_tensor_tensor(out=self.dT[:, c, 0:n], in0=S_[:, 0:n], scalar=1.0 / w, in1=P[:, 8:8 + n],
                                                                        op0=ALU.mult, op1=ALU.subtract),
                  reads=[fk[3], "pT"], writes=[("dT", c)])
            if Gr.first:
                sc.op("dve", lambda e, c=c: e.tensor_tensor(out=A[:, 0:8], in0=S_[:, 0:8], in1=self.poolc[:, c * 8:(c + 1) * 8], op=ALU.mult),
                      reads=[fk[3], "poolc"], writes=[fk[0]])
                sc.op("dve", lambda e, c=c, P=P: e.tensor_tensor(out=self.dT[:, c, 0:8], in0=A[:, 0:8], in1=P[:, 8:16], op=ALU.subtract),
                      reads=[fk[0], "pT"], writes=[("dT", c)])
            if Gr.last:
                sc.op("dve", lambda e, c=c: e.tensor_tensor(out=A[:, 0:8], in0=S_[:, n - 8:n], in1=self.poolc[:, 32 + c * 8:32 + (c + 1) * 8], op=ALU.mult),
                      reads=[fk[3], "poolc"], writes=[fk[0]])
                sc.op("dve", lambda e, c=c, P=P: e.tensor_tensor(out=self.dT[:, c, n - 8:n], in0=A[:, 0:8], in1=P[:, n:n + 8], op=ALU.subtract),
                      reads=[fk[0], "pT"], writes=[("dT", c)])

    def poolmix_pe(self, Gr):
        sc = self.sc
        n = Gr.n
        w, wk = self.wget("pool")
        wv = w[:, 0:512].rearrange("p (g d) -> p g d", d=128)
        for c in range(4):
            ps, pk = self.psum("mm")
            self.mm_group(ps[:, 0:n], pk, [(wv[:, c, :], self.dT[:, c, 0:n], [wk, ("dT", c)])])
            sc.op("act", lambda e, ps=ps, c=c, psc=self.pcol(64 + c, 1): e.activation(out=self.po[:, c, 0:n], in_=ps[:, 0:n], func=AF.Identity, scale=psc),
                  reads=["par"], writes=[pk, ("po", c)])

    def merge_out(self, Gr):
        sc = self.sc
        s, n = Gr.slot, Gr.n
        si = Gr.si
        xb = self.xb[Gr.xs]
        xks = [("xb", Gr.xs, kc) for kc in range(8)]
        hb = self.hb[s]
        psn, pkn = self.psum("n")

        def stat_mm(c):
            st, sp_ = (c == 0), (c == 7)
            return sc.op("pe", lambda e: e.matmul(psn[:, 0:n], lhsT=self.ones_mean, rhs=hb[:, c, 0:n], start=st, stop=sp_),
                         reads=[("hb", s, c), "cmat"], writes=[pkn] if c == 0 else [], signal=sp_)

        for hf in range(2):
            wa, wak = self.wget(f"bra{hf}")
            wp, wpk = self.wget(f"brp{hf}")
            wav = wa[:, :].rearrange("p (k f) -> p k f", f=512)
            wpv = wp[:, :].rearrange("p (k f) -> p k f", f=512)
            for cc in range(4):
                c = hf * 4 + cc
                psa, pka = self.psum("mm")
                self.mm_group(psa[:, 0:n], pka, [(wav[:, k, cc * 128:(cc + 1) * 128], self.attn[:, k, 0:n], [wak, ("attn", k)]) for k in range(4)])
                psp, pkp = self.psum("mm")
                self.mm_group(psp[:, 0:n], pkp, [(wpv[:, k, cc * 128:(cc + 1) * 128], self.po[:, k, 0:n], [wpk, ("po", k)]) for k in range(4)])
                sc.op("dve", lambda e, psa=psa, c=c: e.tensor_tensor(out=self.t1[:, 0:n], in0=psa[:, 0:n], in1=self.gates[:, c, 0:n], op=ALU.mult),
                      reads=[("big", c)], writes=[pka, ("t1", 0)])
                sc.op("dve", lambda e, psp=psp, c=c: e.tensor_tensor(out=self.t2[:, 0:n], in0=psp[:, 0:n], in1=self.gates[:, 8 + c, 0:n], op=ALU.mult),
                      reads=[("big", 8 + c)], writes=[pkp, ("t2", 0)])
                sc.op("dve", lambda e, c=c: e.tensor_tensor(out=self.yb[:, c, 0:n], in0=self.t1[:, 0:n], in1=self.t2[:, 0:n], op=ALU.add),
                      reads=[("t1", 0), ("t2", 0)], writes=[("big", 16 + c)])
        for pi in range(4):
            w, wk = self.wget(f"out{pi}")
            wv = w[:, :].rearrange("p (k f) -> p k f", f=256)
            for cc in range(2):
                c = pi * 2 + cc
                ps, pk = self.psum("mm")
                self.mm_group(ps[:, 0:n], pk, [(wv[:, k, cc * 128:(cc + 1) * 128], self.yb[:, k, 0:n], [wk, ("big", 16 + k)]) for k in range(8)])
                if c >= 1:
                    stat_mm(c - 1)
                g1 = self.mod[:, si, 16 + c:17 + c]
                sc.op("dve", lambda e, ps=ps, c=c, g1=g1: e.scalar_tensor_tensor(out=xb[:, c, 0:n], in0=ps[:, 0:n], scalar=g1,
                                                                                in1=xb[:, c, 0:n], op0=ALU.mult, op1=ALU.add),
                      reads=[self.kmod], writes=[pk, xks[c]])
                sc.op("act", lambda e, c=c: e.activation(out=hb[:, c, 0:n], in_=xb[:, c, 0:n], func=AF.Square), reads=[xks[c]], writes=[("hb", s, c)])
        tokn = stat_mm(7)
        sc.lastw[pkn] = tokn
        self.nstat = (psn, pkn)
        xl = self.xl[Gr.g % 2 if not Gr.is_ctx else 0]
        sc.op("dve", lambda e: e.tensor_copy(out=xl[:, :], in_=xb[:, :, n - 1]), reads=xks, writes=[("xl", Gr.g % 2 if not Gr.is_ctx else 0)])

    def ffn(self, Gr):
        sc = self.sc
        s, n, si = Gr.slot, Gr.n, Gr.si
        xb = self.xb[Gr.xs]
        xks = [("xb", Gr.xs, kc) for kc in range(8)]
        hb = self.hb[s]
        hk = lambda kc: ("hb", s, kc)
        self.norm_mod(Gr, self.gs2, self.kgs2, 24, stats_done=True)
        if Gr.first:
            sc.op("dve", lambda e: e.memset(self.saved[:, :, :], 0.0), writes=["saved"])
        w0T, w1T = self.pcol(68, 44), self.pcol(68 + 44, 44)
        corr = self.corr
        sc.op("dve", lambda e: e.tensor_tensor(out=corr[:, :, 1], in0=self.saved[:, :, 1], in1=w0T, op=ALU.mult), reads=["saved", "par"], writes=["corr"])
        sc.op("dve", lambda e: e.tensor_tensor(out=corr[:, :, 0], in0=self.saved[:, :, 0], in1=w0T, op=ALU.mult), reads=["saved", "par"], writes=["corr"])
        sc.op("dve", lambda e: e.tensor_tensor(out=self.tail_a[:, :], in0=self.saved[:, :, 1], in1=w1T, op=ALU.mult), reads=["saved", "par"], writes=["tail_a"])
        sc.op("dve", lambda e: e.tensor_tensor(out=corr[:, :, 0], in0=corr[:, :, 0], in1=self.tail_a[:, :], op=ALU.add), reads=["tail_a"], writes=["corr"])
        bigkeys = [("big", j) for j in range(24)]
        for i in range(NFC):
            w, wk = self.wget(f"up{i}")
            wv = w[:, :].rearrange("p (k f) -> p k f", f=256)
            accs = []
            for ab in range(2):
                ch = i + ab * NFC
                ps, pk = self.psum("mm")
                self.mm_group(ps[:, 0:n], pk, [(wv[:, kc, ab * 128:(ab + 1) * 128], hb[:, kc, 0:n], [wk, hk(kc)]) for kc in range(8)])
                acc = self.f4[(i % 2) * 2 + ab]
                ak = f"f4_{(i % 2) * 2 + ab}"
                w0, w1, w2, bb = self.pcol(68 + ch), self.pcol(68 + 44 + ch), self.pcol(68 + 88 + ch), self.pcol(200 + ch)
                sv = self.saved[:, ch, :]
                sc.op("act", lambda e, ps=ps, acc=acc, w2=w2, bb=bb: e.activation(out=acc[:, 0:n], in_=ps[:, 0:n], func=AF.Identity, bias=bb, scale=w2),
                      reads=["par"], writes=[pk, ak])
                sc.op("act", lambda e, ps=ps, sv=sv: e.activation(out=sv[:, 0:2], in_=ps[:, n - 2:n], func=AF.Identity), reads=["corr"], writes=[pk, "saved"])
                sc.op("dve", lambda e, ps=ps, acc=acc, w1=w1: e.scalar_tensor_tensor(out=acc[:, 1:n], in0=ps[:, 0:n - 1], scalar=w1, in1=acc[:, 1:n],
                                                                                    op0=ALU.mult, op1=ALU.add),
                      reads=["par"], writes=[pk, ak])
                sc.op("dve", lambda e, ps=ps, acc=acc, w0=w0: e.scalar_tensor_tensor(out=acc[:, 2:n], in0=ps[:, 0:n - 2], scalar=w0, in1=acc[:, 2:n],
                                                                                    op0=ALU.mult, op1=ALU.add),
                      reads=["par"], writes=[pk, ak])
                sc.op("dve", lambda e, acc=acc, ch=ch: e.tensor_tensor(out=acc[:, 0:2], in0=acc[:, 0:2], in1=corr[:, ch, :], op=ALU.add),
                      reads=["corr"], writes=[ak])
                accs.append((acc, ak))
            (aa, aak), (ab_, abk) = accs
            sil, silk = self.sil2[i % 2], ("sil", i % 2)
            sc.op("act", lambda e, aa=aa, sil=sil: e.activation(out=sil[:, 0:n], in_=aa[:, 0:n], func=AF.Silu), reads=[aak], writes=[silk])
            sc.op("pool", lambda e, ab_=ab_, i=i, sil=sil: e.tensor_tensor(out=self.actT[:, i, 0:n], in0=sil[:, 0:n], in1=ab_[:, 0:n], op=ALU.mult),
                  reads=[silk, abk], writes=bigkeys if i == 0 else [("act", i)])
        actkeys = bigkeys + [("act", i) for i in range(1, NFC)]
        c_first = 1 if Gr.first else 0
        xlp = self.xl[(Gr.g - 1) % 2]
        xlpk = ("xl", (Gr.g - 1) % 2)
        hbf = hb[:, :, :].rearrange("p c t -> p (c t)").bitcast(F32).rearrange("p (c t) -> p c t", t=G)
        if Gr.last:
            self.tail_prep()
            pst, pkt = self.psum("mm")
            tokt = None
        attn_f = self.attn[:, :, :].rearrange("p c t -> p (c t)").bitcast(F32).rearrange("p (c t) -> p c t", t=G)
        dT_f = self.dT[:, :, :].rearrange("p c t -> p (c t)").bitcast(F32).rearrange("p (c t) -> p c t", t=G)
        KA = NFC // 2
        for fp in range(4):
            wd = [self.wget(f"dn{fp}_{q}") for q in range(3)]

            def wv_of(kc):
                w, wk = wd[kc // 8]
                return w[:, 0:(8 if kc < 16 else 6) * 256].rearrange("p (k f) -> p k f", f=256), wk

            if Gr.last:
                for cc in range(2):
                    c = fp * 2 + cc
                    for kc in range(NFC):
                        wv, wk = wv_of(kc)
                        st, sp_ = (kc == 0), (kc == NFC - 1)
                        tokt = sc.op("pe", lambda e, c=c, kc=kc, wv=wv, cc=cc, st=st, sp_=sp_: e.matmul(
                            pst[:, 2 * c:2 * c + 1], lhsT=wv[:, kc % 8, cc * 128:(cc + 1) * 128], rhs=self.tail_act[:, kc:kc + 1], start=st, stop=sp_),
                            reads=[wk, "tail_act"], writes=[pkt] if (c == 0 and kc == 0) else [], signal=sp_)
            banks = [self.psum("aux") for _ in range(2)]
            toks = [None, None]
            for (k0, k1) in ((0, KA), (KA, NFC)):
                for cc in range(2):
                    ps, pk = banks[cc]
                    for kc in range(k0, k1):
                        wv, wk = wv_of(kc)
                        st, sp_ = (kc == 0), (kc == NFC - 1)
                        toks[cc] = sc.op("pe", lambda e, ps=ps, kc=kc, wv=wv, cc=cc, st=st, sp_=sp_: e.matmul(
                            ps[:, 0:n], lhsT=wv[:, kc % 8, cc * 128:(cc + 1) * 128], rhs=self.actT[:, kc, 0:n], start=st, stop=sp_),
                            reads=[wk] + (bigkeys if kc == 0 else [("act", kc)]), writes=[pk] if kc == 0 else [], signal=sp_)
            for cc in range(2):
                c = fp * 2 + cc
                ps, pk = banks[cc]
                sc.lastw[pk] = toks[cc]
                g2 = self.mod[:, si, 40 + c:41 + c]
                if c < 4:
                    xo, xok = self.f4[c][:, 0:n - 1], [f"f4_{c}"]
                elif c < 6:
                    xo, xok = attn_f[:, c - 4, 0:n - 1], [("attn", 2 * (c - 4)), ("attn", 2 * (c - 4) + 1)]
                else:
                    xo, xok = dT_f[:, c - 6, 0:n - 1], [("dT", 2 * (c - 6)), ("dT", 2 * (c - 6) + 1)]
                sc.op("dve", lambda e, ps=ps, c=c, g2=g2, xo=xo: e.scalar_tensor_tensor(out=xo, in0=ps[:, 1:n], scalar=g2, in1=xb[:, c, 0:n - 1],
                                                                                       op0=ALU.mult, op1=ALU.add),
                      reads=[self.kmod, xks[c]], writes=[pk] + xok)
                if not Gr.first:
                    sc.op("dve", lambda e, ps=ps, c=c, g2=g2: e.scalar_tensor_tensor(out=xlp[:, c:c + 1], in0=ps[:, 0:1], scalar=g2, in1=xlp[:, c:c + 1],
                                                                                    op0=ALU.mult, op1=ALU.add),
                          reads=[self.kmod], writes=[pk, xlpk])
        dst = (self.dst_c if Gr.is_ctx else self.dst_x).rearrange("(k p) t -> p k t", p=128)
        t0 = Gr.t0
        dkey = (self.xkey_dst, Gr.name, "t")
        for c in range(4):
            sc.dma("sp", dst[:, c, t0:t0 + n - 1], self.f4[c][:, 0:n - 1], reads=[f"f4_{c}"], writes=[(self.xkey_dst, Gr.name, f"m{c}")])
        sc.dma("sp", dst[:, 4:6, t0:t0 + n - 1], attn_f[:, :, 0:n - 1], reads=[("attn", k) for k in range(4)], writes=[(self.xkey_dst, Gr.name, "m4")])
        sc.dma("sp", dst[:, 6:8, t0:t0 + n - 1], dT_f[:, :, 0:n - 1], reads=[("dT", k) for k in range(4)], writes=[(self.xkey_dst, Gr.name, "m5")])
        if not Gr.first:
            pkey = (self.xkey_dst, str(Gr.g - 1), "e")
            sc.dma("sp", dst[:, :, t0 - 1:t0], xlp[:, :].rearrange("p (k o) -> p k o", o=1), reads=[xlpk], writes=[pkey], slow=True)
        if Gr.last:
            sc.lastw[pkt] = tokt
            self.tail_finish(Gr, dst, dkey, pst, pkt)

    def tail_prep(self):
        sc = self.sc
        w0, w1, bb = self.pcol(68, 44), self.pcol(68 + 44, 44), self.pcol(200, 44)
        ta = self.tail_a
        sc.op("dve", lambda e: e.tensor_tensor(out=ta[:, :], in0=self.saved[:, :, 0], in1=w0, op=ALU.mult), reads=["saved", "par"], writes=["tail_a"])
        sc.op("dve", lambda e: e.tensor_tensor(out=self.tail_s[:, :], in0=self.saved[:, 0:NFC, 1], in1=w1[:, 0:NFC], op=ALU.mult), reads=["saved", "par"], writes=["tail_s"])
        sc.op("dve", lambda e: e.tensor_tensor(out=ta[:, 0:NFC], in0=ta[:, 0:NFC], in1=self.tail_s[:, :], op=ALU.add), reads=["tail_s"], writes=["tail_a"])
        sc.op("dve", lambda e: e.tensor_tensor(out=self.tail_s[:, :], in0=self.saved[:, NFC:2 * NFC, 1], in1=w1[:, NFC:2 * NFC], op=ALU.mult), reads=["saved", "par"], writes=["tail_s"])
        sc.op("dve", lambda e: e.tensor_tensor(out=ta[:, NFC:2 * NFC], in0=ta[:, NFC:2 * NFC], in1=self.tail_s[:, :], op=ALU.add), reads=["tail_s"], writes=["tail_a"])
        sc.op("dve", lambda e: e.tensor_tensor(out=ta[:, :], in0=ta[:, :], in1=bb, op=ALU.add), reads=["par"], writes=["tail_a"])
        sc.op("act", lambda e: e.activation(out=self.tail_s[:, :], in_=ta[:, 0:NFC], func=AF.Silu), reads=["tail_a"], writes=["tail_s"])
        sc.op("dve", lambda e: e.tensor_tensor(out=self.tail_act[:, :], in0=self.tail_s[:, :], in1=ta[:, NFC:2 * NFC], op=ALU.mult),
              reads=["tail_s", "tail_a"], writes=["tail_act"])

    def tail_finish(self, Gr, dst, dkey, ps, pk):
        sc = self.sc
        n, si = Gr.n, Gr.si
        xi = Gr.g % 2 if not Gr.is_ctx else 0
        xl, xlk = self.xl[xi], ("xl", xi)
        g2all = self.mod[:, si, 40:48]
        sc.op("dve", lambda e: e.tensor_tensor(out=self.tail_a[:, 0:8], in0=ps[:, 0:16:2], in1=g2all, op=ALU.mult),
              reads=[self.kmod], writes=[pk, "tail_a"])
        sc.op("dve", lambda e: e.tensor_tensor(out=xl[:, :], in0=xl[:, :], in1=self.tail_a[:, 0:8], op=ALU.add), reads=["tail_a"], writes=[xlk])
        t_last = Gr.t0 + n - 1
        sc.dma("sp", dst[:, :, t_last:t_last + 1], xl[:, :].rearrange("p (k o) -> p k o", o=1), reads=[xlk], writes=[dkey], slow=True)


def build_nc(n_layers=DEPTH, dbg=False):
    dry = Builder(n_layers, dbg, plan=None)
    dry.build()
    return Builder(n_layers, dbg, plan=dry.plan).build()


_CACHE = {}


def prepare_inputs(x, c, ctx, c_ctx, w_mod, b_mod, norm1_g, norm2_g, w_in, q_gain, k_gain, sink,
                   w_pool, pool_scale, w_br_attn, w_br_pool, w_out, w_up, conv_w, conv_b, w_down, n_layers=DEPTH):
    f = lambda a: np.asarray(a, dtype=np.float32)
    x, c, ctx, c_ctx = f(x), f(c), f(ctx), f(c_ctx)
    w_mod, b_mod, norm1_g, norm2_g, w_in = f(w_mod), f(b_mod), f(norm1_g), f(norm2_g), f(w_in)
    q_gain, k_gain, sink, w_pool, pool_scale = f(q_gain), f(k_gain), f(sink), f(w_pool), f(pool_scale)
    w_br_attn, w_br_pool, w_out, w_up, conv_w, conv_b, w_down = f(w_br_attn), f(w_br_pool), f(w_out), f(w_up), f(conv_w), f(conv_b), f(w_down)
    cm, cosT, sinT, pc = build_consts()
    params = build_params(b_mod, norm1_g, norm2_g, pool_scale, conv_w, conv_b, q_gain, k_gain, sink)
    shared = {"params": params, "cmat": cm, "cosT": cosT, "sinT": sinT, "poolc": pc}
    for l in range(n_layers):
        shared[f"wst{l}"] = build_wstream(l, w_in, w_pool, w_br_attn, w_br_pool, w_out, w_up, w_down)
        shared[f"wmod{l}"] = build_wmod_stream(l, w_mod)
    in_maps = []
    for b in range(NCORES):
        m = dict(shared)
        m["xT"] = np.ascontiguousarray(x[b].T)
        m["cxT"] = np.ascontiguousarray(ctx[b].T)
        cond = np.zeros((128, 16), np.float32)
        cond[:, 0:8] = c[b].reshape(8, 128).T
        cond[:, 8:16] = c_ctx.reshape(8, 128).T
        m["cond"] = cond
        in_maps.append(m)
    return in_maps


def kernel(**inputs):
    in_maps = prepare_inputs(**inputs)
    if "nc" not in _CACHE:
        _CACHE["nc"] = build_nc(DEPTH)
    nc = _CACHE["nc"]
    res = run_bass_kernel_spmd(nc, in_maps, core_ids=list(range(NCORES)))
    out = np.stack([np.ascontiguousarray(r["outT"].T) for r in res.results], axis=0)
    return out.astype(np.float32)
```

```python
import numpy as np
from contextlib import ExitStack
import concourse.bass as bass
import concourse.mybir as mybir
from concourse.bass_utils import run_bass_kernel_spmd

F32 = mybir.dt.float32
BF16 = mybir.dt.bfloat16
AF = mybir.ActivationFunctionType
ALU = mybir.AluOpType

D = 1024
SEQ = 4096
CTX = 256
DEPTH = 4
NCORES = 8
GRID_W = 64
HD = 64
DFF = 2816
NFC = DFF // 128
EPS = 1e-6
G = 512
NG = SEQ // G
RING = 12
NSLOT = 7
SLOT_E = 2048
NPL = 256

PIECES = []
for _i in range(13):
    PIECES.append((f"in{_i}", 2048))
PIECES.append(("pool", 512))
PIECES += [("bra0", 2048), ("bra1", 2048), ("brp0", 2048), ("brp1", 2048)]
PIECES += [(f"out{_i}", 2048) for _i in range(4)]
PIECES += [(f"up{_i}", 2048) for _i in range(NFC)]
for _f in range(4):
    PIECES += [(f"dn{_f}_0", 2048), (f"dn{_f}_1", 2048), (f"dn{_f}_2", 1536)]
POFF = {}
_o = 0
for _n, _e in PIECES:
    POFF[_n] = (_o, _e)
    _o += 128 * _e
WSTREAM_LEN = _o


def _piece(W, kchunks, cols):
    K = W.shape[0] // 128
    Wr = W.reshape(K, 128, W.shape[1])[kchunks][:, :, cols]
    return np.ascontiguousarray(Wr.transpose(1, 0, 2)).reshape(128, -1)


def _qperm():
    idx = []
    for cq in range(4):
        for half in range(2):
            h = cq + 4 * half
            idx += list(range(h * 64, (h + 1) * 64))
    return np.array(idx)


def build_wstream(l, w_in, w_pool, w_br_attn, w_br_pool, w_out, w_up, w_down):
    out = np.empty(WSTREAM_LEN, np.float32)

    def put(name, arr):
        o, e = POFF[name]
        assert arr.shape == (128, e), (name, arr.shape, e)
        out[o:o + 128 * e] = arr.reshape(-1)

    qp = _qperm()
    cols = np.concatenate([np.arange(512, 640), np.arange(640, 768), qp, np.arange(768, 1280), np.arange(1280, 3328)])
    Wp = w_in[l][:, cols]
    k8 = list(range(8))
    for i in range(13):
        put(f"in{i}", _piece(Wp, k8, np.arange(i * 256, (i + 1) * 256)))
    put("pool", np.ascontiguousarray(w_pool[l].transpose(1, 0, 2)).reshape(128, 512))
    bra = w_br_attn[l][qp, :]
    brp = w_br_pool[l]
    for hf in range(2):
        put(f"bra{hf}", _piece(bra, [0, 1, 2, 3], np.arange(hf * 512, (hf + 1) * 512)))
        put(f"brp{hf}", _piece(brp, [0, 1, 2, 3], np.arange(hf * 512, (hf + 1) * 512)))
    for i in range(4):
        put(f"out{i}", _piece(w_out[l], k8, np.arange(i * 256, (i + 1) * 256)))
    for i in range(NFC):
        cc = np.concatenate([np.arange(i * 128, (i + 1) * 128), np.arange(DFF + i * 128, DFF + (i + 1) * 128)])
        put(f"up{i}", _piece(w_up[l], k8, cc))
    for f in range(4):
        cc = np.arange(f * 256, (f + 1) * 256)
        put(f"dn{f}_0", _piece(w_down[l], list(range(0, 8)), cc))
        put(f"dn{f}_1", _piece(w_down[l], list(range(8, 16)), cc))
        put(f"dn{f}_2", _piece(w_down[l], list(range(16, 22)), cc))
    return out


def build_wmod_stream(l, w_mod):
    W = w_mod[l].reshape(8, 128, 24, 256)
    return np.ascontiguousarray(W.transpose(2, 1, 0, 3)).reshape(24, 128, 2048)


def build_params(b_mod, norm1_g, norm2_g, pool_scale, conv_w, conv_b, q_gain, k_gain, sink):
    P = np.zeros((128, DEPTH * NPL), np.float32)
    for l in range(DEPTH):
        o = l * NPL
        P[:, o:o + 48] = b_mod[l].reshape(48, 128).T
        P[:, o + 48:o + 56] = norm1_g[l].reshape(8, 128).T
        P[:, o + 56:o + 64] = norm2_g[l].reshape(8, 128).T
        P[:, o + 64:o + 68] = pool_scale[l].reshape(4, 128).T
        for j in range(3):
            P[:, o + 68 + j * 44:o + 68 + (j + 1) * 44] = conv_w[l, j].reshape(44, 128).T
        P[:, o + 200:o + 244] = conv_b[l].reshape(44, 128).T
        P[:, o + 244] = np.tile(q_gain[l], 2)
        P[:, o + 245] = np.tile(k_gain[l], 2)
        P[:, o + 246:o + 254] = sink[l][None, :]
    return P


def build_consts():
    cm = np.zeros((128, 6 * 128), np.float32)
    cm[:, 0:128] = 1.0 / 1024.0
    for hb in (0, 64):
        cm[hb:hb + 64, 128 + hb:128 + hb + 64] = 1.0 / 64.0
    for m in range(128):
        d = m % 64
        half = (d % 32) // 16
        partner = m + 16 if half == 0 else m - 16
        cm[partner, 256 + m] = 1.0
    kj = np.arange(128)[:, None]
    qi = np.arange(128)[None, :]
    cm[:, 384:512] = ((kj <= qi).astype(np.float32) - 1.0) * 30000.0
    cm[:, 512:640] = ((kj >= qi).astype(np.float32) - 1.0) * 30000.0
    cm[:, 640:768] = np.eye(128, dtype=np.float32)
    n_freq = HD // 4
    inv = (np.float32(10000.0) ** (-(np.arange(n_freq, dtype=np.float32)) / np.float32(n_freq))).astype(np.float32)
    t = np.arange(SEQ)
    row = (t // GRID_W).astype(np.float32)
    col = (t % GRID_W).astype(np.float32)
    cosT = np.zeros((128, SEQ), np.float32)
    sinT = np.zeros((128, SEQ), np.float32)
    for p in range(128):
        d = p % 64
        axis = d // 32
        half = (d % 32) // 16
        f = d % 16
        ang = ((row if axis == 0 else col) * inv[f]).astype(np.float32)
        cosT[p] = np.cos(ang).astype(np.float32)
        s = np.sin(ang).astype(np.float32)
        sinT[p] = -s if half == 0 else s
    pc = np.zeros((128, 64), np.float32)
    for c, w in enumerate((2, 4, 8, 16)):
        for i in range(8):
            cntl = (i + w // 2) - max(i - w // 2, 0)
            pc[:, c * 8 + i] = 1.0 / cntl
            tt = -8 + i
            hi = min(tt + w // 2, 0)
            lo = tt - w // 2
            pc[:, 32 + c * 8 + i] = 1.0 / (hi - lo)
    return cm, cosT, sinT, pc


class Tok:
    __slots__ = ("eng", "sem", "val")

    def __init__(self, eng):
        self.eng = eng
        self.sem = None
        self.val = None


ENGS = ("pe", "act", "dve", "pool", "sp")
SEM_ROLL = 30000


class Sched:
    def __init__(self, nc, es):
        self.nc = nc
        self.es = es
        self.prog = {e: [] for e in ENGS}
        self.esem = {}
        self.ecount = {}
        self.nroll = {}
        for e in ("pe", "act", "dve", "pool"):
            self.esem[e] = es.enter_context(nc.semaphore(f"c_{e}_0"))
            self.ecount[e] = 0
            self.nroll[e] = 0
        self.pending = {e: [] for e in ENGS}
        self.waited = {e: {} for e in ENGS}
        self.lastw = {}
        self.readers = {}
        self.dsem = {q: [es.enter_context(nc.semaphore(f"d_{q}_{i}")) for i in range(12)] for q in ("sp", "pool")}
        self.dcount = {q: [0] * 12 for q in ("sp", "pool")}
        self.drr = {"sp": 0, "pool": 0}
        self.ninstr = {e: 0 for e in ENGS}

    def _deps(self, reads, writes):
        toks = []
        for k in reads:
            t = self.lastw.get(k)
            if t is not None:
                toks.append(t)
        for k in writes:
            t = self.lastw.get(k)
            if t is not None:
                toks.append(t)
            toks.extend(self.readers.get(k, ()))
        return toks

    def _waits(self, eng, toks):
        waits = []
        for t in toks:
            assert t.val is not None, "dependency on an unresolved (unsignalled) op"
            if t.eng == "pe" and eng == "pe":
                continue
            sid = id(t.sem)
            if t.val > self.waited[eng].get(sid, 0):
                self.waited[eng][sid] = t.val
                waits.append((t.sem, t.val))
        return waits

    def _commit(self, tok, reads, writes):
        for k in reads:
            self.readers.setdefault(k, []).append(tok)
        for k in writes:
            self.lastw[k] = tok
            self.readers[k] = []

    def op(self, eng, fn, reads=(), writes=(), signal=True):
        waits = self._waits(eng, self._deps(reads, writes))
        tok = Tok(eng)
        sem = None
        if signal:
            if self.ecount[eng] >= SEM_ROLL:
                self.nroll[eng] += 1
                self.esem[eng] = self.es.enter_context(self.nc.semaphore(f"c_{eng}_{self.nroll[eng]}"))
                self.ecount[eng] = 0
            self.ecount[eng] += 1
            sem = self.esem[eng]
            tok.sem, tok.val = sem, self.ecount[eng]
            for p in self.pending[eng]:
                p.sem, p.val = tok.sem, tok.val
            self.pending[eng] = []
        else:
            self.pending[eng].append(tok)

        def run(e, waits=waits, fn=fn, sem=sem):
            for s, v in waits:
                e.wait_ge(s, v)
            ins = fn(e)
            if sem is not None:
                ins.then_inc(sem, 1)

        self.prog[eng].append(run)
        self.ninstr[eng] += 1
        self._commit(tok, reads, writes)
        return tok

    def dma(self, q, out_ap, in_ap, reads=(), writes=(), slow=False):
        waits = self._waits(q, self._deps(reads, writes))
        i = self.drr[q]
        self.drr[q] = (i + 1) % len(self.dsem[q])
        sem = self.dsem[q][i]
        c = self.dcount[q][i]
        if c > self.waited[q].get(id(sem), 0):
            self.waited[q][id(sem)] = c
            waits.append((sem, c))
        self.dcount[q][i] = c + 16
        tok = Tok("dma")
        tok.sem, tok.val = sem, c + 16

        def run(e, waits=waits, sem=sem, out_ap=out_ap, in_ap=in_ap, slow=slow):
            for s, v in waits:
                e.wait_ge(s, v)
            if slow:
                e.dma_start(out=out_ap, in_=in_ap, allow_slow_non_contiguous=True).then_inc(sem, 16)
            else:
                e.dma_start(out=out_ap, in_=in_ap).then_inc(sem, 16)

        self.prog[q].append(run)
        self.ninstr[q] += 1
        self._commit(tok, reads, writes)
        return tok

    def finish(self):
        finals = [(self.dsem["sp"][i], self.dcount["sp"][i]) for i in range(12) if self.dcount["sp"][i] > 0]

        def run(e, finals=finals):
            for s, v in finals:
                e.wait_ge(s, v)

        self.prog["sp"].append(run)


class DrySched:
    def __init__(self):
        self.lastw = {}
        self.prog = {e: [] for e in ENGS}
        self.ninstr = {e: 0 for e in ENGS}

    def op(self, eng, fn, reads=(), writes=(), signal=True):
        t = Tok(eng)
        t.val = 1
        return t

    def dma(self, q, out_ap, in_ap, reads=(), writes=(), slow=False):
        t = Tok("dma")
        t.val = 1
        return t

    def finish(self):
        pass


AHEAD = 4
assert AHEAD + 3 <= NSLOT

class Grp:
    def __init__(self, is_ctx, g):
        self.is_ctx = is_ctx
        self.g = g
        self.n = CTX if is_ctx else G
        self.nb = self.n // 128
        self.slot = 1 if is_ctx else g % 2
        self.xs = 2 if is_ctx else g % 3
        self.t0 = 0 if is_ctx else g * G
        self.si = 1 if is_ctx else 0
        self.first = is_ctx or g == 0
        self.last = is_ctx or g == NG - 1
        if is_ctx:
            self.kslots = [0, 1]
        else:
            self.kslots = [2 + ((4 * g + i) % RING) for i in range(4)]
        self.name = "c" if is_ctx else str(g)


def mslot(b):
    return 2 + (b % RING)


class Builder:
    def __init__(self, n_layers=DEPTH, dbg=False, plan=None):
        self.n_layers = n_layers
        self.dbg = dbg
        self.dry = plan is None
        self.plan = [] if plan is None else plan
        self.pidx = 0
        self.issued = 0
        self.pslots = {}
        self.nc = bass.Bass("TRN2", target_bir_lowering=False)
        self.es = ExitStack()

    def sb(self, name, shape, dt):
        return self.es.enter_context(self.nc.sbuf_tensor(name, shape, dt))

    def dram_in(self, name, shape, dt=F32):
        return self.nc.dram_tensor(name, shape, dt, kind="ExternalInput").ap()

    def build(self):
        nc, es = self.nc, self.es
        L = self.n_layers
        self.xT = self.dram_in("xT", [D, SEQ])
        self.cxT = self.dram_in("cxT", [D, CTX])
        self.cond = self.dram_in("cond", [128, 16])
        self.params_d = self.dram_in("params", [128, DEPTH * NPL])
        self.cmat_d = self.dram_in("cmat", [128, 768])
        self.cos_d = self.dram_in("cosT", [128, SEQ])
        self.sin_d = self.dram_in("sinT", [128, SEQ])
        self.poolc_d = self.dram_in("poolc", [128, 64])
        self.wst = [self.dram_in(f"wst{l}", [WSTREAM_LEN]) for l in range(L)]
        self.wmod = [self.dram_in(f"wmod{l}", [24, 128, 2048]) for l in range(L)]
        self.outT = nc.dram_tensor("outT", [D, SEQ], F32, kind="ExternalOutput").ap()
        self.S = [nc.dram_tensor(f"S{i}", [D, SEQ], F32, kind="ExternalOutput" if self.dbg else "Internal").ap() for i in range(2)]
        self.C = [nc.dram_tensor(f"C{i}", [D, CTX], F32, kind="ExternalOutput" if self.dbg else "Internal").ap() for i in range(2)]

        self.sc = DrySched() if self.dry else Sched(nc, es)
        sb = self.sb
        self.wslot = [sb(f"wslot{i}", [128, SLOT_E], BF16) for i in range(NSLOT)]
        self.wrr = 0
        self.xb = [sb(f"xb{i}", [128, 8, G], F32) for i in range(3)]
        self.hb = [sb(f"hb{i}", [128, 8, G], BF16) for i in range(2)]
        self.ntmp = [sb(f"ntmp{i}", [128, G], F32) for i in range(2)]
        self.rln = sb("rln", [128, G], F32)
        self.rstd = sb("rstd", [128, G], F32)
        self.kT = [sb(f"kT{i}", [128, (2 + RING) * 128], BF16) for i in range(2)]
        self.V = sb("V", [128, 2 + RING, 2, 128], BF16)
        self.qb = sb("qb", [128, 4, G], BF16)
        self.big = sb("big", [128, 24 * G], BF16)
        self.gates = self.big[:, 0:16 * G].rearrange("p (c t) -> p c t", t=G)
        self.yb = self.big[:, 16 * G:24 * G].rearrange("p (c t) -> p c t", t=G)
        self.actT = self.big[:, 0:NFC * G].rearrange("p (c t) -> p c t", t=G)
        self.pT = sb("pT", [128, 4, G + 16], F32)
        self.attn = sb("attn", [128, 4, G], BF16)
        self.dT = sb("dT", [128, 4, G], BF16)
        self.po = sb("po", [128, 4, G], BF16)
        self.f4 = [sb(f"f4_{i}", [128, G + 16], F32) for i in range(4)]
        self.PT = [sb(f"PT{i}", [128, G], BF16) for i in range(3)]
        self.ptr = 0
        self.qsq2 = [sb(f"qsq{i}", [128, G], BF16) for i in range(2)]
        self.qln2 = [sb("qln0", [128, G], F32)] * 2
        self.qrs2 = [sb(f"qrs{i}", [128, G], F32) for i in range(2)]
        self.qn2 = [sb(f"qn{i}", [128, G], BF16) for i in range(2)]
        self.qkpar = 0
        self.deferred = []
        self.cosb = sb("cosb", [128, G], F32)
        self.sinb = sb("sinb", [128, G], F32)
        self.t1b = [sb("t1_0", [128, G], F32)] * 2
        self.t2b = [sb("t2_0", [128, G], F32)] * 2
        self.t1, self.t2 = self.t1b[0], self.t2b[0]
        self.lnden = sb("lnden", [128, G], F32)
        self.rden = sb("rden", [128, G], F32)
        self.sil2 = [sb(f"sil{i}", [128, G], F32) for i in range(2)]
        self.corr = sb("corr", [128, 2 * NFC, 2], F32)
        self.saved = sb("saved", [128, 2 * NFC, 2], F32)
        self.xl = [sb(f"xl{i}", [128, 8], F32) for i in range(2)]
        self.cmat = sb("cmat_s", [128, 768], BF16)
        self.poolc = sb("poolc_s", [128, 64], F32)
        self.par = sb("par_s", [128, DEPTH * NPL], F32)
        self.condb = sb("condb", [128, 16], F32)
        self.scond = sb("scond", [128, 16], BF16)
        self.modL = [sb(f"mod{i}", [128, 2, 48], F32) for i in range(2)]
        self.gs1L = [sb(f"gs1_{i}", [128, 2, 8], F32) for i in range(2)]
        self.gs2L = [sb(f"gs2_{i}", [128, 2, 8], F32) for i in range(2)]
        self.esinkL = [sb(f"esink{i}", [128, 8], F32) for i in range(2)]
        self.tail_a = sb("tail_a", [128, 2 * NFC], F32)
        self.tail_s = sb("tail_s", [128, NFC], F32)
        self.tail_act = sb("tail_act", [128, NFC], BF16)
        self.epsb = sb("epsb", [128, 1], F32)
        self.sbuf_left = nc.sbuf_bytes_remaining
        self.ps = [es.enter_context(nc.psum_tensor(f"ps{i}", [128, 512], F32)) for i in range(8)]
        self.pools = {"mm": [0, 1, 2, 3], "st": [4, 5, 0], "o": [6, 1], "n": [7], "aux": [4, 5, 6, 7], "qk": [4, 5], "pp": [2, 3]}
        self.prr = {k: 0 for k in self.pools}

        sc = self.sc
        sc.dma("pool", self.cmat[:, :], self.cmat_d[:, :], writes=["cmat"])
        sc.dma("sp", self.poolc[:, :], self.poolc_d[:, :], writes=["poolc"])
        sc.dma("sp", self.par[:, :], self.params_d[:, :], writes=["par"])
        sc.dma("sp", self.condb[:, :], self.cond[:, :], writes=["condb"])
        sc.op("dve", lambda e: e.memset(self.kT[0][:, :], 0.0), writes=[("kT", s) for s in range(2 + RING)])
        sc.op("dve", lambda e: e.memset(self.kT[1][:, :], 0.0), writes=[("kT", s) for s in range(2 + RING)])
        sc.op("dve", lambda e: e.memset(self.V[:, :, 0, 64:128], 1.0), writes=[("V", s) for s in range(2 + RING)])
        sc.op("dve", lambda e: e.memset(self.V[:, :, 1, 0:64], 1.0), writes=[("V", s) for s in range(2 + RING)])
        sc.op("act", lambda e: e.activation(out=self.scond[:, :], in_=self.condb[:, :], func=AF.Silu),
              reads=["condb"], writes=["scond"])
        sc.op("dve", lambda e: e.memset(self.epsb[:, :], EPS), writes=["epsb"])

        self.ones_mean = self.cmat[:, 0:128]
        self.bd_mean = self.cmat[:, 128:256]
        self.perm = self.cmat[:, 256:384]
        self.mask_next = self.cmat[:, 384:512]
        self.mask_prev = self.cmat[:, 512:640]
        self.ident = self.cmat[:, 640:768]

        self.l = 0
        for k in range(8):
            self.adaln_part(0, k)
        self.adaln_finish(0)
        for l in range(L):
            self.set_layer(l)
            last = (l == DEPTH - 1)
            self.src_x = self.xT if l == 0 else self.S[(l - 1) % 2]
            self.src_c = self.cxT if l == 0 else self.C[(l - 1) % 2]
            self.dst_x = self.outT if l == L - 1 else self.S[l % 2]
            self.dst_c = self.C[l % 2]
            self.xkey_src = ("X", "in" if l == 0 else (l - 1) % 2)
            self.xkey_dst = ("X", "out" if l == L - 1 else l % 2)
            gc = Grp(True, 0)
            grps = [Grp(False, g) for g in range(NG)]
            self.front_load(gc)
            self.front_load(grps[0])
            self.front_sq(gc)
            self.front_a(gc)
            self.front_b(gc)
            if not last:
                self.back_proj_a(gc, 0)
                self.back_proj_a(gc, 1)
                self.back_rest(gc, None)
                self.ffn(gc)
            self.front_load(grps[1])
            self.front_sq(grps[0])
            self.front_a(grps[0])
            self.front_b(grps[0])
            for g in range(NG):
                Gr = grps[g]
                nxt = grps[g + 1] if g + 1 < NG else None
                if g + 2 < NG:
                    self.front_load(grps[g + 2])
                if nxt is not None:
                    self.front_sq(nxt)
                self.back_proj_a(Gr, 0)
                if nxt is not None:
                    self.front_a(nxt)
                self.back_proj_a(Gr, 1)
                if nxt is not None:
                    self.front_b(nxt)
                self.back_rest(Gr, nxt)
                if l + 1 < L:
                    self.adaln_part(l + 1, g)
                self.ffn(Gr)
            if l + 1 < L:
                self.adaln_finish(l + 1)
        sc.finish()

        prog = sc.prog
        if self.dry:
            es.close()
            return None
        with nc.Block() as block:
            @block.tensor
            def _(e):
                for f in prog["pe"]:
                    f(e)

            @block.scalar
            def _(e):
                for f in prog["act"]:
                    f(e)

            @block.vector
            def _(e):
                for f in prog["dve"]:
                    f(e)

            @block.gpsimd
            def _(e):
                for f in prog["pool"]:
                    f(e)

            @block.sync
            def _(e):
                for f in prog["sp"]:
                    f(e)
        es.close()
        return nc

    def psum(self, pool):
        banks = self.pools[pool]
        i = self.prr[pool]
        self.prr[pool] = (i + 1) % len(banks)
        b = banks[i]
        return self.ps[b], ("ps", b)

    def _issue_upto(self, k):
        k = min(k, len(self.plan) - 1)
        while self.issued <= k:
            idx = self.issued
            kind, l, nm = self.plan[idx]
            i = idx % NSLOT
            slot = self.wslot[i]
            key = ("w", i)
            if kind == "w":
                o, e = POFF[nm]
                src = self.wst[l][o:o + 128 * e].rearrange("(p e) -> p e", e=e)
                self.sc.dma("pool", slot[:, 0:e], src, writes=[key])
            else:
                self.sc.dma("pool", slot[:, :], self.wmod[l][nm, :, :], writes=[key])
            self.issued += 1

    def wget(self, name, kind="w", layer=None):
        ent = (kind, self.l if layer is None else layer, name)
        if self.dry:
            self.plan.append(ent)
            return self.wslot[0], ("w", 0)
        idx = self.pidx
        assert self.plan[idx] == ent, (self.plan[idx], ent)
        self.pidx += 1
        self._issue_upto(idx + AHEAD)
        i = idx % NSLOT
        return self.wslot[i], ("w", i)

    def pcol(self, off, n=1):
        o = self.l * NPL + off
        return self.par[:, o:o + n]

    def mm_group(self, out_ap, pskey, terms, extra_reads=()):
        sc = self.sc
        nt = len(terms)
        tok = None
        for i, (lh, rh, rk) in enumerate(terms):
            st, sp_ = (i == 0), (i == nt - 1)
            tok = sc.op("pe", lambda e, lh=lh, rh=rh, st=st, sp_=sp_: e.matmul(out_ap, lhsT=lh, rhs=rh, start=st, stop=sp_),
                        reads=list(rk) + (list(extra_reads) if i == 0 else []),
                        writes=[pskey] if i == 0 else [], signal=sp_)
        self.sc.lastw[pskey] = tok
        return tok

    def pcol_l(self, l, off, n=1):
        o = l * NPL + off
        return self.par[:, o:o + n]

    def adaln_part(self, l, k):
        sc = self.sc
        p2 = l % 2
        mod, kmod = self.modL[p2], ("mod", p2)
        ps, pk = self.psum("mm")
        tok = None
        rhs_all = self.scond[:, :].rearrange("p (s k) -> p k s", s=2)
        first = True
        for i2 in range(3 * k, 3 * k + 3):
            slot, key = self.wget(i2, kind="m", layer=l)
            wv = slot[:, :].rearrange("p (k f) -> p k f", f=256)
            for cc in range(2):
                jj = 2 * (i2 - 3 * k) + cc
                for kc in range(8):
                    st, sp_ = (kc == 0), (kc == 7)
                    tok = sc.op("pe", lambda e, wv=wv, kc=kc, jj=jj, cc=cc, st=st, sp_=sp_: e.matmul(
                        ps[:, 2 * jj:2 * jj + 2], lhsT=wv[:, kc, cc * 128:(cc + 1) * 128], rhs=rhs_all[:, kc, :], start=st, stop=sp_),
                        reads=[key, "scond"], writes=[pk] if first else [], signal=sp_)
                    first = False
        sc.lastw[pk] = tok
        bm = self.pcol_l(l, 6 * k, 6)
        for s_ in range(2):
            sc.op("dve", lambda e, s_=s_: e.tensor_tensor(out=mod[:, s_, 6 * k:6 * k + 6], in0=ps[:, s_:12:2], in1=bm, op=ALU.add),
                  reads=["par"], writes=[pk, kmod])

    def adaln_finish(self, l):
        sc = self.sc
        p2 = l % 2
        mod, kmod = self.modL[p2], ("mod", p2)
        for (gsb, sco, ngo, nm) in ((self.gs1L[p2], 8, 48, ("gs1", p2)), (self.gs2L[p2], 32, 56, ("gs2", p2))):
            ng = self.pcol_l(l, ngo, 8)
            for s_ in range(2):
                sc.op("dve", lambda e, s_=s_, gsb=gsb, sco=sco, ng=ng: e.scalar_tensor_tensor(
                    out=gsb[:, s_, :], in0=mod[:, s_, sco:sco + 8], scalar=1.0, in1=ng, op0=ALU.add, op1=ALU.mult),
                    reads=[kmod, "par"], writes=[nm])
        esink = self.esinkL[p2]
        sk = self.pcol_l(l, 246, 8)
        sc.op("act", lambda e: e.activation(out=esink[:, :], in_=sk, func=AF.Exp), reads=["par"], writes=[("esink", p2)])

    def set_layer(self, l):
        p2 = l % 2
        self.l = l
        self.mod, self.gs1, self.gs2, self.esink = self.modL[p2], self.gs1L[p2], self.gs2L[p2], self.esinkL[p2]
        self.kmod, self.kgs1, self.kgs2, self.kesink = ("mod", p2), ("gs1", p2), ("gs2", p2), ("esink", p2)

    def norm_mod(self, Gr, gsb, gsname, sh_off, stats_done=False, interleave=False, sq_done=False):
        sc = self.sc
        s, n, si = Gr.slot, Gr.n, Gr.si
        xb, hb = self.xb[Gr.xs], self.hb[s]
        mod, kmod = self.mod, self.kmod
        xks = [("xb", Gr.xs, kc) for kc in range(8)]
        hks = [("hb", s, kc) for kc in range(8)]
        if not stats_done:
            if not sq_done:
                sc.op("act", lambda e: e.activation(out=hb[:, :, 0:n], in_=xb[:, :, 0:n], func=AF.Square), reads=xks, writes=hks)
            ps, pk = self.psum("n")
            self.mm_group(ps[:, 0:n], pk, [(self.ones_mean, hb[:, kc, 0:n], [hks[kc], "cmat"]) for kc in range(8)])
        else:
            ps, pk = self.nstat
        sc.op("act", lambda e: e.activation(out=self.rln[:, 0:n], in_=ps[:, 0:n], func=AF.Ln, bias=self.epsb[:, 0:1], scale=1.0),
              reads=["epsb"], writes=[pk, "rln"])
        sc.op("act", lambda e: e.activation(out=self.rstd[:, 0:n], in_=self.rln[:, 0:n], func=AF.Exp, scale=-0.5),
              reads=["rln"], writes=["rstd"])
        if interleave and not stats_done:
            self.tick()
        def chunk(kc):
            nt = self.ntmp[kc % 2]
            nk = ("ntmp", kc % 2)
            sc.op("dve", lambda e: e.tensor_tensor(out=nt[:, 0:n], in0=xb[:, kc, 0:n], in1=self.rstd[:, 0:n], op=ALU.mult),
                  reads=[xks[kc], "rstd"], writes=[nk])
            sc.op("act", lambda e: e.activation(out=hb[:, kc, 0:n], in_=nt[:, 0:n], func=AF.Identity,
                                                bias=mod[:, si, sh_off + kc:sh_off + kc + 1], scale=gsb[:, si, kc:kc + 1]),
                  reads=[nk, kmod, gsname], writes=[hks[kc]])

        for kc in range(8):
            if interleave:
                self.defer(kc + 1, lambda kc=kc: chunk(kc))
            else:
                chunk(kc)

    def defer(self, delay, fn):
        self.deferred.append([delay, fn])

    def tick(self):
        due = [d for d in self.deferred if d[0] <= 1]
        self.deferred = [[d[0] - 1, d[1]] for d in self.deferred if d[0] > 1]
        for d in due:
            d[1]()

    def flush(self):
        while self.deferred:
            self.tick()

    def qk_post(self, ps, pk, n, gain_off, rope, dst_ap, dst_keys):
        sc = self.sc
        par = self.qkpar
        self.qkpar ^= 1
        qsq, qln, qrs, qn, t1, t2 = self.qsq2[par], self.qln2[par], self.qrs2[par], self.qn2[par], self.t1b[par], self.t2b[par]
        kq, kr, kn = [(nm, par) for nm in ("qsq", "qrs", "qn")]
        kl, k1, k2 = ("qln", 0), ("t1", 0), ("t2", 0)
        gain = self.pcol(gain_off, 1)
        if not isinstance(dst_ap, list):
            dst_ap = [(0, 128, dst_ap)]
        sc.op("act", lambda e: e.activation(out=qsq[:, 0:n], in_=ps[:, 0:n], func=AF.Square), reads=[], writes=[pk, kq])

        def step1():
            pn, pnk = self.psum("n")
            self.mm_group(pn[:, 0:n], pnk, [(self.bd_mean, qsq[:, 0:n], [kq, "cmat"])])
            sc.op("act", lambda e: e.activation(out=qln[:, 0:n], in_=pn[:, 0:n], func=AF.Ln, bias=self.epsb[:, 0:1], scale=1.0),
                  reads=["epsb"], writes=[pnk, kl])
            sc.op("act", lambda e: e.activation(out=qrs[:, 0:n], in_=qln[:, 0:n], func=AF.Exp, scale=-0.5), reads=[kl], writes=[kr])
            if not rope:
                for (p0, p1, dap) in dst_ap:
                    sc.op("dve", lambda e, p0=p0, p1=p1, dap=dap: e.scalar_tensor_tensor(out=dap, in0=ps[p0:p1, 0:n], scalar=gain[p0:p1, :], in1=qrs[p0:p1, 0:n],
                                                                                   op0=ALU.mult, op1=ALU.mult),
                          reads=[kr, "par"], writes=[pk] + dst_keys)
            else:
                sc.op("dve", lambda e: e.scalar_tensor_tensor(out=qn[:, 0:n], in0=ps[:, 0:n], scalar=gain, in1=qrs[:, 0:n], op0=ALU.mult, op1=ALU.mult),
                      reads=[kr, "par"], writes=[pk, kn])

        def step2():
            pr, prk = self.psum("qk")
            self.mm_group(pr[:, 0:n], prk, [(self.perm, qn[:, 0:n], [kn, "cmat"])])
            sc.op("dve", lambda e: e.tensor_tensor(out=t1[:, 0:n], in0=qn[:, 0:n], in1=self.cosb[:, 0:n], op=ALU.mult), reads=[kn, "cosb"], writes=[k1])
            sc.op("dve", lambda e: e.tensor_tensor(out=t2[:, 0:n], in0=pr[:, 0:n], in1=self.sinb[:, 0:n], op=ALU.mult), reads=["sinb"], writes=[prk, k2])
            for (p0, p1, dap) in dst_ap:
                sc.op("dve", lambda e, p0=p0, p1=p1, dap=dap: e.tensor_tensor(out=dap, in0=t1[p0:p1, 0:n], in1=t2[p0:p1, 0:n], op=ALU.add),
                      reads=[k1, k2], writes=dst_keys)

        self.defer(1, step1)
        if rope:
            self.defer(3, step2)

    def front_load(self, Gr):
        sc = self.sc
        n, t0 = Gr.n, Gr.t0
        src = (self.src_c if Gr.is_ctx else self.src_x).rearrange("(k p) t -> p k t", p=128)
        sc.dma("sp", self.xb[Gr.xs][:, :, 0:n], src[:, :, t0:t0 + n],
               reads=[(self.xkey_src, Gr.name, pt_) for pt_ in ("m0", "m1", "m2", "m3", "m4", "m5", "e", "t")], writes=[("xb", Gr.xs, kc) for kc in range(8)])

    def front_sq(self, Gr):
        s, n = Gr.slot, Gr.n
        xb, hb = self.xb[Gr.xs], self.hb[s]
        self.sc.op("act", lambda e: e.activation(out=hb[:, :, 0:n], in_=xb[:, :, 0:n], func=AF.Square),
                   reads=[("xb", Gr.xs, kc) for kc in range(8)], writes=[("hb", s, kc) for kc in range(8)])

    def front_a(self, Gr):
        self.norm_mod(Gr, self.gs1, self.kgs1, 0, interleave=True, sq_done=True)

    def front_b(self, Gr):
        self.flush()
        sc = self.sc
        s, n, t0 = Gr.slot, Gr.n, Gr.t0
        if not Gr.is_ctx:
            sc.dma("sp", self.cosb[:, :], self.cos_d[:, t0:t0 + n], writes=["cosb"])
            sc.dma("sp", self.sinb[:, :], self.sin_d[:, t0:t0 + n], writes=["sinb"])
        hb = self.hb[s]
        hk = lambda kc: ("hb", s, kc)
        w, wk = self.wget("in0")
        wv = w[:, :].rearrange("p (k f) -> p k f", f=256)
        ps, pk = self.psum("mm")
        self.mm_group(ps[:, 0:n], pk, [(wv[:, kc, 0:128], hb[:, kc, 0:n], [wk, hk(kc)]) for kc in range(8)])
        s0 = Gr.kslots[0]
        kdst = [(0, 64, self.kT[0][0:64, s0 * 128:s0 * 128 + n]), (64, 128, self.kT[1][64:128, s0 * 128:s0 * 128 + n])]
        self.qk_post(ps, pk, n, 245, not Gr.is_ctx, kdst, [("kT", sl) for sl in Gr.kslots])
        ps2, pk2 = self.psum("mm")
        tok = None
        for b in range(Gr.nb):
            for kc in range(8):
                st, sp_ = (kc == 0), (kc == 7)
                tok = sc.op("pe", lambda e, b=b, kc=kc, st=st, sp_=sp_: e.matmul(
                    ps2[:, b * 128:(b + 1) * 128], lhsT=hb[:, kc, b * 128:(b + 1) * 128], rhs=wv[:, kc, 128:256], start=st, stop=sp_),
                    reads=[wk, hk(kc)], writes=[pk2] if (b == 0 and kc == 0) else [], signal=sp_)
            self.tick()
        self.flush()
        sc.lastw[pk2] = tok
        psv = ps2[:, 0:n].rearrange("p (b f) -> p b f", f=128)
        vk = [("V", sl) for sl in Gr.kslots]
        sc.op("act", lambda e: e.activation(out=self.V[:, s0:s0 + Gr.nb, 0, 0:64], in_=psv[:, :, 0:64], func=AF.Identity),
              writes=[pk2] + vk)
        sc.op("act", lambda e: e.activation(out=self.V[:, s0:s0 + Gr.nb, 1, 64:128], in_=psv[:, :, 64:128], func=AF.Identity),
              writes=[pk2] + vk)

    def back_proj_a(self, Gr, part):
        sc = self.sc
        s, n = Gr.slot, Gr.n
        hb = self.hb[s]
        hk = lambda kc: ("hb", s, kc)
        for pi in (range(2) if part == 0 else []):
            w, wk = self.wget(f"in{1 + pi}")
            wv = w[:, :].rearrange("p (k f) -> p k f", f=256)
            for cc in range(2):
                cq = pi * 2 + cc
                ps, pk = self.psum("mm")
                self.mm_group(ps[:, 0:n], pk, [(wv[:, kc, cc * 128:(cc + 1) * 128], hb[:, kc, 0:n], [wk, hk(kc)]) for kc in range(8)])
                self.tick()
                self.qk_post(ps, pk, n, 244, not Gr.is_ctx, self.qb[:, cq, 0:n], [("qb", cq)])
        for pi in ([] if part == 0 else range(8)):
            w, wk = self.wget(f"in{5 + pi}")
            wv = w[:, :].rearrange("p (k f) -> p k f", f=256)
            for cc in range(2):
                j = pi * 2 + cc
                ps, pk = self.psum("mm")
                self.mm_group(ps[:, 0:n], pk, [(wv[:, kc, cc * 128:(cc + 1) * 128], hb[:, kc, 0:n], [wk, hk(kc)]) for kc in range(8)])
                sc.op("act", lambda e, ps=ps, j=j: e.activation(out=self.gates[:, j, 0:n], in_=ps[:, 0:n], func=AF.Sigmoid),
                      writes=[pk, ("big", j)])
                self.tick()

    def back_proj_b_steps(self, Gr, nxt):
        sc = self.sc
        s, n = Gr.slot, Gr.n
        hb = self.hb[s]
        hk = lambda kc: ("hb", s, kc)
        pT = self.pT
        st8 = {}

        def init():
            if Gr.first:
                sc.op("dve", lambda e: e.memset(pT[:, :, 0:8], 0.0), writes=["pT"])
            else:
                sc.op("dve", lambda e: e.tensor_copy(out=pT[:, :, 0:8], in_=pT[:, :, n:n + 8]), reads=[], writes=["pT"])
            if nxt is None:
                sc.op("dve", lambda e: e.memset(pT[:, :, 8 + n:16 + n], 0.0), writes=["pT"])
            else:
                st8["psh"] = self.psum("n")

        def chunk(c):
            pi, cc = c // 2, c % 2
            if cc == 0:
                st8["w"] = self.wget(f"in{3 + pi}")
            w, wk = st8["w"]
            wv = w[:, :].rearrange("p (k f) -> p k f", f=256)
            ps, pk = self.psum("pp")
            self.mm_group(ps[:, 0:n], pk, [(wv[:, kc, cc * 128:(cc + 1) * 128], hb[:, kc, 0:n], [wk, hk(kc)]) for kc in range(8)])
            sc.op("dve", lambda e: e.tensor_copy(out=pT[:, c, 8:8 + n], in_=ps[:, 0:n]), writes=[pk, "pT"])
            if nxt is not None:
                psh, pkh = st8["psh"]
                hbn = self.hb[nxt.slot]
                tok = None
                for kc in range(8):
                    st, sp_ = (kc == 0), (kc == 7)
                    tok = sc.op("pe", lambda e, kc=kc, st=st, sp_=sp_: e.matmul(
                        psh[:, c * 8:(c + 1) * 8], lhsT=wv[:, kc, cc * 128:(cc + 1) * 128], rhs=hbn[:, kc, 0:8], start=st, stop=sp_),
                        reads=[wk, ("hb", nxt.slot, kc)], writes=[pkh] if (c == 0 and kc == 0) else [], signal=sp_)
                sc.lastw[pkh] = tok

        def fin():
            if nxt is not None:
                psh, pkh = st8["psh"]
                sc.op("dve", lambda e: e.tensor_copy(out=pT[:, :, 8 + n:16 + n], in_=psh[:, 0:32].rearrange("p (c f) -> p c f", f=8)),
                      writes=[pkh, "pT"])

        return [init] + [(lambda c=c: chunk(c)) for c in range(4)] + [fin]

    def back_rest(self, Gr, nxt):
        self.flush()
        steps = self.back_proj_b_steps(Gr, nxt) + [lambda: self.poolmix(Gr)]
        spacing = 1 if Gr.is_ctx else 4
        for i, st in enumerate(steps):
            self.defer(1 + i * spacing, st)
        self.attention(Gr)
        self.flush()
        self.poolmix_pe(Gr)
        self.merge_out(Gr)

    def attention(self, Gr):
        sc = self.sc
        n, g = Gr.n, Gr.g
        tiles = [(0, 0, n, []), (1, 0, n, [])]
        if not Gr.is_ctx:
            for j in range(4 * g - 1, 4 * g + 5):
                if j < 0 or j >= SEQ // 128:
                    continue
                lo = max(j - 1, 4 * g)
                hi = min(j + 1, 4 * g + 3)
                masks = []
                for i in range(lo, hi + 1):
                    if i == j - 1:
                        masks.append(((i - lo) * 128, self.mask_next))
                    elif i == j + 1:
                        masks.append(((i - lo) * 128, self.mask_prev))
                tiles.append((mslot(j), (lo - 4 * g) * 128, (hi + 1 - 4 * g) * 128, masks))
        units = [(cq, half) for cq in range(4) for half in range(2)]
        esink, kesink = self.esink, self.kesink
        nt = len(tiles)
        seq = [(u, ti) for u in range(len(units)) for ti in range(nt)]
        psS_of = {}

        def emit_S(idx):
            u, ti = seq[idx]
            cq, half = units[u]
            slot, c0, c1, masks = tiles[ti]
            N = c1 - c0
            psS, pkS = self.psum("st")
            nmm = 1 + len(masks)
            tok = sc.op("pe", lambda e: e.matmul(psS[:, 0:N], lhsT=self.kT[half][:, slot * 128:(slot + 1) * 128], rhs=self.qb[:, cq, c0:c1],
                                                 start=True, stop=(nmm == 1)),
                        reads=[("kT", slot), ("qb", cq)], writes=[pkS], signal=(nmm == 1))
            for mi, (co, mk) in enumerate(masks):
                lastm = (mi == len(masks) - 1)
                tok = sc.op("pe", lambda e, co=co, mk=mk, lastm=lastm: e.matmul(psS[:, co:co + 128], lhsT=self.ident, rhs=mk, start=False, stop=lastm),
                            reads=["cmat"], writes=[], signal=lastm)
            sc.lastw[pkS] = tok
            psS_of[idx] = (psS, pkS)

        def emit_norm(u, psO, pkO):
            cq, half = units[u]
            hbp = half * 64
            ob = 64 - hbp
            h = cq + 4 * half
            sc.op("act", lambda e: e.activation(out=self.lnden[ob:ob + 64, 0:n], in_=psO[ob:ob + 64, 0:n], func=AF.Ln,
                                                bias=esink[ob:ob + 64, h:h + 1], scale=1.0),
                  reads=[kesink], writes=[pkO, "lnden"])
            sc.op("act", lambda e: e.activation(out=self.rden[hbp:hbp + 64, 0:n], in_=self.lnden[ob:ob + 64, 0:n], func=AF.Exp, scale=-1.0),
                  reads=["lnden"], writes=["rden"])
            sc.op("dve", lambda e: e.tensor_tensor(out=self.attn[hbp:hbp + 64, cq, 0:n], in0=psO[hbp:hbp + 64, 0:n],
                                                   in1=self.rden[hbp:hbp + 64, 0:n], op=ALU.mult),
                  reads=["rden"], writes=[pkO, ("attn", cq)])

        pending = None
        cur = None
        emit_S(0)
        emit_S(1)
        for idx in range(len(seq)):
            u, ti = seq[idx]
            cq, half = units[u]
            slot, c0, c1, masks = tiles[ti]
            N = c1 - c0
            if idx + 2 < len(seq):
                emit_S(idx + 2)
            if ti == 0:
                cur = self.psum("o")
            psO, pkO = cur
            psS, pkS = psS_of.pop(idx)
            pt = self.PT[self.ptr]
            ptk = ("PT", self.ptr)
            self.ptr = (self.ptr + 1) % len(self.PT)
            sc.op("act", lambda e, pt=pt, psS=psS, N=N: e.activation(out=pt[:, 0:N], in_=psS[:, 0:N], func=AF.Exp, scale=0.125),
                  writes=[pkS, ptk])
            st, sp_ = (ti == 0), (ti == nt - 1)
            tokO = sc.op("pe", lambda e, psO=psO, slot=slot, half=half, pt=pt, N=N, c0=c0, c1=c1, st=st, sp_=sp_: e.matmul(
                psO[:, c0:c1], lhsT=self.V[:, slot, half, :], rhs=pt[:, 0:N], start=st, stop=sp_),
                reads=[("V", slot), ptk], writes=[pkO] if ti == 0 else [], signal=sp_)
            self.tick()
            if ti == 1 and pending is not None:
                emit_norm(*pending)
                pending = None
            if ti == nt - 1:
                sc.lastw[pkO] = tokO
                pending = (u, psO, pkO)
        emit_norm(*pending)

    def poolmix(self, Gr):
        sc = self.sc
        n = Gr.n
        pT = self.pT
        A, B_, C8, S_ = self.f4
        fk = ["f4_0", "f4_1", "f4_2", "f4_3"]

        def add(out_ap, a, b, reads, writes):
            sc.op("dve", lambda e: e.tensor_tensor(out=out_ap, in0=a, in1=b, op=ALU.add), reads=reads, writes=writes)

        for c, w in enumerate((2, 4, 8, 16)):
            P = pT[:, c, :]
            if c == 0:
                add(S_[:, 0:n], P[:, 7:7 + n], P[:, 8:8 + n], ["pT"], [fk[3]])
            elif c == 1:
                add(A[:, 0:n + 2], P[:, 6:8 + n], P[:, 7:9 + n], ["pT"], [fk[0]])
                add(S_[:, 0:n], A[:, 0:n], A[:, 2:n + 2], [fk[0]], [fk[3]])
            elif c == 2:
                add(A[:, 0:n + 6], P[:, 4:10 + n], P[:, 5:11 + n], ["pT"], [fk[0]])
                add(B_[:, 0:n + 4], A[:, 0:n + 4], A[:, 2:n + 6], [fk[0]], [fk[1]])
                add(S_[:, 0:n], B_[:, 0:n], B_[:, 4:n + 4], [fk[1]], [fk[3]])
            else:
                add(A[:, 0:n + 14], P[:, 0:14 + n], P[:, 1:15 + n], ["pT"], [fk[0]])
                add(B_[:, 0:n + 12], A[:, 0:n + 12], A[:, 2:n + 14], [fk[0]], [fk[1]])
                add(C8[:, 0:n + 8], B_[:, 0:n + 8], B_[:, 4:n + 12], [fk[1]], [fk[2]])
                add(S_[:, 0:n], C8[:, 0:n], C8[:, 8:n + 8], [fk[2]], [fk[3]])
            sc.op("dve", lambda e, c=c, w=w, P=P: e.scalar_tensor_tensor(out=self.dT[:, c, 0:n], in0=S_[:, 0:n], scalar=1.0 / w, in1=P[:, 8:8 + n],
                                                                        op0=ALU.mult, op1=ALU.subtract),
                  reads=[fk[3], "pT"], writes=[("dT", c)])
            if Gr.first:
                sc.op("dve", lambda e, c=c: e.tensor_tensor(out=A[:, 0:8], in0=S_[:, 0:8], in1=self.poolc[:, c * 8:(c + 1) * 8], op=ALU.mult),
                      reads=[fk[3], "poolc"], writes=[fk[0]])
                sc.op("dve", lambda e, c=c, P=P: e.tensor_tensor(out=self.dT[:, c, 0:8], in0=A[:, 0:8], in1=P[:, 8:16], op=ALU.subtract),
                      reads=[fk[0], "pT"], writes=[("dT", c)])
            if Gr.last:
                sc.op("dve", lambda e, c=c: e.tensor_tensor(out=A[:, 0:8], in0=S_[:, n - 8:n], in1=self.poolc[:, 32 + c * 8:32 + (c + 1) * 8], op=ALU.mult),
                      reads=[fk[3], "poolc"], writes=[fk[0]])
                sc.op("dve", lambda e, c=c, P=P: e.tensor_tensor(out=self.dT[:, c, n - 8:n], in0=A[:, 0:8], in1=P[:, n:n + 8], op=ALU.subtract),
                      reads=[fk[0], "pT"], writes=[("dT", c)])

    def poolmix_pe(self, Gr):
        sc = self.sc
        n = Gr.n
        w, wk = self.wget("pool")
        wv = w[:, 0:512].rearrange("p (g d) -> p g d", d=128)
        for c in range(4):
            ps, pk = self.psum("mm")
            self.mm_group(ps[:, 0:n], pk, [(wv[:, c, :], self.dT[:, c, 0:n], [wk, ("dT", c)])])
            sc.op("act", lambda e, ps=ps, c=c, psc=self.pcol(64 + c, 1): e.activation(out=self.po[:, c, 0:n], in_=ps[:, 0:n], func=AF.Identity, scale=psc),
                  reads=["par"], writes=[pk, ("po", c)])

    def merge_out(self, Gr):
        sc = self.sc
        s, n = Gr.slot, Gr.n
        si = Gr.si
        xb = self.xb[Gr.xs]
        xks = [("xb", Gr.xs, kc) for kc in range(8)]
        hb = self.hb[s]
        psn, pkn = self.psum("n")

        def stat_mm(c):
            st, sp_ = (c == 0), (c == 7)
            return sc.op("pe", lambda e: e.matmul(psn[:, 0:n], lhsT=self.ones_mean, rhs=hb[:, c, 0:n], start=st, stop=sp_),
                         reads=[("hb", s, c), "cmat"], writes=[pkn] if c == 0 else [], signal=sp_)

        for hf in range(2):
            wa, wak = self.wget(f"bra{hf}")
            wp, wpk = self.wget(f"brp{hf}")
            wav = wa[:, :].rearrange("p (k f) -> p k f", f=512)
            wpv = wp[:, :].rearrange("p (k f) -> p k f", f=512)
            for cc in range(4):
                c = hf * 4 + cc
                psa, pka = self.psum("mm")
                self.mm_group(psa[:, 0:n], pka, [(wav[:, k, cc * 128:(cc + 1) * 128], self.attn[:, k, 0:n], [wak, ("attn", k)]) for k in range(4)])
                psp, pkp = self.psum("mm")
                self.mm_group(psp[:, 0:n], pkp, [(wpv[:, k, cc * 128:(cc + 1) * 128], self.po[:, k, 0:n], [wpk, ("po", k)]) for k in range(4)])
                sc.op("dve", lambda e, psa=psa, c=c: e.tensor_tensor(out=self.t1[:, 0:n], in0=psa[:, 0:n], in1=self.gates[:, c, 0:n], op=ALU.mult),
                      reads=[("big", c)], writes=[pka, ("t1", 0)])
                sc.op("dve", lambda e, psp=psp, c=c: e.tensor_tensor(out=self.t2[:, 0:n], in0=psp[:, 0:n], in1=self.gates[:, 8 + c, 0:n], op=ALU.mult),
                      reads=[("big", 8 + c)], writes=[pkp, ("t2", 0)])
                sc.op("dve", lambda e, c=c: e.tensor_tensor(out=self.yb[:, c, 0:n], in0=self.t1[:, 0:n], in1=self.t2[:, 0:n], op=ALU.add),
                      reads=[("t1", 0), ("t2", 0)], writes=[("big", 16 + c)])
        for pi in range(4):
            w, wk = self.wget(f"out{pi}")
            wv = w[:, :].rearrange("p (k f) -> p k f", f=256)
            for cc in range(2):
                c = pi * 2 + cc
                ps, pk = self.psum("mm")
                self.mm_group(ps[:, 0:n], pk, [(wv[:, k, cc * 128:(cc + 1) * 128], self.yb[:, k, 0:n], [wk, ("big", 16 + k)]) for k in range(8)])
                if c >= 1:
                    stat_mm(c - 1)
                g1 = self.mod[:, si, 16 + c:17 + c]
                sc.op("dve", lambda e, ps=ps, c=c, g1=g1: e.scalar_tensor_tensor(out=xb[:, c, 0:n], in0=ps[:, 0:n], scalar=g1,
                                                                                in1=xb[:, c, 0:n], op0=ALU.mult, op1=ALU.add),
                      reads=[self.kmod], writes=[pk, xks[c]])
                sc.op("act", lambda e, c=c: e.activation(out=hb[:, c, 0:n], in_=xb[:, c, 0:n], func=AF.Square), reads=[xks[c]], writes=[("hb", s, c)])
        tokn = stat_mm(7)
        sc.lastw[pkn] = tokn
        self.nstat = (psn, pkn)
        xl = self.xl[Gr.g % 2 if not Gr.is_ctx else 0]
        sc.op("dve", lambda e: e.tensor_copy(out=xl[:, :], in_=xb[:, :, n - 1]), reads=xks, writes=[("xl", Gr.g % 2 if not Gr.is_ctx else 0)])

    def ffn(self, Gr):
        sc = self.sc
        s, n, si = Gr.slot, Gr.n, Gr.si
        xb = self.xb[Gr.xs]
        xks = [("xb", Gr.xs, kc) for kc in range(8)]
        hb = self.hb[s]
        hk = lambda kc: ("hb", s, kc)
        self.norm_mod(Gr, self.gs2, self.kgs2, 24, stats_done=True)
        if Gr.first:
            sc.op("dve", lambda e: e.memset(self.saved[:, :, :], 0.0), writes=["saved"])
        w0T, w1T = self.pcol(68, 44), self.pcol(68 + 44, 44)
        corr = self.corr
        sc.op("dve", lambda e: e.tensor_tensor(out=corr[:, :, 1], in0=self.saved[:, :, 1], in1=w0T, op=ALU.mult), reads=["saved", "par"], writes=["corr"])
        sc.op("dve", lambda e: e.tensor_tensor(out=corr[:, :, 0], in0=self.saved[:, :, 0], in1=w0T, op=ALU.mult), reads=["saved", "par"], writes=["corr"])
        sc.op("dve", lambda e: e.tensor_tensor(out=self.tail_a[:, :], in0=self.saved[:, :, 1], in1=w1T, op=ALU.mult), reads=["saved", "par"], writes=["tail_a"])
        sc.op("dve", lambda e: e.tensor_tensor(out=corr[:, :, 0], in0=corr[:, :, 0], in1=self.tail_a[:, :], op=ALU.add), reads=["tail_a"], writes=["corr"])
        bigkeys = [("big", j) for j in range(24)]
        for i in range(NFC):
            w, wk = self.wget(f"up{i}")
            wv = w[:, :].rearrange("p (k f) -> p k f", f=256)
            accs = []
            for ab in range(2):
                ch = i + ab * NFC
                ps, pk = self.psum("mm")
                self.mm_group(ps[:, 0:n], pk, [(wv[:, kc, ab * 128:(ab + 1) * 128], hb[:, kc, 0:n], [wk, hk(kc)]) for kc in range(8)])
                acc = self.f4[(i % 2) * 2 + ab]
                ak = f"f4_{(i % 2) * 2 + ab}"
                w0, w1, w2, bb = self.pcol(68 + ch), self.pcol(68 + 44 + ch), self.pcol(68 + 88 + ch), self.pcol(200 + ch)
                sv = self.saved[:, ch, :]
                sc.op("act", lambda e, ps=ps, acc=acc, w2=w2, bb=bb: e.activation(out=acc[:, 0:n], in_=ps[:, 0:n], func=AF.Identity, bias=bb, scale=w2),
                      reads=["par"], writes=[pk, ak])
                sc.op("act", lambda e, ps=ps, sv=sv: e.activation(out=sv[:, 0:2], in_=ps[:, n - 2:n], func=AF.Identity), reads=["corr"], writes=[pk, "saved"])
                sc.op("dve", lambda e, ps=ps, acc=acc, w1=w1: e.scalar_tensor_tensor(out=acc[:, 1:n], in0=ps[:, 0:n - 1], scalar=w1, in1=acc[:, 1:n],
                                                                                    op0=ALU.mult, op1=ALU.add),
                      reads=["par"], writes=[pk, ak])
                sc.op("dve", lambda e, ps=ps, acc=acc, w0=w0: e.scalar_tensor_tensor(out=acc[:, 2:n], in0=ps[:, 0:n - 2], scalar=w0, in1=acc[:, 2:n],
                                                                                    op0=ALU.mult, op1=ALU.add),
                      reads=["par"], writes=[pk, ak])
                sc.op("dve", lambda e, acc=acc, ch=ch: e.tensor_tensor(out=acc[:, 0:2], in0=acc[:, 0:2], in1=corr[:, ch, :], op=ALU.add),
                      reads=["corr"], writes=[ak])
                accs.append((acc, ak))
            (aa, aak), (ab_, abk) = accs
            sil, silk = self.sil2[i % 2], ("sil", i % 2)
            sc.op("act", lambda e, aa=aa, sil=sil: e.activation(out=sil[:, 0:n], in_=aa[:, 0:n], func=AF.Silu), reads=[aak], writes=[silk])
            sc.op("pool", lambda e, ab_=ab_, i=i, sil=sil: e.tensor_tensor(out=self.actT[:, i, 0:n], in0=sil[:, 0:n], in1=ab_[:, 0:n], op=ALU.mult),
                  reads=[silk, abk], writes=bigkeys if i == 0 else [("act", i)])
        actkeys = bigkeys + [("act", i) for i in range(1, NFC)]
        c_first = 1 if Gr.first else 0
        xlp = self.xl[(Gr.g - 1) % 2]
        xlpk = ("xl", (Gr.g - 1) % 2)
        hbf = hb[:, :, :].rearrange("p c t -> p (c t)").bitcast(F32).rearrange("p (c t) -> p c t", t=G)
        if Gr.last:
            self.tail_prep()
            pst, pkt = self.psum("mm")
            tokt = None
        attn_f = self.attn[:, :, :].rearrange("p c t -> p (c t)").bitcast(F32).rearrange("p (c t) -> p c t", t=G)
        dT_f = self.dT[:, :, :].rearrange("p c t -> p (c t)").bitcast(F32).rearrange("p (c t) -> p c t", t=G)
        KA = NFC // 2
        for fp in range(4):
            wd = [self.wget(f"dn{fp}_{q}") for q in range(3)]

            def wv_of(kc):
                w, wk = wd[kc // 8]
                return w[:, 0:(8 if kc < 16 else 6) * 256].rearrange("p (k f) -> p k f", f=256), wk

            if Gr.last:
                for cc in range(2):
                    c = fp * 2 + cc
                    for kc in range(NFC):
                        wv, wk = wv_of(kc)
                        st, sp_ = (kc == 0), (kc == NFC - 1)
                        tokt = sc.op("pe", lambda e, c=c, kc=kc, wv=wv, cc=cc, st=st, sp_=sp_: e.matmul(
                            pst[:, 2 * c:2 * c + 1], lhsT=wv[:, kc % 8, cc * 128:(cc + 1) * 128], rhs=self.tail_act[:, kc:kc + 1], start=st, stop=sp_),
                            reads=[wk, "tail_act"], writes=[pkt] if (c == 0 and kc == 0) else [], signal=sp_)
            banks = [self.psum("aux") for _ in range(2)]
            toks = [None, None]
            for (k0, k1) in ((0, KA), (KA, NFC)):
                for cc in range(2):
                    ps, pk = banks[cc]
                    for kc in range(k0, k1):
                        wv, wk = wv_of(kc)
                        st, sp_ = (kc == 0), (kc == NFC - 1)
                        toks[cc] = sc.op("pe", lambda e, ps=ps, kc=kc, wv=wv, cc=cc, st=st, sp_=sp_: e.matmul(
                            ps[:, 0:n], lhsT=wv[:, kc % 8, cc * 128:(cc + 1) * 128], rhs=self.actT[:, kc, 0:n], start=st, stop=sp_),
                            reads=[wk] + (bigkeys if kc == 0 else [("act", kc)]), writes=[pk] if kc == 0 else [], signal=sp_)
            for cc in range(2):
                c = fp * 2 + cc
                ps, pk = banks[cc]
                sc.lastw[pk] = toks[cc]
                g2 = self.mod[:, si, 40 + c:41 + c]
                if c < 4:
                    xo, xok = self.f4[c][:, 0:n - 1], [f"f4_{c}"]
                elif c < 6:
                    xo, xok = attn_f[:, c - 4, 0:n - 1], [("attn", 2 * (c - 4)), ("attn", 2 * (c - 4) + 1)]
                else:
                    xo, xok = dT_f[:, c - 6, 0:n - 1], [("dT", 2 * (c - 6)), ("dT", 2 * (c - 6) + 1)]
                sc.op("dve", lambda e, ps=ps, c=c, g2=g2, xo=xo: e.scalar_tensor_tensor(out=xo, in0=ps[:, 1:n], scalar=g2, in1=xb[:, c, 0:n - 1],
                                                                                       op0=ALU.mult, op1=ALU.add),
                      reads=[self.kmod, xks[c]], writes=[pk] + xok)
                if not Gr.first:
                    sc.op("dve", lambda e, ps=ps, c=c, g2=g2: e.scalar_tensor_tensor(out=xlp[:, c:c + 1], in0=ps[:, 0:1], scalar=g2, in1=xlp[:, c:c + 1],
                                                                                    op0=ALU.mult, op1=ALU.add),
                          reads=[self.kmod], writes=[pk, xlpk])
        dst = (self.dst_c if Gr.is_ctx else self.dst_x).rearrange("(k p) t -> p k t", p=128)
        t0 = Gr.t0
        dkey = (self.xkey_dst, Gr.name, "t")
        for c in range(4):
            sc.dma("sp", dst[:, c, t0:t0 + n - 1], self.f4[c][:, 0:n - 1], reads=[f"f4_{c}"], writes=[(self.xkey_dst, Gr.name, f"m{c}")])
        sc.dma("sp", dst[:, 4:6, t0:t0 + n - 1], attn_f[:, :, 0:n - 1], reads=[("attn", k) for k in range(4)], writes=[(self.xkey_dst, Gr.name, "m4")])
        sc.dma("sp", dst[:, 6:8, t0:t0 + n - 1], dT_f[:, :, 0:n - 1], reads=[("dT", k) for k in range(4)], writes=[(self.xkey_dst, Gr.name, "m5")])
        if not Gr.first:
            pkey = (self.xkey_dst, str(Gr.g - 1), "e")
            sc.dma("sp", dst[:, :, t0 - 1:t0], xlp[:, :].rearrange("p (k o) -> p k o", o=1), reads=[xlpk], writes=[pkey], slow=True)
        if Gr.last:
            sc.lastw[pkt] = tokt
            self.tail_finish(Gr, dst, dkey, pst, pkt)

    def tail_prep(self):
        sc = self.sc
        w0, w1, bb = self.pcol(68, 44), self.pcol(68 + 44, 44), self.pcol(200, 44)
        ta = self.tail_a
        sc.op("dve", lambda e: e.tensor_tensor(out=ta[:, :], in0=self.saved[:, :, 0], in1=w0, op=ALU.mult), reads=["saved", "par"], writes=["tail_a"])
        sc.op("dve", lambda e: e.tensor_tensor(out=self.tail_s[:, :], in0=self.saved[:, 0:NFC, 1], in1=w1[:, 0:NFC], op=ALU.mult), reads=["saved", "par"], writes=["tail_s"])
        sc.op("dve", lambda e: e.tensor_tensor(out=ta[:, 0:NFC], in0=ta[:, 0:NFC], in1=self.tail_s[:, :], op=ALU.add), reads=["tail_s"], writes=["tail_a"])
        sc.op("dve", lambda e: e.tensor_tensor(out=self.tail_s[:, :], in0=self.saved[:, NFC:2 * NFC, 1], in1=w1[:, NFC:2 * NFC], op=ALU.mult), reads=["saved", "par"], writes=["tail_s"])
        sc.op("dve", lambda e: e.tensor_tensor(out=ta[:, NFC:2 * NFC], in0=ta[:, NFC:2 * NFC], in1=self.tail_s[:, :], op=ALU.add), reads=["tail_s"], writes=["tail_a"])
        sc.op("dve", lambda e: e.tensor_tensor(out=ta[:, :], in0=ta[:, :], in1=bb, op=ALU.add), reads=["par"], writes=["tail_a"])
        sc.op("act", lambda e: e.activation(out=self.tail_s[:, :], in_=ta[:, 0:NFC], func=AF.Silu), reads=["tail_a"], writes=["tail_s"])
        sc.op("dve", lambda e: e.tensor_tensor(out=self.tail_act[:, :], in0=self.tail_s[:, :], in1=ta[:, NFC:2 * NFC], op=ALU.mult),
              reads=["tail_s", "tail_a"], writes=["tail_act"])

    def tail_finish(self, Gr, dst, dkey, ps, pk):
        sc = self.sc
        n, si = Gr.n, Gr.si
        xi = Gr.g % 2 if not Gr.is_ctx else 0
        xl, xlk = self.xl[xi], ("xl", xi)
        g2all = self.mod[:, si, 40:48]
        sc.op("dve", lambda e: e.tensor_tensor(out=self.tail_a[:, 0:8], in0=ps[:, 0:16:2], in1=g2all, op=ALU.mult),
              reads=[self.kmod], writes=[pk, "tail_a"])
        sc.op("dve", lambda e: e.tensor_tensor(out=xl[:, :], in0=xl[:, :], in1=self.tail_a[:, 0:8], op=ALU.add), reads=["tail_a"], writes=[xlk])
        t_last = Gr.t0 + n - 1
        sc.dma("sp", dst[:, :, t_last:t_last + 1], xl[:, :].rearrange("p (k o) -> p k o", o=1), reads=[xlk], writes=[dkey], slow=True)


def build_nc(n_layers=DEPTH, dbg=False):
    dry = Builder(n_layers, dbg, plan=None)
    dry.build()
    return Builder(n_layers, dbg, plan=dry.plan).build()


_CACHE = {}


def prepare_inputs(x, c, ctx, c_ctx, w_mod, b_mod, norm1_g, norm2_g, w_in, q_gain, k_gain, sink,
                   w_pool, pool_scale, w_br_attn, w_br_pool, w_out, w_up, conv_w, conv_b, w_down, n_layers=DEPTH):
    f = lambda a: np.asarray(a, dtype=np.float32)
    x, c, ctx, c_ctx = f(x), f(c), f(ctx), f(c_ctx)
    w_mod, b_mod, norm1_g, norm2_g, w_in = f(w_mod), f(b_mod), f(norm1_g), f(norm2_g), f(w_in)
    q_gain, k_gain, sink, w_pool, pool_scale = f(q_gain), f(k_gain), f(sink), f(w_pool), f(pool_scale)
    w_br_attn, w_br_pool, w_out, w_up, conv_w, conv_b, w_down = f(w_br_attn), f(w_br_pool), f(w_out), f(w_up), f(conv_w), f(conv_b), f(w_down)
    cm, cosT, sinT, pc = build_consts()
    params = build_params(b_mod, norm1_g, norm2_g, pool_scale, conv_w, conv_b, q_gain, k_gain, sink)
    shared = {"params": params, "cmat": cm, "cosT": cosT, "sinT": sinT, "poolc": pc}
    for l in range(n_layers):
        shared[f"wst{l}"] = build_wstream(l, w_in, w_pool, w_br_attn, w_br_pool, w_out, w_up, w_down)
        shared[f"wmod{l}"] = build_wmod_stream(l, w_mod)
    in_maps = []
    for b in range(NCORES):
        m = dict(shared)
        m["xT"] = np.ascontiguousarray(x[b].T)
        m["cxT"] = np.ascontiguousarray(ctx[b].T)
        cond = np.zeros((128, 16), np.float32)
        cond[:, 0:8] = c[b].reshape(8, 128).T
        cond[:, 8:16] = c_ctx.reshape(8, 128).T
        m["cond"] = cond
        in_maps.append(m)
    return in_maps


def kernel(**inputs):
    in_maps = prepare_inputs(**inputs)
    if "nc" not in _CACHE:
        _CACHE["nc"] = build_nc(DEPTH)
    nc = _CACHE["nc"]
    res = run_bass_kernel_spmd(nc, in_maps, core_ids=list(range(NCORES)))
    out = np.stack([np.ascontiguousarray(r["outT"].T) for r in res.results], axis=0)
    return out.astype(np.float32)
```

```python
import numpy as np
from contextlib import ExitStack
import concourse.bass as bass
import concourse.mybir as mybir
from concourse.bass_utils import run_bass_kernel_spmd

F32 = mybir.dt.float32
BF16 = mybir.dt.bfloat16
AF = mybir.ActivationFunctionType
ALU = mybir.AluOpType

D = 1024
SEQ = 4096
CTX = 256
DEPTH = 4
NCORES = 8
GRID_W = 64
HD = 64
DFF = 2816
NFC = DFF // 128
EPS = 1e-6
G = 512
NG = SEQ // G
RING = 12
NSLOT = 7
SLOT_E = 2048
NPL = 256

PIECES = []
for _i in range(13):
    PIECES.append((f"in{_i}", 2048))
PIECES.append(("pool", 512))
PIECES += [("bra0", 2048), ("bra1", 2048), ("brp0", 2048), ("brp1", 2048)]
PIECES += [(f"out{_i}", 2048) for _i in range(4)]
PIECES += [(f"up{_i}", 2048) for _i in range(NFC)]
for _f in range(4):
    PIECES += [(f"dn{_f}_0", 2048), (f"dn{_f}_1", 2048), (f"dn{_f}_2", 1536)]
POFF = {}
_o = 0
for _n, _e in PIECES:
    POFF[_n] = (_o, _e)
    _o += 128 * _e
WSTREAM_LEN = _o


def _piece(W, kchunks, cols):
    K = W.shape[0] // 128
    Wr = W.reshape(K, 128, W.shape[1])[kchunks][:, :, cols]
    return np.ascontiguousarray(Wr.transpose(1, 0, 2)).reshape(128, -1)


def _qperm():
    idx = []
    for cq in range(4):
        for half in range(2):
            h = cq + 4 * half
            idx += list(range(h * 64, (h + 1) * 64))
    return np.array(idx)


def build_wstream(l, w_in, w_pool, w_br_attn, w_br_pool, w_out, w_up, w_down):
    out = np.empty(WSTREAM_LEN, np.float32)

    def put(name, arr):
        o, e = POFF[name]
        assert arr.shape == (128, e), (name, arr.shape, e)
        out[o:o + 128 * e] = arr.reshape(-1)

    qp = _qperm()
    cols = np.concatenate([np.arange(512, 640), np.arange(640, 768), qp, np.arange(768, 1280), np.arange(1280, 3328)])
    Wp = w_in[l][:, cols]
    k8 = list(range(8))
    for i in range(13):
        put(f"in{i}", _piece(Wp, k8, np.arange(i * 256, (i + 1) * 256)))
    put("pool", np.ascontiguousarray(w_pool[l].transpose(1, 0, 2)).reshape(128, 512))
    bra = w_br_attn[l][qp, :]
    brp = w_br_pool[l]
    for hf in range(2):
        put(f"bra{hf}", _piece(bra, [0, 1, 2, 3], np.arange(hf * 512, (hf + 1) * 512)))
        put(f"brp{hf}", _piece(brp, [0, 1, 2, 3], np.arange(hf * 512, (hf + 1) * 512)))
    for i in range(4):
        put(f"out{i}", _piece(w_out[l], k8, np.arange(i * 256, (i + 1) * 256)))
    for i in range(NFC):
        cc = np.concatenate([np.arange(i * 128, (i + 1) * 128), np.arange(DFF + i * 128, DFF + (i + 1) * 128)])
        put(f"up{i}", _piece(w_up[l], k8, cc))
    for f in range(4):
        cc = np.arange(f * 256, (f + 1) * 256)
        put(f"dn{f}_0", _piece(w_down[l], list(range(0, 8)), cc))
        put(f"dn{f}_1", _piece(w_down[l], list(range(8, 16)), cc))
        put(f"dn{f}_2", _piece(w_down[l], list(range(16, 22)), cc))
    return out


def build_wmod_stream(l, w_mod):
    W = w_mod[l].reshape(8, 128, 24, 256)
    return np.ascontiguousarray(W.transpose(2, 1, 0, 3)).reshape(24, 128, 2048)


def build_params(b_mod, norm1_g, norm2_g, pool_scale, conv_w, conv_b, q_gain, k_gain, sink):
    P = np.zeros((128, DEPTH * NPL), np.float32)
    for l in range(DEPTH):
        o = l * NPL
        P[:, o:o + 48] = b_mod[l].reshape(48, 128).T
        P[:, o + 48:o + 56] = norm1_g[l].reshape(8, 128).T
        P[:, o + 56:o + 64] = norm2_g[l].reshape(8, 128).T
        P[:, o + 64:o + 68] = pool_scale[l].reshape(4, 128).T
        for j in range(3):
            P[:, o + 68 + j * 44:o + 68 + (j + 1) * 44] = conv_w[l, j].reshape(44, 128).T
        P[:, o + 200:o + 244] = conv_b[l].reshape(44, 128).T
        P[:, o + 244] = np.tile(q_gain[l], 2)
        P[:, o + 245] = np.tile(k_gain[l], 2)
        P[:, o + 246:o + 254] = sink[l][None, :]
    return P


def build_consts():
    cm = np.zeros((128, 6 * 128), np.float32)
    cm[:, 0:128] = 1.0 / 1024.0
    for hb in (0, 64):
        cm[hb:hb + 64, 128 + hb:128 + hb + 64] = 1.0 / 64.0
    for m in range(128):
        d = m % 64
        half = (d % 32) // 16
        partner = m + 16 if half == 0 else m - 16
        cm[partner, 256 + m] = 1.0
    kj = np.arange(128)[:, None]
    qi = np.arange(128)[None, :]
    cm[:, 384:512] = ((kj <= qi).astype(np.float32) - 1.0) * 30000.0
    cm[:, 512:640] = ((kj >= qi).astype(np.float32) - 1.0) * 30000.0
    cm[:, 640:768] = np.eye(128, dtype=np.float32)
    n_freq = HD // 4
    inv = (np.float32(10000.0) ** (-(np.arange(n_freq, dtype=np.float32)) / np.float32(n_freq))).astype(np.float32)
    t = np.arange(SEQ)
    row = (t // GRID_W).astype(np.float32)
    col = (t % GRID_W).astype(np.float32)
    cosT = np.zeros((128, SEQ), np.float32)
    sinT = np.zeros((128, SEQ), np.float32)
    for p in range(128):
        d = p % 64
        axis = d // 32
        half = (d % 32) // 16
        f = d % 16
        ang = ((row if axis == 0 else col) * inv[f]).astype(np.float32)
        cosT[p] = np.cos(ang).astype(np.float32)
        s = np.sin(ang).astype(np.float32)
        sinT[p] = -s if half == 0 else s
    pc = np.zeros((128, 64), np.float32)
    for c, w in enumerate((2, 4, 8, 16)):
        for i in range(8):
            cntl = (i + w // 2) - max(i - w // 2, 0)
            pc[:, c * 8 + i] = 1.0 / cntl
            tt = -8 + i
            hi = min(tt + w // 2, 0)
            lo = tt - w // 2
            pc[:, 32 + c * 8 + i] = 1.0 / (hi - lo)
    return cm, cosT, sinT, pc


class Tok:
    __slots__ = ("eng", "sem", "val")

    def __init__(self, eng):
        self.eng = eng
        self.sem = None
        self.val = None


ENGS = ("pe", "act", "dve", "pool", "sp")
SEM_ROLL = 30000


class Sched:
    def __init__(self, nc, es):
        self.nc = nc
        self.es = es
        self.prog = {e: [] for e in ENGS}
        self.esem = {}
        self.ecount = {}
        self.nroll = {}
        for e in ("pe", "act", "dve", "pool"):
            self.esem[e] = es.enter_context(nc.semaphore(f"c_{e}_0"))
            self.ecount[e] = 0
            self.nroll[e] = 0
        self.pending = {e: [] for e in ENGS}
        self.waited = {e: {} for e in ENGS}
        self.lastw = {}
        self.readers = {}
        self.dsem = {q: [es.enter_context(nc.semaphore(f"d_{q}_{i}")) for i in range(12)] for q in ("sp", "pool")}
        self.dcount = {q: [0] * 12 for q in ("sp", "pool")}
        self.drr = {"sp": 0, "pool": 0}
        self.ninstr = {e: 0 for e in ENGS}

    def _deps(self, reads, writes):
        toks = []
        for k in reads:
            t = self.lastw.get(k)
            if t is not None:
                toks.append(t)
        for k in writes:
            t = self.lastw.get(k)
            if t is not None:
                toks.append(t)
            toks.extend(self.readers.get(k, ()))
        return toks

    def _waits(self, eng, toks):
        waits = []
        for t in toks:
            assert t.val is not None, "dependency on an unresolved (unsignalled) op"
            if t.eng == "pe" and eng == "pe":
                continue
            sid = id(t.sem)
            if t.val > self.waited[eng].get(sid, 0):
                self.waited[eng][sid] = t.val
                waits.append((t.sem, t.val))
        return waits

    def _commit(self, tok, reads, writes):
        for k in reads:
            self.readers.setdefault(k, []).append(tok)
        for k in writes:
            self.lastw[k] = tok
            self.readers[k] = []

    def op(self, eng, fn, reads=(), writes=(), signal=True):
        waits = self._waits(eng, self._deps(reads, writes))
        tok = Tok(eng)
        sem = None
        if signal:
            if self.ecount[eng] >= SEM_ROLL:
                self.nroll[eng] += 1
                self.esem[eng] = self.es.enter_context(self.nc.semaphore(f"c_{eng}_{self.nroll[eng]}"))
                self.ecount[eng] = 0
            self.ecount[eng] += 1
            sem = self.esem[eng]
            tok.sem, tok.val = sem, self.ecount[eng]
            for p in self.pending[eng]:
                p.sem, p.val = tok.sem, tok.val
            self.pending[eng] = []
        else:
            self.pending[eng].append(tok)

        def run(e, waits=waits, fn=fn, sem=sem):
            for s, v in waits:
                e.wait_ge(s, v)
            ins = fn(e)
            if sem is not None:
                ins.then_inc(sem, 1)

        self.prog[eng].append(run)
        self.ninstr[eng] += 1
        self._commit(tok, reads, writes)
        return tok

    def dma(self, q, out_ap, in_ap, reads=(), writes=(), slow=False):
        waits = self._waits(q, self._deps(reads, writes))
        i = self.drr[q]
        self.drr[q] = (i + 1) % len(self.dsem[q])
        sem = self.dsem[q][i]
        c = self.dcount[q][i]
        if c > self.waited[q].get(id(sem), 0):
            self.waited[q][id(sem)] = c
            waits.append((sem, c))
        self.dcount[q][i] = c + 16
        tok = Tok("dma")
        tok.sem, tok.val = sem, c + 16

        def run(e, waits=waits, sem=sem, out_ap=out_ap, in_ap=in_ap, slow=slow):
            for s, v in waits:
                e.wait_ge(s, v)
            if slow:
                e.dma_start(out=out_ap, in_=in_ap, allow_slow_non_contiguous=True).then_inc(sem, 16)
            else:
                e.dma_start(out=out_ap, in_=in_ap).then_inc(sem, 16)

        self.prog[q].append(run)
        self.ninstr[q] += 1
        self._commit(tok, reads, writes)
        return tok

    def finish(self):
        finals = [(self.dsem["sp"][i], self.dcount["sp"][i]) for i in range(12) if self.dcount["sp"][i] > 0]

        def run(e, finals=finals):
            for s, v in finals:
                e.wait_ge(s, v)

        self.prog["sp"].append(run)


class DrySched:
    def __init__(self):
        self.lastw = {}
        self.prog = {e: [] for e in ENGS}
        self.ninstr = {e: 0 for e in ENGS}

    def op(self, eng, fn, reads=(), writes=(), signal=True):
        t = Tok(eng)
        t.val = 1
        return t

    def dma(self, q, out_ap, in_ap, reads=(), writes=(), slow=False):
        t = Tok("dma")
        t.val = 1
        return t

    def finish(self):
        pass


AHEAD = 4
assert AHEAD + 3 <= NSLOT

class Grp:
    def __init__(self, is_ctx, g):
        self.is_ctx = is_ctx
        self.g = g
        self.n = CTX if is_ctx else G
        self.nb = self.n // 128
        self.slot = 1 if is_ctx else g % 2
        self.xs = 2 if is_ctx else g % 3
        self.t0 = 0 if is_ctx else g * G
        self.si = 1 if is_ctx else 0
        self.first = is_ctx or g == 0
        self.last = is_ctx or g == NG - 1
        if is_ctx:
            self.kslots = [0, 1]
        else:
            self.kslots = [2 + ((4 * g + i) % RING) for i in range(4)]
        self.name = "c" if is_ctx else str(g)


def mslot(b):
    return 2 + (b % RING)


class Builder:
    def __init__(self, n_layers=DEPTH, dbg=False, plan=None):
        self.n_layers = n_layers
        self.dbg = dbg
        self.dry = plan is None
        self.plan = [] if plan is None else plan
        self.pidx = 0
        self.issued = 0
        self.pslots = {}
        self.nc = bass.Bass("TRN2", target_bir_lowering=False)
        self.es = ExitStack()

    def sb(self, name, shape, dt):
        return self.es.enter_context(self.nc.sbuf_tensor(name, shape, dt))

    def dram_in(self, name, shape, dt=F32):
        return self.nc.dram_tensor(name, shape, dt, kind="ExternalInput").ap()

    def build(self):
        nc, es = self.nc, self.es
        L = self.n_layers
        self.xT = self.dram_in("xT", [D, SEQ])
        self.cxT = self.dram_in("cxT", [D, CTX])
        self.cond = self.dram_in("cond", [128, 16])
        self.params_d = self.dram_in("params", [128, DEPTH * NPL])
        self.cmat_d = self.dram_in("cmat", [128, 768])
        self.cos_d = self.dram_in("cosT", [128, SEQ])
        self.sin_d = self.dram_in("sinT", [128, SEQ])
        self.poolc_d = self.dram_in("poolc", [128, 64])
        self.wst = [self.dram_in(f"wst{l}", [WSTREAM_LEN]) for l in range(L)]
        self.wmod = [self.dram_in(f"wmod{l}", [24, 128, 2048]) for l in range(L)]
        self.outT = nc.dram_tensor("outT", [D, SEQ], F32, kind="ExternalOutput").ap()
        self.S = [nc.dram_tensor(f"S{i}", [D, SEQ], F32, kind="ExternalOutput" if self.dbg else "Internal").ap() for i in range(2)]
        self.C = [nc.dram_tensor(f"C{i}", [D, CTX], F32, kind="ExternalOutput" if self.dbg else "Internal").ap() for i in range(2)]

        self.sc = DrySched() if self.dry else Sched(nc, es)
        sb = self.sb
        self.wslot = [sb(f"wslot{i}", [128, SLOT_E], BF16) for i in range(NSLOT)]
        self.wrr = 0
        self.xb = [sb(f"xb{i}", [128, 8, G], F32) for i in range(3)]
        self.hb = [sb(f"hb{i}", [128, 8, G], BF16) for i in range(2)]
        self.ntmp = [sb(f"ntmp{i}", [128, G], F32) for i in range(2)]
        self.rln = sb("rln", [128, G], F32)
        self.rstd = sb("rstd", [128, G], F32)
        self.kT = [sb(f"kT{i}", [128, (2 + RING) * 128], BF16) for i in range(2)]
        self.V = sb("V", [128, 2 + RING, 2, 128], BF16)
        self.qb = sb("qb", [128, 4, G], BF16)
        self.big = sb("big", [128, 24 * G], BF16)
        self.gates = self.big[:, 0:16 * G].rearrange("p (c t) -> p c t", t=G)
        self.yb = self.big[:, 16 * G:24 * G].rearrange("p (c t) -> p c t", t=G)
        self.actT = self.big[:, 0:NFC * G].rearrange("p (c t) -> p c t", t=G)
        self.pT = sb("pT", [128, 4, G + 16], F32)
        self.attn = sb("attn", [128, 4, G], BF16)
        self.dT = sb("dT", [128, 4, G], BF16)
        self.po = sb("po", [128, 4, G], BF16)
        self.f4 = [sb(f"f4_{i}", [128, G + 16], F32) for i in range(4)]
        self.PT = [sb(f"PT{i}", [128, G], BF16) for i in range(3)]
        self.ptr = 0
        self.qsq2 = [sb(f"qsq{i}", [128, G], BF16) for i in range(2)]
        self.qln2 = [sb("qln0", [128, G], F32)] * 2
        self.qrs2 = [sb(f"qrs{i}", [128, G], F32) for i in range(2)]
        self.qn2 = [sb(f"qn{i}", [128, G], BF16) for i in range(2)]
        self.qkpar = 0
        self.deferred = []
        self.cosb = sb("cosb", [128, G], F32)
        self.sinb = sb("sinb", [128, G], F32)
        self.t1b = [sb("t1_0", [128, G], F32)] * 2
        self.t2b = [sb("t2_0", [128, G], F32)] * 2
        self.t1, self.t2 = self.t1b[0], self.t2b[0]
        self.lnden = sb("lnden", [128, G], F32)
        self.rden = sb("rden", [128, G], F32)
        self.sil2 = [sb(f"sil{i}", [128, G], F32) for i in range(2)]
        self.corr = sb("corr", [128, 2 * NFC, 2], F32)
        self.saved = sb("saved", [128, 2 * NFC, 2], F32)
        self.xl = [sb(f"xl{i}", [128, 8], F32) for i in range(2)]
        self.cmat = sb("cmat_s", [128, 768], BF16)
        self.poolc = sb("poolc_s", [128, 64], F32)
        self.par = sb("par_s", [128, DEPTH * NPL], F32)
        self.condb = sb("condb", [128, 16], F32)
        self.scond = sb("scond", [128, 16], BF16)
        self.modL = [sb(f"mod{i}", [128, 2, 48], F32) for i in range(2)]
        self.gs1L = [sb(f"gs1_{i}", [128, 2, 8], F32) for i in range(2)]
        self.gs2L = [sb(f"gs2_{i}", [128, 2, 8], F32) for i in range(2)]
        self.esinkL = [sb(f"esink{i}", [128, 8], F32) for i in range(2)]
        self.tail_a = sb("tail_a", [128, 2 * NFC], F32)
        self.tail_s = sb("tail_s", [128, NFC], F32)
        self.tail_act = sb("tail_act", [128, NFC], BF16)
        self.epsb = sb("epsb", [128, 1], F32)
        self.sbuf_left = nc.sbuf_bytes_remaining
        self.ps = [es.enter_context(nc.psum_tensor(f"ps{i}", [128, 512], F32)) for i in range(8)]
        self.pools = {"mm": [0, 1, 2, 3], "st": [4, 5, 0], "o": [6, 1], "n": [7], "aux": [4, 5, 6, 7], "qk": [4, 5], "pp": [2, 3]}
        self.prr = {k: 0 for k in self.pools}

        sc = self.sc
        sc.dma("pool", self.cmat[:, :], self.cmat_d[:, :], writes=["cmat"])
        sc.dma("sp", self.poolc[:, :], self.poolc_d[:, :], writes=["poolc"])
        sc.dma("sp", self.par[:, :], self.params_d[:, :], writes=["par"])
        sc.dma("sp", self.condb[:, :], self.cond[:, :], writes=["condb"])
        sc.op("dve", lambda e: e.memset(self.kT[0][:, :], 0.0), writes=[("kT", s) for s in range(2 + RING)])
        sc.op("dve", lambda e: e.memset(self.kT[1][:, :], 0.0), writes=[("kT", s) for s in range(2 + RING)])
        sc.op("dve", lambda e: e.memset(self.V[:, :, 0, 64:128], 1.0), writes=[("V", s) for s in range(2 + RING)])
        sc.op("dve", lambda e: e.memset(self.V[:, :, 1, 0:64], 1.0), writes=[("V", s) for s in range(2 + RING)])
        sc.op("act", lambda e: e.activation(out=self.scond[:, :], in_=self.condb[:, :], func=AF.Silu),
              reads=["condb"], writes=["scond"])
        sc.op("dve", lambda e: e.memset(self.epsb[:, :], EPS), writes=["epsb"])

        self.ones_mean = self.cmat[:, 0:128]
        self.bd_mean = self.cmat[:, 128:256]
        self.perm = self.cmat[:, 256:384]
        self.mask_next = self.cmat[:, 384:512]
        self.mask_prev = self.cmat[:, 512:640]
        self.ident = self.cmat[:, 640:768]

        self.l = 0
        for k in range(8):
            self.adaln_part(0, k)
        self.adaln_finish(0)
        for l in range(L):
            self.set_layer(l)
            last = (l == DEPTH - 1)
            self.src_x = self.xT if l == 0 else self.S[(l - 1) % 2]
            self.src_c = self.cxT if l == 0 else self.C[(l - 1) % 2]
            self.dst_x = self.outT if l == L - 1 else self.S[l % 2]
            self.dst_c = self.C[l % 2]
            self.xkey_src = ("X", "in" if l == 0 else (l - 1) % 2)
            self.xkey_dst = ("X", "out" if l == L - 1 else l % 2)
            gc = Grp(True, 0)
            grps = [Grp(False, g) for g in range(NG)]
            self.front_load(gc)
            self.front_load(grps[0])
            self.front_sq(gc)
            self.front_a(gc)
            self.front_b(gc)
            if not last:
                self.back_proj_a(gc, 0)
                self.back_proj_a(gc, 1)
                self.back_rest(gc, None)
                self.ffn(gc)
            self.front_load(grps[1])
            self.front_sq(grps[0])
            self.front_a(grps[0])
            self.front_b(grps[0])
            for g in range(NG):
                Gr = grps[g]
                nxt = grps[g + 1] if g + 1 < NG else None
                if g + 2 < NG:
                    self.front_load(grps[g + 2])
                if nxt is not None:
                    self.front_sq(nxt)
                self.back_proj_a(Gr, 0)
                if nxt is not None:
                    self.front_a(nxt)
                self.back_proj_a(Gr, 1)
                if nxt is not None:
                    self.front_b(nxt)
                self.back_rest(Gr, nxt)
                if l + 1 < L:
                    self.adaln_part(l + 1, g)
                self.ffn(Gr)
            if l + 1 < L:
                self.adaln_finish(l + 1)
        sc.finish()

        prog = sc.prog
        if self.dry:
            es.close()
            return None
        with nc.Block() as block:
            @block.tensor
            def _(e):
                for f in prog["pe"]:
                    f(e)

            @block.scalar
            def _(e):
                for f in prog["act"]:
                    f(e)

            @block.vector
            def _(e):
                for f in prog["dve"]:
                    f(e)

            @block.gpsimd
            def _(e):
                for f in prog["pool"]:
                    f(e)

            @block.sync
            def _(e):
                for f in prog["sp"]:
                    f(e)
        es.close()
        return nc

    def psum(self, pool):
        banks = self.pools[pool]
        i = self.prr[pool]
        self.prr[pool] = (i + 1) % len(banks)
        b = banks[i]
        return self.ps[b], ("ps", b)

    def _issue_upto(self, k):
        k = min(k, len(self.plan) - 1)
        while self.issued <= k:
            idx = self.issued
            kind, l, nm = self.plan[idx]
            i = idx % NSLOT
            slot = self.wslot[i]
            key = ("w", i)
            if kind == "w":
                o, e = POFF[nm]
                src = self.wst[l][o:o + 128 * e].rearrange("(p e) -> p e", e=e)
                self.sc.dma("pool", slot[:, 0:e], src, writes=[key])
            else:
                self.sc.dma("pool", slot[:, :], self.wmod[l][nm, :, :], writes=[key])
            self.issued += 1

    def wget(self, name, kind="w", layer=None):
        ent = (kind, self.l if layer is None else layer, name)
        if self.dry:
            self.plan.append(ent)
            return self.wslot[0], ("w", 0)
        idx = self.pidx
        assert self.plan[idx] == ent, (self.plan[idx], ent)
        self.pidx += 1
        self._issue_upto(idx + AHEAD)
        i = idx % NSLOT
        return self.wslot[i], ("w", i)

    def pcol(self, off, n=1):
        o = self.l * NPL + off
        return self.par[:, o:o + n]

    def mm_group(self, out_ap, pskey, terms, extra_reads=()):
        sc = self.sc
        nt = len(terms)
        tok = None
        for i, (lh, rh, rk) in enumerate(terms):
            st, sp_ = (i == 0), (i == nt - 1)
            tok = sc.op("pe", lambda e, lh=lh, rh=rh, st=st, sp_=sp_: e.matmul(out_ap, lhsT=lh, rhs=rh, start=st, stop=sp_),
                        reads=list(rk) + (list(extra_reads) if i == 0 else []),
                        writes=[pskey] if i == 0 else [], signal=sp_)
        self.sc.lastw[pskey] = tok
        return tok

    def pcol_l(self, l, off, n=1):
        o = l * NPL + off
        return self.par[:, o:o + n]

    def adaln_part(self, l, k):
        sc = self.sc
        p2 = l % 2
        mod, kmod = self.modL[p2], ("mod", p2)
        ps, pk = self.psum("mm")
        tok = None
        rhs_all = self.scond[:, :].rearrange("p (s k) -> p k s", s=2)
        first = True
        for i2 in range(3 * k, 3 * k + 3):
            slot, key = self.wget(i2, kind="m", layer=l)
            wv = slot[:, :].rearrange("p (k f) -> p k f", f=256)
            for cc in range(2):
                jj = 2 * (i2 - 3 * k) + cc
                for kc in range(8):
                    st, sp_ = (kc == 0), (kc == 7)
                    tok = sc.op("pe", lambda e, wv=wv, kc=kc, jj=jj, cc=cc, st=st, sp_=sp_: e.matmul(
                        ps[:, 2 * jj:2 * jj + 2], lhsT=wv[:, kc, cc * 128:(cc + 1) * 128], rhs=rhs_all[:, kc, :], start=st, stop=sp_),
                        reads=[key, "scond"], writes=[pk] if first else [], signal=sp_)
                    first = False
        sc.lastw[pk] = tok
        bm = self.pcol_l(l, 6 * k, 6)
        for s_ in range(2):
            sc.op("dve", lambda e, s_=s_: e.tensor_tensor(out=mod[:, s_, 6 * k:6 * k + 6], in0=ps[:, s_:12:2], in1=bm, op=ALU.add),
                  reads=["par"], writes=[pk, kmod])

    def adaln_finish(self, l):
        sc = self.sc
        p2 = l % 2
        mod, kmod = self.modL[p2], ("mod", p2)
        for (gsb, sco, ngo, nm) in ((self.gs1L[p2], 8, 48, ("gs1", p2)), (self.gs2L[p2], 32, 56, ("gs2", p2))):
            ng = self.pcol_l(l, ngo, 8)
            for s_ in range(2):
                sc.op("dve", lambda e, s_=s_, gsb=gsb, sco=sco, ng=ng: e.scalar_tensor_tensor(
                    out=gsb[:, s_, :], in0=mod[:, s_, sco:sco + 8], scalar=1.0, in1=ng, op0=ALU.add, op1=ALU.mult),
                    reads=[kmod, "par"], writes=[nm])
        esink = self.esinkL[p2]
        sk = self.pcol_l(l, 246, 8)
        sc.op("act", lambda e: e.activation(out=esink[:, :], in_=sk, func=AF.Exp), reads=["par"], writes=[("esink", p2)])

    def set_layer(self, l):
        p2 = l % 2
        self.l = l
        self.mod, self.gs1, self.gs2, self.esink = self.modL[p2], self.gs1L[p2], self.gs2L[p2], self.esinkL[p2]
        self.kmod, self.kgs1, self.kgs2, self.kesink = ("mod", p2), ("gs1", p2), ("gs2", p2), ("esink", p2)

    def norm_mod(self, Gr, gsb, gsname, sh_off, stats_done=False, interleave=False, sq_done=False):
        sc = self.sc
        s, n, si = Gr.slot, Gr.n, Gr.si
        xb, hb = self.xb[Gr.xs], self.hb[s]
        mod, kmod = self.mod, self.kmod
        xks = [("xb", Gr.xs, kc) for kc in range(8)]
        hks = [("hb", s, kc) for kc in range(8)]
        if not stats_done:
            if not sq_done:
                sc.op("act", lambda e: e.activation(out=hb[:, :, 0:n], in_=xb[:, :, 0:n], func=AF.Square), reads=xks, writes=hks)
            ps, pk = self.psum("n")
            self.mm_group(ps[:, 0:n], pk, [(self.ones_mean, hb[:, kc, 0:n], [hks[kc], "cmat"]) for kc in range(8)])
        else:
            ps, pk = self.nstat
        sc.op("act", lambda e: e.activation(out=self.rln[:, 0:n], in_=ps[:, 0:n], func=AF.Ln, bias=self.epsb[:, 0:1], scale=1.0),
              reads=["epsb"], writes=[pk, "rln"])
        sc.op("act", lambda e: e.activation(out=self.rstd[:, 0:n], in_=self.rln[:, 0:n], func=AF.Exp, scale=-0.5),
              reads=["rln"], writes=["rstd"])
        if interleave and not stats_done:
            self.tick()
        def chunk(kc):
            nt = self.ntmp[kc % 2]
            nk = ("ntmp", kc % 2)
            sc.op("dve", lambda e: e.tensor_tensor(out=nt[:, 0:n], in0=xb[:, kc, 0:n], in1=self.rstd[:, 0:n], op=ALU.mult),
                  reads=[xks[kc], "rstd"], writes=[nk])
            sc.op("act", lambda e: e.activation(out=hb[:, kc, 0:n], in_=nt[:, 0:n], func=AF.Identity,
                                                bias=mod[:, si, sh_off + kc:sh_off + kc + 1], scale=gsb[:, si, kc:kc + 1]),
                  reads=[nk, kmod, gsname], writes=[hks[kc]])

        for kc in range(8):
            if interleave:
                self.defer(kc + 1, lambda kc=kc: chunk(kc))
            else:
                chunk(kc)

    def defer(self, delay, fn):
        self.deferred.append([delay, fn])

    def tick(self):
        due = [d for d in self.deferred if d[0] <= 1]
        self.deferred = [[d[0] - 1, d[1]] for d in self.deferred if d[0] > 1]
        for d in due:
            d[1]()

    def flush(self):
        while self.deferred:
            self.tick()

    def qk_post(self, ps, pk, n, gain_off, rope, dst_ap, dst_keys):
        sc = self.sc
        par = self.qkpar
        self.qkpar ^= 1
        qsq, qln, qrs, qn, t1, t2 = self.qsq2[par], self.qln2[par], self.qrs2[par], self.qn2[par], self.t1b[par], self.t2b[par]
        kq, kr, kn = [(nm, par) for nm in ("qsq", "qrs", "qn")]
        kl, k1, k2 = ("qln", 0), ("t1", 0), ("t2", 0)
        gain = self.pcol(gain_off, 1)
        if not isinstance(dst_ap, list):
            dst_ap = [(0, 128, dst_ap)]
        sc.op("act", lambda e: e.activation(out=qsq[:, 0:n], in_=ps[:, 0:n], func=AF.Square), reads=[], writes=[pk, kq])

        def step1():
            pn, pnk = self.psum("n")
            self.mm_group(pn[:, 0:n], pnk, [(self.bd_mean, qsq[:, 0:n], [kq, "cmat"])])
            sc.op("act", lambda e: e.activation(out=qln[:, 0:n], in_=pn[:, 0:n], func=AF.Ln, bias=self.epsb[:, 0:1], scale=1.0),
                  reads=["epsb"], writes=[pnk, kl])
            sc.op("act", lambda e: e.activation(out=qrs[:, 0:n], in_=qln[:, 0:n], func=AF.Exp, scale=-0.5), reads=[kl], writes=[kr])
            if not rope:
                for (p0, p1, dap) in dst_ap:
                    sc.op("dve", lambda e, p0=p0, p1=p1, dap=dap: e.scalar_tensor_tensor(out=dap, in0=ps[p0:p1, 0:n], scalar=gain[p0:p1, :], in1=qrs[p0:p1, 0:n],
                                                                                   op0=ALU.mult, op1=ALU.mult),
                          reads=[kr, "par"], writes=[pk] + dst_keys)
            else:
                sc.op("dve", lambda e: e.scalar_tensor_tensor(out=qn[:, 0:n], in0=ps[:, 0:n], scalar=gain, in1=qrs[:, 0:n], op0=ALU.mult, op1=ALU.mult),
                      reads=[kr, "par"], writes=[pk, kn])

        def step2():
            pr, prk = self.psum("qk")
            self.mm_group(pr[:, 0:n], prk, [(self.perm, qn[:, 0:n], [kn, "cmat"])])
            sc.op("pool", lambda e: e.tensor_tensor(out=t1[:, 0:n], in0=qn[:, 0:n], in1=self.cosb[:, 0:n], op=ALU.mult), reads=[kn, "cosb"], writes=[k1])
            sc.op("dve", lambda e: e.tensor_tensor(out=t2[:, 0:n], in0=pr[:, 0:n], in1=self.sinb[:, 0:n], op=ALU.mult), reads=["sinb"], writes=[prk, k2])
            for (p0, p1, dap) in dst_ap:
                sc.op("pool", lambda e, p0=p0, p1=p1, dap=dap: e.tensor_tensor(out=dap, in0=t1[p0:p1, 0:n], in1=t2[p0:p1, 0:n], op=ALU.add),
                      reads=[k1, k2], writes=dst_keys)

        self.defer(1, step1)
        if rope:
            self.defer(3, step2)

    def front_load(self, Gr):
        sc = self.sc
        n, t0 = Gr.n, Gr.t0
        src = (self.src_c if Gr.is_ctx else self.src_x).rearrange("(k p) t -> p k t", p=128)
        sc.dma("sp", self.xb[Gr.xs][:, :, 0:n], src[:, :, t0:t0 + n],
               reads=[(self.xkey_src, Gr.name, pt_) for pt_ in ("m0", "m1", "m2", "m3", "m4", "m5", "e", "t")], writes=[("xb", Gr.xs, kc) for kc in range(8)])

    def front_sq(self, Gr):
        s, n = Gr.slot, Gr.n
        xb, hb = self.xb[Gr.xs], self.hb[s]
        self.sc.op("act", lambda e: e.activation(out=hb[:, :, 0:n], in_=xb[:, :, 0:n], func=AF.Square),
                   reads=[("xb", Gr.xs, kc) for kc in range(8)], writes=[("hb", s, kc) for kc in range(8)])

    def front_a(self, Gr):
        self.norm_mod(Gr, self.gs1, self.kgs1, 0, interleave=True, sq_done=True)

    def front_b(self, Gr):
        self.flush()
        sc = self.sc
        s, n, t0 = Gr.slot, Gr.n, Gr.t0
        if not Gr.is_ctx:
            sc.dma("sp", self.cosb[:, :], self.cos_d[:, t0:t0 + n], writes=["cosb"])
            sc.dma("sp", self.sinb[:, :], self.sin_d[:, t0:t0 + n], writes=["sinb"])
        hb = self.hb[s]
        hk = lambda kc: ("hb", s, kc)
        w, wk = self.wget("in0")
        wv = w[:, :].rearrange("p (k f) -> p k f", f=256)
        ps, pk = self.psum("mm")
        self.mm_group(ps[:, 0:n], pk, [(wv[:, kc, 0:128], hb[:, kc, 0:n], [wk, hk(kc)]) for kc in range(8)])
        s0 = Gr.kslots[0]
        kdst = [(0, 64, self.kT[0][0:64, s0 * 128:s0 * 128 + n]), (64, 128, self.kT[1][64:128, s0 * 128:s0 * 128 + n])]
        self.qk_post(ps, pk, n, 245, not Gr.is_ctx, kdst, [("kT", sl) for sl in Gr.kslots])
        ps2, pk2 = self.psum("mm")
        tok = None
        for b in range(Gr.nb):
            for kc in range(8):
                st, sp_ = (kc == 0), (kc == 7)
                tok = sc.op("pe", lambda e, b=b, kc=kc, st=st, sp_=sp_: e.matmul(
                    ps2[:, b * 128:(b + 1) * 128], lhsT=hb[:, kc, b * 128:(b + 1) * 128], rhs=wv[:, kc, 128:256], start=st, stop=sp_),
                    reads=[wk, hk(kc)], writes=[pk2] if (b == 0 and kc == 0) else [], signal=sp_)
            self.tick()
        self.flush()
        sc.lastw[pk2] = tok
        psv = ps2[:, 0:n].rearrange("p (b f) -> p b f", f=128)
        vk = [("V", sl) for sl in Gr.kslots]
        sc.op("act", lambda e: e.activation(out=self.V[:, s0:s0 + Gr.nb, 0, 0:64], in_=psv[:, :, 0:64], func=AF.Identity),
              writes=[pk2] + vk)
        sc.op("act", lambda e: e.activation(out=self.V[:, s0:s0 + Gr.nb, 1, 64:128], in_=psv[:, :, 64:128], func=AF.Identity),
              writes=[pk2] + vk)

    def back_proj_a(self, Gr, part):
        sc = self.sc
        s, n = Gr.slot, Gr.n
        hb = self.hb[s]
        hk = lambda kc: ("hb", s, kc)
        for pi in (range(2) if part == 0 else []):
            w, wk = self.wget(f"in{1 + pi}")
            wv = w[:, :].rearrange("p (k f) -> p k f", f=256)
            for cc in range(2):
                cq = pi * 2 + cc
                ps, pk = self.psum("mm")
                self.mm_group(ps[:, 0:n], pk, [(wv[:, kc, cc * 128:(cc + 1) * 128], hb[:, kc, 0:n], [wk, hk(kc)]) for kc in range(8)])
                self.tick()
                self.qk_post(ps, pk, n, 244, not Gr.is_ctx, self.qb[:, cq, 0:n], [("qb", cq)])
        for pi in ([] if part == 0 else range(8)):
            w, wk = self.wget(f"in{5 + pi}")
            wv = w[:, :].rearrange("p (k f) -> p k f", f=256)
            for cc in range(2):
                j = pi * 2 + cc
                ps, pk = self.psum("mm")
                self.mm_group(ps[:, 0:n], pk, [(wv[:, kc, cc * 128:(cc + 1) * 128], hb[:, kc, 0:n], [wk, hk(kc)]) for kc in range(8)])
                sc.op("act", lambda e, ps=ps, j=j: e.activation(out=self.gates[:, j, 0:n], in_=ps[:, 0:n], func=AF.Sigmoid),
                      writes=[pk, ("big", j)])
                self.tick()

    def back_proj_b_steps(self, Gr, nxt):
        sc = self.sc
        s, n = Gr.slot, Gr.n
        hb = self.hb[s]
        hk = lambda kc: ("hb", s, kc)
        pT = self.pT
        st8 = {}

        def init():
            if Gr.first:
                sc.op("dve", lambda e: e.memset(pT[:, :, 0:8], 0.0), writes=["pT"])
            else:
                sc.op("dve", lambda e: e.tensor_copy(out=pT[:, :, 0:8], in_=pT[:, :, n:n + 8]), reads=[], writes=["pT"])
            if nxt is None:
                sc.op("dve", lambda e: e.memset(pT[:, :, 8 + n:16 + n], 0.0), writes=["pT"])
            else:
                st8["psh"] = self.psum("n")

        def chunk(c):
            pi, cc = c // 2, c % 2
            if cc == 0:
                st8["w"] = self.wget(f"in{3 + pi}")
            w, wk = st8["w"]
            wv = w[:, :].rearrange("p (k f) -> p k f", f=256)
            ps, pk = self.psum("pp")
            self.mm_group(ps[:, 0:n], pk, [(wv[:, kc, cc * 128:(cc + 1) * 128], hb[:, kc, 0:n], [wk, hk(kc)]) for kc in range(8)])
            sc.op("dve", lambda e: e.tensor_copy(out=pT[:, c, 8:8 + n], in_=ps[:, 0:n]), writes=[pk, "pT"])
            if nxt is not None:
                psh, pkh = st8["psh"]
                hbn = self.hb[nxt.slot]
                tok = None
                for kc in range(8):
                    st, sp_ = (kc == 0), (kc == 7)
                    tok = sc.op("pe", lambda e, kc=kc, st=st, sp_=sp_: e.matmul(
                        psh[:, c * 8:(c + 1) * 8], lhsT=wv[:, kc, cc * 128:(cc + 1) * 128], rhs=hbn[:, kc, 0:8], start=st, stop=sp_),
                        reads=[wk, ("hb", nxt.slot, kc)], writes=[pkh] if (c == 0 and kc == 0) else [], signal=sp_)
                sc.lastw[pkh] = tok

        def fin():
            if nxt is not None:
                psh, pkh = st8["psh"]
                sc.op("dve", lambda e: e.tensor_copy(out=pT[:, :, 8 + n:16 + n], in_=psh[:, 0:32].rearrange("p (c f) -> p c f", f=8)),
                      writes=[pkh, "pT"])

        return [init] + [(lambda c=c: chunk(c)) for c in range(4)] + [fin]

    def back_rest(self, Gr, nxt):
        self.flush()
        steps = self.back_proj_b_steps(Gr, nxt) + [lambda: self.poolmix(Gr)]
        spacing = 1 if Gr.is_ctx else 4
        for i, st in enumerate(steps):
            self.defer(1 + i * spacing, st)
        self.attention(Gr)
        self.flush()
        self.poolmix_pe(Gr)
        self.merge_out(Gr)

    def attention(self, Gr):
        sc = self.sc
        n, g = Gr.n, Gr.g
        tiles = [(0, 0, n, []), (1, 0, n, [])]
        if not Gr.is_ctx:
            for j in range(4 * g - 1, 4 * g + 5):
                if j < 0 or j >= SEQ // 128:
                    continue
                lo = max(j - 1, 4 * g)
                hi = min(j + 1, 4 * g + 3)
                masks = []
                for i in range(lo, hi + 1):
                    if i == j - 1:
                        masks.append(((i - lo) * 128, self.mask_next))
                    elif i == j + 1:
                        masks.append(((i - lo) * 128, self.mask_prev))
                tiles.append((mslot(j), (lo - 4 * g) * 128, (hi + 1 - 4 * g) * 128, masks))
        units = [(cq, half) for cq in range(4) for half in range(2)]
        esink, kesink = self.esink, self.kesink
        nt = len(tiles)
        seq = [(u, ti) for u in range(len(units)) for ti in range(nt)]
        psS_of = {}

        def emit_S(idx):
            u, ti = seq[idx]
            cq, half = units[u]
            slot, c0, c1, masks = tiles[ti]
            N = c1 - c0
            psS, pkS = self.psum("st")
            nmm = 1 + len(masks)
            tok = sc.op("pe", lambda e: e.matmul(psS[:, 0:N], lhsT=self.kT[half][:, slot * 128:(slot + 1) * 128], rhs=self.qb[:, cq, c0:c1],
                                                 start=True, stop=(nmm == 1)),
                        reads=[("kT", slot), ("qb", cq)], writes=[pkS], signal=(nmm == 1))
            for mi, (co, mk) in enumerate(masks):
                lastm = (mi == len(masks) - 1)
                tok = sc.op("pe", lambda e, co=co, mk=mk, lastm=lastm: e.matmul(psS[:, co:co + 128], lhsT=self.ident, rhs=mk, start=False, stop=lastm),
                            reads=["cmat"], writes=[], signal=lastm)
            sc.lastw[pkS] = tok
            psS_of[idx] = (psS, pkS)

        def emit_norm(u, psO, pkO):
            cq, half = units[u]
            hbp = half * 64
            ob = 64 - hbp
            h = cq + 4 * half
            sc.op("act", lambda e: e.activation(out=self.lnden[ob:ob + 64, 0:n], in_=psO[ob:ob + 64, 0:n], func=AF.Ln,
                                                bias=esink[ob:ob + 64, h:h + 1], scale=1.0),
                  reads=[kesink], writes=[pkO, "lnden"])
            sc.op("act", lambda e: e.activation(out=self.rden[hbp:hbp + 64, 0:n], in_=self.lnden[ob:ob + 64, 0:n], func=AF.Exp, scale=-1.0),
                  reads=["lnden"], writes=["rden"])
            sc.op("dve", lambda e: e.tensor_tensor(out=self.attn[hbp:hbp + 64, cq, 0:n], in0=psO[hbp:hbp + 64, 0:n],
                                                   in1=self.rden[hbp:hbp + 64, 0:n], op=ALU.mult),
                  reads=["rden"], writes=[pkO, ("attn", cq)])

        pending = None
        cur = None
        emit_S(0)
        emit_S(1)
        for idx in range(len(seq)):
            u, ti = seq[idx]
            cq, half = units[u]
            slot, c0, c1, masks = tiles[ti]
            N = c1 - c0
            if idx + 2 < len(seq):
                emit_S(idx + 2)
            if ti == 0:
                cur = self.psum("o")
            psO, pkO = cur
            psS, pkS = psS_of.pop(idx)
            pt = self.PT[self.ptr]
            ptk = ("PT", self.ptr)
            self.ptr = (self.ptr + 1) % len(self.PT)
            sc.op("act", lambda e, pt=pt, psS=psS, N=N: e.activation(out=pt[:, 0:N], in_=psS[:, 0:N], func=AF.Exp, scale=0.125),
                  writes=[pkS, ptk])
            st, sp_ = (ti == 0), (ti == nt - 1)
            tokO = sc.op("pe", lambda e, psO=psO, slot=slot, half=half, pt=pt, N=N, c0=c0, c1=c1, st=st, sp_=sp_: e.matmul(
                psO[:, c0:c1], lhsT=self.V[:, slot, half, :], rhs=pt[:, 0:N], start=st, stop=sp_),
                reads=[("V", slot), ptk], writes=[pkO] if ti == 0 else [], signal=sp_)
            self.tick()
            if ti == 1 and pending is not None:
                emit_norm(*pending)
                pending = None
            if ti == nt - 1:
                sc.lastw[pkO] = tokO
                pending = (u, psO, pkO)
        emit_norm(*pending)

    def poolmix(self, Gr):
        sc = self.sc
        n = Gr.n
        pT = self.pT
        A, B_, C8, S_ = self.f4
        fk = ["f4_0", "f4_1", "f4_2", "f4_3"]

        def add(out_ap, a, b, reads, writes):
            sc.op("dve", lambda e: e.tensor_tensor(out=out_ap, in0=a, in1=b, op=ALU.add), reads=reads, writes=writes)

        for c, w in enumerate((2, 4, 8, 16)):
            P = pT[:, c, :]
            if c == 0:
                add(S_[:, 0:n], P[:, 7:7 + n], P[:, 8:8 + n], ["pT"], [fk[3]])
            elif c == 1:
                add(A[:, 0:n + 2], P[:, 6:8 + n], P[:, 7:9 + n], ["pT"], [fk[0]])
                add(S_[:, 0:n], A[:, 0:n], A[:, 2:n + 2], [fk[0]], [fk[3]])
            elif c == 2:
                add(A[:, 0:n + 6], P[:, 4:10 + n], P[:, 5:11 + n], ["pT"], [fk[0]])
                add(B_[:, 0:n + 4], A[:, 0:n + 4], A[:, 2:n + 6], [fk[0]], [fk[1]])
                add(S_[:, 0:n], B_[:, 0:n], B_[:, 4:n + 4], [fk[1]], [fk[3]])
            else:
                add(A[:, 0:n + 14], P[:, 0:14 + n], P[:, 1:15 + n], ["pT"], [fk[0]])
                add(B_[:, 0:n + 12], A[:, 0:n + 12], A[:, 2:n + 14], [fk[0]], [fk[1]])
                add(C8[:, 0:n + 8], B_[:, 0:n + 8], B_[:, 4:n + 12], [fk[1]], [fk[2]])
                add(S_[:, 0:n], C8[:, 0:n], C8[:, 8:n + 8], [fk[2]], [fk[3]])
            sc.op("dve", lambda e, c=c, w=w, P=P: e.scalar_tensor_tensor(out=self.dT[:, c, 0:n], in0=S_[:, 0:n], scalar=1.0 / w, in1=P[:, 8:8 + n],
                                                                        op0=ALU.mult, op1=ALU.subtract),
                  reads=[fk[3], "pT"], writes=[("dT", c)])
            if Gr.first:
                sc.op("dve", lambda e, c=c: e.tensor_tensor(out=A[:, 0:8], in0=S_[:, 0:8], in1=self.poolc[:, c * 8:(c + 1) * 8], op=ALU.mult),
                      reads=[fk[3], "poolc"], writes=[fk[0]])
                sc.op("dve", lambda e, c=c, P=P: e.tensor_tensor(out=self.dT[:, c, 0:8], in0=A[:, 0:8], in1=P[:, 8:16], op=ALU.subtract),
                      reads=[fk[0], "pT"], writes=[("dT", c)])
            if Gr.last:
                sc.op("dve", lambda e, c=c: e.tensor_tensor(out=A[:, 0:8], in0=S_[:, n - 8:n], in1=self.poolc[:, 32 + c * 8:32 + (c + 1) * 8], op=ALU.mult),
                      reads=[fk[3], "poolc"], writes=[fk[0]])
                sc.op("dve", lambda e, c=c, P=P: e.tensor_tensor(out=self.dT[:, c, n - 8:n], in0=A[:, 0:8], in1=P[:, n:n + 8], op=ALU.subtract),
                      reads=[fk[0], "pT"], writes=[("dT", c)])

    def poolmix_pe(self, Gr):
        sc = self.sc
        n = Gr.n
        w, wk = self.wget("pool")
        wv = w[:, 0:512].rearrange("p (g d) -> p g d", d=128)
        for c in range(4):
            ps, pk = self.psum("mm")
            self.mm_group(ps[:, 0:n], pk, [(wv[:, c, :], self.dT[:, c, 0:n], [wk, ("dT", c)])])
            sc.op("act", lambda e, ps=ps, c=c, psc=self.pcol(64 + c, 1): e.activation(out=self.po[:, c, 0:n], in_=ps[:, 0:n], func=AF.Identity, scale=psc),
                  reads=["par"], writes=[pk, ("po", c)])

    def merge_out(self, Gr):
        sc = self.sc
        s, n = Gr.slot, Gr.n
        si = Gr.si
        xb = self.xb[Gr.xs]
        xks = [("xb", Gr.xs, kc) for kc in range(8)]
        hb = self.hb[s]
        psn, pkn = self.psum("n")

        def stat_mm(c):
            st, sp_ = (c == 0), (c == 7)
            return sc.op("pe", lambda e: e.matmul(psn[:, 0:n], lhsT=self.ones_mean, rhs=hb[:, c, 0:n], start=st, stop=sp_),
                         reads=[("hb", s, c), "cmat"], writes=[pkn] if c == 0 else [], signal=sp_)

        for hf in range(2):
            wa, wak = self.wget(f"bra{hf}")
            wp, wpk = self.wget(f"brp{hf}")
            wav = wa[:, :].rearrange("p (k f) -> p k f", f=512)
            wpv = wp[:, :].rearrange("p (k f) -> p k f", f=512)
            for cc in range(4):
                c = hf * 4 + cc
                psa, pka = self.psum("mm")
                self.mm_group(psa[:, 0:n], pka, [(wav[:, k, cc * 128:(cc + 1) * 128], self.attn[:, k, 0:n], [wak, ("attn", k)]) for k in range(4)])
                psp, pkp = self.psum("mm")
                self.mm_group(psp[:, 0:n], pkp, [(wpv[:, k, cc * 128:(cc + 1) * 128], self.po[:, k, 0:n], [wpk, ("po", k)]) for k in range(4)])
                m1, m1k = self.f4[(c % 2) * 2], f"f4_{(c % 2) * 2}"
                m2, m2k = self.f4[(c % 2) * 2 + 1], f"f4_{(c % 2) * 2 + 1}"
                sc.op("dve", lambda e, psa=psa, c=c, m1=m1: e.tensor_tensor(out=m1[:, 0:n], in0=psa[:, 0:n], in1=self.gates[:, c, 0:n], op=ALU.mult),
                      reads=[("big", c)], writes=[pka, m1k])
                sc.op("dve", lambda e, psp=psp, c=c, m2=m2: e.tensor_tensor(out=m2[:, 0:n], in0=psp[:, 0:n], in1=self.gates[:, 8 + c, 0:n], op=ALU.mult),
                      reads=[("big", 8 + c)], writes=[pkp, m2k])
                sc.op("pool", lambda e, c=c, m1=m1, m2=m2: e.tensor_tensor(out=self.yb[:, c, 0:n], in0=m1[:, 0:n], in1=m2[:, 0:n], op=ALU.add),
                      reads=[m1k, m2k], writes=[("big", 16 + c)])
        for pi in range(4):
            w, wk = self.wget(f"out{pi}")
            wv = w[:, :].rearrange("p (k f) -> p k f", f=256)
            for cc in range(2):
                c = pi * 2 + cc
                ps, pk = self.psum("mm")
                self.mm_group(ps[:, 0:n], pk, [(wv[:, k, cc * 128:(cc + 1) * 128], self.yb[:, k, 0:n], [wk, ("big", 16 + k)]) for k in range(8)])
                if c >= 1:
                    stat_mm(c - 1)
                g1 = self.mod[:, si, 16 + c:17 + c]
                sc.op("dve", lambda e, ps=ps, c=c, g1=g1: e.scalar_tensor_tensor(out=xb[:, c, 0:n], in0=ps[:, 0:n], scalar=g1,
                                                                                in1=xb[:, c, 0:n], op0=ALU.mult, op1=ALU.add),
                      reads=[self.kmod], writes=[pk, xks[c]])
                sc.op("act", lambda e, c=c: e.activation(out=hb[:, c, 0:n], in_=xb[:, c, 0:n], func=AF.Square), reads=[xks[c]], writes=[("hb", s, c)])
        tokn = stat_mm(7)
        sc.lastw[pkn] = tokn
        self.nstat = (psn, pkn)
        xl = self.xl[Gr.g % 2 if not Gr.is_ctx else 0]
        sc.op("dve", lambda e: e.tensor_copy(out=xl[:, :], in_=xb[:, :, n - 1]), reads=xks, writes=[("xl", Gr.g % 2 if not Gr.is_ctx else 0)])

    def ffn(self, Gr):
        sc = self.sc
        s, n, si = Gr.slot, Gr.n, Gr.si
        xb = self.xb[Gr.xs]
        xks = [("xb", Gr.xs, kc) for kc in range(8)]
        hb = self.hb[s]
        hk = lambda kc: ("hb", s, kc)
        self.norm_mod(Gr, self.gs2, self.kgs2, 24, stats_done=True)
        if Gr.first:
            sc.op("dve", lambda e: e.memset(self.saved[:, :, :], 0.0), writes=["saved"])
        w0T, w1T = self.pcol(68, 44), self.pcol(68 + 44, 44)
        corr = self.corr
        sc.op("dve", lambda e: e.tensor_tensor(out=corr[:, :, 1], in0=self.saved[:, :, 1], in1=w0T, op=ALU.mult), reads=["saved", "par"], writes=["corr"])
        sc.op("dve", lambda e: e.tensor_tensor(out=corr[:, :, 0], in0=self.saved[:, :, 0], in1=w0T, op=ALU.mult), reads=["saved", "par"], writes=["corr"])
        sc.op("dve", lambda e: e.tensor_tensor(out=self.tail_a[:, :], in0=self.saved[:, :, 1], in1=w1T, op=ALU.mult), reads=["saved", "par"], writes=["tail_a"])
        sc.op("dve", lambda e: e.tensor_tensor(out=corr[:, :, 0], in0=corr[:, :, 0], in1=self.tail_a[:, :], op=ALU.add), reads=["tail_a"], writes=["corr"])
        bigkeys = [("big", j) for j in range(24)]
        for i in range(NFC):
            w, wk = self.wget(f"up{i}")
            wv = w[:, :].rearrange("p (k f) -> p k f", f=256)
            accs = []
            for ab in range(2):
                ch = i + ab * NFC
                ps, pk = self.psum("mm")
                self.mm_group(ps[:, 0:n], pk, [(wv[:, kc, ab * 128:(ab + 1) * 128], hb[:, kc, 0:n], [wk, hk(kc)]) for kc in range(8)])
                acc = self.f4[(i % 2) * 2 + ab]
                ak = f"f4_{(i % 2) * 2 + ab}"
                w0, w1, w2, bb = self.pcol(68 + ch), self.pcol(68 + 44 + ch), self.pcol(68 + 88 + ch), self.pcol(200 + ch)
                sv = self.saved[:, ch, :]
                sc.op("act", lambda e, ps=ps, acc=acc, w2=w2, bb=bb: e.activation(out=acc[:, 0:n], in_=ps[:, 0:n], func=AF.Identity, bias=bb, scale=w2),
                      reads=["par"], writes=[pk, ak])
                sc.op("act", lambda e, ps=ps, sv=sv: e.activation(out=sv[:, 0:2], in_=ps[:, n - 2:n], func=AF.Identity), reads=["corr"], writes=[pk, "saved"])
                sc.op("dve", lambda e, ps=ps, acc=acc, w1=w1: e.scalar_tensor_tensor(out=acc[:, 1:n], in0=ps[:, 0:n - 1], scalar=w1, in1=acc[:, 1:n],
                                                                                    op0=ALU.mult, op1=ALU.add),
                      reads=["par"], writes=[pk, ak])
                sc.op("dve", lambda e, ps=ps, acc=acc, w0=w0: e.scalar_tensor_tensor(out=acc[:, 2:n], in0=ps[:, 0:n - 2], scalar=w0, in1=acc[:, 2:n],
                                                                                    op0=ALU.mult, op1=ALU.add),
                      reads=["par"], writes=[pk, ak])
                sc.op("dve", lambda e, acc=acc, ch=ch: e.tensor_tensor(out=acc[:, 0:2], in0=acc[:, 0:2], in1=corr[:, ch, :], op=ALU.add),
                      reads=["corr"], writes=[ak])
                accs.append((acc, ak))
            (aa, aak), (ab_, abk) = accs
            sil, silk = self.sil2[i % 2], ("sil", i % 2)
            sc.op("act", lambda e, aa=aa, sil=sil: e.activation(out=sil[:, 0:n], in_=aa[:, 0:n], func=AF.Silu), reads=[aak], writes=[silk])
            sc.op("pool", lambda e, ab_=ab_, i=i, sil=sil: e.tensor_tensor(out=self.actT[:, i, 0:n], in0=sil[:, 0:n], in1=ab_[:, 0:n], op=ALU.mult),
                  reads=[silk, abk], writes=bigkeys if i == 0 else [("act", i)])
        actkeys = bigkeys + [("act", i) for i in range(1, NFC)]
        c_first = 1 if Gr.first else 0
        xlp = self.xl[(Gr.g - 1) % 2]
        xlpk = ("xl", (Gr.g - 1) % 2)
        hbf = hb[:, :, :].rearrange("p c t -> p (c t)").bitcast(F32).rearrange("p (c t) -> p c t", t=G)
        if Gr.last:
            self.tail_prep()
            pst, pkt = self.psum("mm")
            tokt = None
        attn_f = self.attn[:, :, :].rearrange("p c t -> p (c t)").bitcast(F32).rearrange("p (c t) -> p c t", t=G)
        dT_f = self.dT[:, :, :].rearrange("p c t -> p (c t)").bitcast(F32).rearrange("p (c t) -> p c t", t=G)
        KA = NFC // 2
        for fp in range(4):
            wd = [self.wget(f"dn{fp}_{q}") for q in range(3)]

            def wv_of(kc):
                w, wk = wd[kc // 8]
                return w[:, 0:(8 if kc < 16 else 6) * 256].rearrange("p (k f) -> p k f", f=256), wk

            if Gr.last:
                for cc in range(2):
                    c = fp * 2 + cc
                    for kc in range(NFC):
                        wv, wk = wv_of(kc)
                        st, sp_ = (kc == 0), (kc == NFC - 1)
                        tokt = sc.op("pe", lambda e, c=c, kc=kc, wv=wv, cc=cc, st=st, sp_=sp_: e.matmul(
                            pst[:, 2 * c:2 * c + 1], lhsT=wv[:, kc % 8, cc * 128:(cc + 1) * 128], rhs=self.tail_act[:, kc:kc + 1], start=st, stop=sp_),
                            reads=[wk, "tail_act"], writes=[pkt] if (c == 0 and kc == 0) else [], signal=sp_)
            banks = [self.psum("aux") for _ in range(2)]
            toks = [None, None]
            for (k0, k1) in ((0, KA), (KA, NFC)):
                for cc in range(2):
                    ps, pk = banks[cc]
                    for kc in range(k0, k1):
                        wv, wk = wv_of(kc)
                        st, sp_ = (kc == 0), (kc == NFC - 1)
                        toks[cc] = sc.op("pe", lambda e, ps=ps, kc=kc, wv=wv, cc=cc, st=st, sp_=sp_: e.matmul(
                            ps[:, 0:n], lhsT=wv[:, kc % 8, cc * 128:(cc + 1) * 128], rhs=self.actT[:, kc, 0:n], start=st, stop=sp_),
                            reads=[wk] + (bigkeys if kc == 0 else [("act", kc)]), writes=[pk] if kc == 0 else [], signal=sp_)
            for cc in range(2):
                c = fp * 2 + cc
                ps, pk = banks[cc]
                sc.lastw[pk] = toks[cc]
                g2 = self.mod[:, si, 40 + c:41 + c]
                if c < 4:
                    xo, xok = self.f4[c][:, 0:n - 1], [f"f4_{c}"]
                elif c < 6:
                    xo, xok = attn_f[:, c - 4, 0:n - 1], [("attn", 2 * (c - 4)), ("attn", 2 * (c - 4) + 1)]
                else:
                    xo, xok = dT_f[:, c - 6, 0:n - 1], [("dT", 2 * (c - 6)), ("dT", 2 * (c - 6) + 1)]
                sc.op("dve", lambda e, ps=ps, c=c, g2=g2, xo=xo: e.scalar_tensor_tensor(out=xo, in0=ps[:, 1:n], scalar=g2, in1=xb[:, c, 0:n - 1],
                                                                                       op0=ALU.mult, op1=ALU.add),
                      reads=[self.kmod, xks[c]], writes=[pk] + xok)
                if not Gr.first:
                    sc.op("dve", lambda e, ps=ps, c=c, g2=g2: e.scalar_tensor_tensor(out=xlp[:, c:c + 1], in0=ps[:, 0:1], scalar=g2, in1=xlp[:, c:c + 1],
                                                                                    op0=ALU.mult, op1=ALU.add),
                          reads=[self.kmod], writes=[pk, xlpk])
        dst = (self.dst_c if Gr.is_ctx else self.dst_x).rearrange("(k p) t -> p k t", p=128)
        t0 = Gr.t0
        dkey = (self.xkey_dst, Gr.name, "t")
        for c in range(4):
            sc.dma("sp", dst[:, c, t0:t0 + n - 1], self.f4[c][:, 0:n - 1], reads=[f"f4_{c}"], writes=[(self.xkey_dst, Gr.name, f"m{c}")])
        sc.dma("sp", dst[:, 4:6, t0:t0 + n - 1], attn_f[:, :, 0:n - 1], reads=[("attn", k) for k in range(4)], writes=[(self.xkey_dst, Gr.name, "m4")])
        sc.dma("sp", dst[:, 6:8, t0:t0 + n - 1], dT_f[:, :, 0:n - 1], reads=[("dT", k) for k in range(4)], writes=[(self.xkey_dst, Gr.name, "m5")])
        if not Gr.first:
            pkey = (self.xkey_dst, str(Gr.g - 1), "e")
            sc.dma("sp", dst[:, :, t0 - 1:t0], xlp[:, :].rearrange("p (k o) -> p k o", o=1), reads=[xlpk], writes=[pkey], slow=True)
        if Gr.last:
            sc.lastw[pkt] = tokt
            self.tail_finish(Gr, dst, dkey, pst, pkt)

    def tail_prep(self):
        sc = self.sc
        w0, w1, bb = self.pcol(68, 44), self.pcol(68 + 44, 44), self.pcol(200, 44)
        ta = self.tail_a
        sc.op("dve", lambda e: e.tensor_tensor(out=ta[:, :], in0=self.saved[:, :, 0], in1=w0, op=ALU.mult), reads=["saved", "par"], writes=["tail_a"])
        sc.op("dve", lambda e: e.tensor_tensor(out=self.tail_s[:, :], in0=self.saved[:, 0:NFC, 1], in1=w1[:, 0:NFC], op=ALU.mult), reads=["saved", "par"], writes=["tail_s"])
        sc.op("dve", lambda e: e.tensor_tensor(out=ta[:, 0:NFC], in0=ta[:, 0:NFC], in1=self.tail_s[:, :], op=ALU.add), reads=["tail_s"], writes=["tail_a"])
        sc.op("dve", lambda e: e.tensor_tensor(out=self.tail_s[:, :], in0=self.saved[:, NFC:2 * NFC, 1], in1=w1[:, NFC:2 * NFC], op=ALU.mult), reads=["saved", "par"], writes=["tail_s"])
        sc.op("dve", lambda e: e.tensor_tensor(out=ta[:, NFC:2 * NFC], in0=ta[:, NFC:2 * NFC], in1=self.tail_s[:, :], op=ALU.add), reads=["tail_s"], writes=["tail_a"])
        sc.op("dve", lambda e: e.tensor_tensor(out=ta[:, :], in0=ta[:, :], in1=bb, op=ALU.add), reads=["par"], writes=["tail_a"])
        sc.op("act", lambda e: e.activation(out=self.tail_s[:, :], in_=ta[:, 0:NFC], func=AF.Silu), reads=["tail_a"], writes=["tail_s"])
        sc.op("dve", lambda e: e.tensor_tensor(out=self.tail_act[:, :], in0=self.tail_s[:, :], in1=ta[:, NFC:2 * NFC], op=ALU.mult),
              reads=["tail_s", "tail_a"], writes=["tail_act"])

    def tail_finish(self, Gr, dst, dkey, ps, pk):
        sc = self.sc
        n, si = Gr.n, Gr.si
        xi = Gr.g % 2 if not Gr.is_ctx else 0
        xl, xlk = self.xl[xi], ("xl", xi)
        g2all = self.mod[:, si, 40:48]
        sc.op("dve", lambda e: e.tensor_tensor(out=self.tail_a[:, 0:8], in0=ps[:, 0:16:2], in1=g2all, op=ALU.mult),
              reads=[self.kmod], writes=[pk, "tail_a"])
        sc.op("dve", lambda e: e.tensor_tensor(out=xl[:, :], in0=xl[:, :], in1=self.tail_a[:, 0:8], op=ALU.add), reads=["tail_a"], writes=[xlk])
        t_last = Gr.t0 + n - 1
        sc.dma("sp", dst[:, :, t_last:t_last + 1], xl[:, :].rearrange("p (k o) -> p k o", o=1), reads=[xlk], writes=[dkey], slow=True)


def build_nc(n_layers=DEPTH, dbg=False):
    dry = Builder(n_layers, dbg, plan=None)
    dry.build()
    return Builder(n_layers, dbg, plan=dry.plan).build()


_CACHE = {}


def prepare_inputs(x, c, ctx, c_ctx, w_mod, b_mod, norm1_g, norm2_g, w_in, q_gain, k_gain, sink,
                   w_pool, pool_scale, w_br_attn, w_br_pool, w_out, w_up, conv_w, conv_b, w_down, n_layers=DEPTH):
    f = lambda a: np.asarray(a, dtype=np.float32)
    x, c, ctx, c_ctx = f(x), f(c), f(ctx), f(c_ctx)
    w_mod, b_mod, norm1_g, norm2_g, w_in = f(w_mod), f(b_mod), f(norm1_g), f(norm2_g), f(w_in)
    q_gain, k_gain, sink, w_pool, pool_scale = f(q_gain), f(k_gain), f(sink), f(w_pool), f(pool_scale)
    w_br_attn, w_br_pool, w_out, w_up, conv_w, conv_b, w_down = f(w_br_attn), f(w_br_pool), f(w_out), f(w_up), f(conv_w), f(conv_b), f(w_down)
    cm, cosT, sinT, pc = build_consts()
    params = build_params(b_mod, norm1_g, norm2_g, pool_scale, conv_w, conv_b, q_gain, k_gain, sink)
    shared = {"params": params, "cmat": cm, "cosT": cosT, "sinT": sinT, "poolc": pc}
    for l in range(n_layers):
        shared[f"wst{l}"] = build_wstream(l, w_in, w_pool, w_br_attn, w_br_pool, w_out, w_up, w_down)
        shared[f"wmod{l}"] = build_wmod_stream(l, w_mod)
    in_maps = []
    for b in range(NCORES):
        m = dict(shared)
        m["xT"] = np.ascontiguousarray(x[b].T)
        m["cxT"] = np.ascontiguousarray(ctx[b].T)
        cond = np.zeros((128, 16), np.float32)
        cond[:, 0:8] = c[b].reshape(8, 128).T
        cond[:, 8:16] = c_ctx.reshape(8, 128).T
        m["cond"] = cond
        in_maps.append(m)
    return in_maps


def kernel(**inputs):
    in_maps = prepare_inputs(**inputs)
    if "nc" not in _CACHE:
        _CACHE["nc"] = build_nc(DEPTH)
    nc = _CACHE["nc"]
    res = run_bass_kernel_spmd(nc, in_maps, core_ids=list(range(NCORES)))
    out = np.stack([np.ascontiguousarray(r["outT"].T) for r in res.results], axis=0)
    return out.astype(np.float32)
```

```python
import numpy as np
from contextlib import ExitStack
import concourse.bass as bass
import concourse.mybir as mybir
from concourse.bass_utils import run_bass_kernel_spmd

F32 = mybir.dt.float32
BF16 = mybir.dt.bfloat16
AF = mybir.ActivationFunctionType
ALU = mybir.AluOpType

D = 1024
SEQ = 4096
CTX = 256
DEPTH = 4
NCORES = 8
GRID_W = 64
HD = 64
DFF = 2816
NFC = DFF // 128
EPS = 1e-6
G = 512
NG = SEQ // G
RING = 12
NSLOT = 7
SLOT_E = 2048
NPL = 256

PIECES = []
for _i in range(13):
    PIECES.append((f"in{_i}", 2048))
PIECES.append(("pool", 512))
PIECES += [("bra0", 2048), ("bra1", 2048), ("brp0", 2048), ("brp1", 2048)]
PIECES += [(f"out{_i}", 2048) for _i in range(4)]
PIECES += [(f"up{_i}", 2048) for _i in range(NFC)]
for _f in range(4):
    PIECES += [(f"dn{_f}_0", 2048), (f"dn{_f}_1", 2048), (f"dn{_f}_2", 1536)]
POFF = {}
_o = 0
for _n, _e in PIECES:
    POFF[_n] = (_o, _e)
    _o += 128 * _e
WSTREAM_LEN = _o


def _piece(W, kchunks, cols):
    K = W.shape[0] // 128
    Wr = W.reshape(K, 128, W.shape[1])[kchunks][:, :, cols]
    return np.ascontiguousarray(Wr.transpose(1, 0, 2)).reshape(128, -1)


def _qperm():
    idx = []
    for cq in range(4):
        for half in range(2):
            h = cq + 4 * half
            idx += list(range(h * 64, (h + 1) * 64))
    return np.array(idx)


def build_wstream(l, w_in, w_pool, w_br_attn, w_br_pool, w_out, w_up, w_down):
    out = np.empty(WSTREAM_LEN, np.float32)

    def put(name, arr):
        o, e = POFF[name]
        assert arr.shape == (128, e), (name, arr.shape, e)
        out[o:o + 128 * e] = arr.reshape(-1)

    qp = _qperm()
    cols = np.concatenate([np.arange(512, 640), np.arange(640, 768), qp, np.arange(768, 1280), np.arange(1280, 3328)])
    Wp = w_in[l][:, cols]
    k8 = list(range(8))
    for i in range(13):
        put(f"in{i}", _piece(Wp, k8, np.arange(i * 256, (i + 1) * 256)))
    put("pool", np.ascontiguousarray(w_pool[l].transpose(1, 0, 2)).reshape(128, 512))
    bra = w_br_attn[l][qp, :]
    brp = w_br_pool[l]
    for hf in range(2):
        put(f"bra{hf}", _piece(bra, [0, 1, 2, 3], np.arange(hf * 512, (hf + 1) * 512)))
        put(f"brp{hf}", _piece(brp, [0, 1, 2, 3], np.arange(hf * 512, (hf + 1) * 512)))
    for i in range(4):
        put(f"out{i}", _piece(w_out[l], k8, np.arange(i * 256, (i + 1) * 256)))
    for i in range(NFC):
        cc = np.concatenate([np.arange(i * 128, (i + 1) * 128), np.arange(DFF + i * 128, DFF + (i + 1) * 128)])
        put(f"up{i}", _piece(w_up[l], k8, cc))
    for f in range(4):
        cc = np.arange(f * 256, (f + 1) * 256)
        put(f"dn{f}_0", _piece(w_down[l], list(range(0, 8)), cc))
        put(f"dn{f}_1", _piece(w_down[l], list(range(8, 16)), cc))
        put(f"dn{f}_2", _piece(w_down[l], list(range(16, 22)), cc))
    return out


def build_wmod_stream(l, w_mod):
    W = w_mod[l].reshape(8, 128, 24, 256)
    return np.ascontiguousarray(W.transpose(2, 1, 0, 3)).reshape(24, 128, 2048)


def build_params(b_mod, norm1_g, norm2_g, pool_scale, conv_w, conv_b, q_gain, k_gain, sink):
    P = np.zeros((128, DEPTH * NPL), np.float32)
    for l in range(DEPTH):
        o = l * NPL
        P[:, o:o + 48] = b_mod[l].reshape(48, 128).T
        P[:, o + 48:o + 56] = norm1_g[l].reshape(8, 128).T
        P[:, o + 56:o + 64] = norm2_g[l].reshape(8, 128).T
        P[:, o + 64:o + 68] = pool_scale[l].reshape(4, 128).T
        for j in range(3):
            P[:, o + 68 + j * 44:o + 68 + (j + 1) * 44] = conv_w[l, j].reshape(44, 128).T
        P[:, o + 200:o + 244] = conv_b[l].reshape(44, 128).T
        P[:, o + 244] = np.tile(q_gain[l], 2)
        P[:, o + 245] = np.tile(k_gain[l], 2)
        P[:, o + 246:o + 254] = sink[l][None, :]
    return P


def build_consts():
    cm = np.zeros((128, 6 * 128), np.float32)
    cm[:, 0:128] = 1.0 / 1024.0
    for hb in (0, 64):
        cm[hb:hb + 64, 128 + hb:128 + hb + 64] = 1.0 / 64.0
    for m in range(128):
        d = m % 64
        half = (d % 32) // 16
        partner = m + 16 if half == 0 else m - 16
        cm[partner, 256 + m] = 1.0
    kj = np.arange(128)[:, None]
    qi = np.arange(128)[None, :]
    cm[:, 384:512] = ((kj <= qi).astype(np.float32) - 1.0) * 30000.0
    cm[:, 512:640] = ((kj >= qi).astype(np.float32) - 1.0) * 30000.0
    cm[:, 640:768] = np.eye(128, dtype=np.float32)
    n_freq = HD // 4
    inv = (np.float32(10000.0) ** (-(np.arange(n_freq, dtype=np.float32)) / np.float32(n_freq))).astype(np.float32)
    t = np.arange(SEQ)
    row = (t // GRID_W).astype(np.float32)
    col = (t % GRID_W).astype(np.float32)
    cosT = np.zeros((128, SEQ), np.float32)
    sinT = np.zeros((128, SEQ), np.float32)
    for p in range(128):
        d = p % 64
        axis = d // 32
        half = (d % 32) // 16
        f = d % 16
        ang = ((row if axis == 0 else col) * inv[f]).astype(np.float32)
        cosT[p] = np.cos(ang).astype(np.float32)
        s = np.sin(ang).astype(np.float32)
        sinT[p] = -s if half == 0 else s
    pc = np.zeros((128, 64), np.float32)
    for c, w in enumerate((2, 4, 8, 16)):
        for i in range(8):
            cntl = (i + w // 2) - max(i - w // 2, 0)
            pc[:, c * 8 + i] = 1.0 / cntl
            tt = -8 + i
            hi = min(tt + w // 2, 0)
            lo = tt - w // 2
            pc[:, 32 + c * 8 + i] = 1.0 / (hi - lo)
    return cm, cosT, sinT, pc


class Tok:
    __slots__ = ("eng", "sem", "val")

    def __init__(self, eng):
        self.eng = eng
        self.sem = None
        self.val = None


ENGS = ("pe", "act", "dve", "pool", "sp")
SEM_ROLL = 30000


class Sched:
    def __init__(self, nc, es):
        self.nc = nc
        self.es = es
        self.prog = {e: [] for e in ENGS}
        self.esem = {}
        self.ecount = {}
        self.nroll = {}
        for e in ("pe", "act", "dve", "pool"):
            self.esem[e] = es.enter_context(nc.semaphore(f"c_{e}_0"))
            self.ecount[e] = 0
            self.nroll[e] = 0
        self.pending = {e: [] for e in ENGS}
        self.waited = {e: {} for e in ENGS}
        self.lastw = {}
        self.readers = {}
        self.dsem = {q: [es.enter_context(nc.semaphore(f"d_{q}_{i}")) for i in range(12)] for q in ("sp", "pool")}
        self.dcount = {q: [0] * 12 for q in ("sp", "pool")}
        self.drr = {"sp": 0, "pool": 0}
        self.ninstr = {e: 0 for e in ENGS}

    def _deps(self, reads, writes):
        toks = []
        for k in reads:
            t = self.lastw.get(k)
            if t is not None:
                toks.append(t)
        for k in writes:
            t = self.lastw.get(k)
            if t is not None:
                toks.append(t)
            toks.extend(self.readers.get(k, ()))
        return toks

    def _waits(self, eng, toks):
        waits = []
        for t in toks:
            assert t.val is not None, "dependency on an unresolved (unsignalled) op"
            if t.eng == "pe" and eng == "pe":
                continue
            sid = id(t.sem)
            if t.val > self.waited[eng].get(sid, 0):
                self.waited[eng][sid] = t.val
                waits.append((t.sem, t.val))
        return waits

    def _commit(self, tok, reads, writes):
        for k in reads:
            self.readers.setdefault(k, []).append(tok)
        for k in writes:
            self.lastw[k] = tok
            self.readers[k] = []

    def op(self, eng, fn, reads=(), writes=(), signal=True):
        waits = self._waits(eng, self._deps(reads, writes))
        tok = Tok(eng)
        sem = None
        if signal:
            if self.ecount[eng] >= SEM_ROLL:
                self.nroll[eng] += 1
                self.esem[eng] = self.es.enter_context(self.nc.semaphore(f"c_{eng}_{self.nroll[eng]}"))
                self.ecount[eng] = 0
            self.ecount[eng] += 1
            sem = self.esem[eng]
            tok.sem, tok.val = sem, self.ecount[eng]
            for p in self.pending[eng]:
                p.sem, p.val = tok.sem, tok.val
            self.pending[eng] = []
        else:
            self.pending[eng].append(tok)

        def run(e, waits=waits, fn=fn, sem=sem):
            for s, v in waits:
                e.wait_ge(s, v)
            ins = fn(e)
            if sem is not None:
                ins.then_inc(sem, 1)

        self.prog[eng].append(run)
        self.ninstr[eng] += 1
        self._commit(tok, reads, writes)
        return tok

    def dma(self, q, out_ap, in_ap, reads=(), writes=(), slow=False):
        waits = self._waits(q, self._deps(reads, writes))
        i = self.drr[q]
        self.drr[q] = (i + 1) % len(self.dsem[q])
        sem = self.dsem[q][i]
        c = self.dcount[q][i]
        if c > self.waited[q].get(id(sem), 0):
            self.waited[q][id(sem)] = c
            waits.append((sem, c))
        self.dcount[q][i] = c + 16
        tok = Tok("dma")
        tok.sem, tok.val = sem, c + 16

        def run(e, waits=waits, sem=sem, out_ap=out_ap, in_ap=in_ap, slow=slow):
            for s, v in waits:
                e.wait_ge(s, v)
            if slow:
                e.dma_start(out=out_ap, in_=in_ap, allow_slow_non_contiguous=True).then_inc(sem, 16)
            else:
                e.dma_start(out=out_ap, in_=in_ap).then_inc(sem, 16)

        self.prog[q].append(run)
        self.ninstr[q] += 1
        self._commit(tok, reads, writes)
        return tok

    def finish(self):
        finals = [(self.dsem["sp"][i], self.dcount["sp"][i]) for i in range(12) if self.dcount["sp"][i] > 0]

        def run(e, finals=finals):
            for s, v in finals:
                e.wait_ge(s, v)

        self.prog["sp"].append(run)


class DrySched:
    def __init__(self):
        self.lastw = {}
        self.prog = {e: [] for e in ENGS}
        self.ninstr = {e: 0 for e in ENGS}

    def op(self, eng, fn, reads=(), writes=(), signal=True):
        t = Tok(eng)
        t.val = 1
        return t

    def dma(self, q, out_ap, in_ap, reads=(), writes=(), slow=False):
        t = Tok("dma")
        t.val = 1
        return t

    def finish(self):
        pass


AHEAD = 4
assert AHEAD + 3 <= NSLOT

class Grp:
    def __init__(self, is_ctx, g):
        self.is_ctx = is_ctx
        self.g = g
        self.n = CTX if is_ctx else G
        self.nb = self.n // 128
        self.slot = 1 if is_ctx else g % 2
        self.xs = 2 if is_ctx else g % 3
        self.t0 = 0 if is_ctx else g * G
        self.si = 1 if is_ctx else 0
        self.first = is_ctx or g == 0
        self.last = is_ctx or g == NG - 1
        if is_ctx:
            self.kslots = [0, 1]
        else:
            self.kslots = [2 + ((4 * g + i) % RING) for i in range(4)]
        self.name = "c" if is_ctx else str(g)


def mslot(b):
    return 2 + (b % RING)


class Builder:
    def __init__(self, n_layers=DEPTH, dbg=False, plan=None):
        self.n_layers = n_layers
        self.dbg = dbg
        self.dry = plan is None
        self.plan = [] if plan is None else plan
        self.pidx = 0
        self.issued = 0
        self.pslots = {}
        self.nc = bass.Bass("TRN2", target_bir_lowering=False)
        self.es = ExitStack()

    def sb(self, name, shape, dt):
        return self.es.enter_context(self.nc.sbuf_tensor(name, shape, dt))

    def dram_in(self, name, shape, dt=F32):
        return self.nc.dram_tensor(name, shape, dt, kind="ExternalInput").ap()

    def build(self):
        nc, es = self.nc, self.es
        L = self.n_layers
        self.xT = self.dram_in("xT", [D, SEQ])
        self.cxT = self.dram_in("cxT", [D, CTX])
        self.cond = self.dram_in("cond", [128, 16])
        self.params_d = self.dram_in("params", [128, DEPTH * NPL])
        self.cmat_d = self.dram_in("cmat", [128, 768])
        self.cos_d = self.dram_in("cosT", [128, SEQ])
        self.sin_d = self.dram_in("sinT", [128, SEQ])
        self.poolc_d = self.dram_in("poolc", [128, 64])
        self.wst = [self.dram_in(f"wst{l}", [WSTREAM_LEN]) for l in range(L)]
        self.wmod = [self.dram_in(f"wmod{l}", [24, 128, 2048]) for l in range(L)]
        self.outT = nc.dram_tensor("outT", [D, SEQ], F32, kind="ExternalOutput").ap()
        self.S = [nc.dram_tensor(f"S{i}", [D, SEQ], F32, kind="ExternalOutput" if self.dbg else "Internal").ap() for i in range(2)]
        self.C = [nc.dram_tensor(f"C{i}", [D, CTX], F32, kind="ExternalOutput" if self.dbg else "Internal").ap() for i in range(2)]

        self.sc = DrySched() if self.dry else Sched(nc, es)
        sb = self.sb
        self.wslot = [sb(f"wslot{i}", [128, SLOT_E], BF16) for i in range(NSLOT)]
        self.wrr = 0
        self.xb = [sb(f"xb{i}", [128, 8, G], F32) for i in range(3)]
        self.hb = [sb(f"hb{i}", [128, 8, G], BF16) for i in range(2)]
        self.ntmp = [sb(f"ntmp{i}", [128, G], F32) for i in range(2)]
        self.rln = sb("rln", [128, G], F32)
        self.rstd = sb("rstd", [128, G], F32)
        self.kT = [sb(f"kT{i}", [128, (2 + RING) * 128], BF16) for i in range(2)]
        self.V = sb("V", [128, 2 + RING, 2, 128], BF16)
        self.qb = sb("qb", [128, 4, G], BF16)
        self.big = sb("big", [128, 24 * G], BF16)
        self.gates = self.big[:, 0:16 * G].rearrange("p (c t) -> p c t", t=G)
        self.yb = self.big[:, 16 * G:24 * G].rearrange("p (c t) -> p c t", t=G)
        self.actT = self.big[:, 0:NFC * G].rearrange("p (c t) -> p c t", t=G)
        self.pT = sb("pT", [128, 4, G + 16], F32)
        self.attn = sb("attn", [128, 4, G], BF16)
        self.dT = sb("dT", [128, 4, G], BF16)
        self.po = sb("po", [128, 4, G], BF16)
        self.f4 = [sb(f"f4_{i}", [128, G + 16], F32) for i in range(4)]
        self.PT = [sb(f"PT{i}", [128, G], BF16) for i in range(3)]
        self.ptr = 0
        self.qsq2 = [sb(f"qsq{i}", [128, G], BF16) for i in range(2)]
        self.qln2 = [sb("qln0", [128, G], F32)] * 2
        self.qrs2 = [sb(f"qrs{i}", [128, G], F32) for i in range(2)]
        self.qn2 = [sb(f"qn{i}", [128, G], BF16) for i in range(2)]
        self.qkpar = 0
        self.deferred = []
        self.cosb = sb("cosb", [128, G], F32)
        self.sinb = sb("sinb", [128, G], F32)
        self.t1b = [sb("t1_0", [128, G], F32)] * 2
        self.t2b = [sb("t2_0", [128, G], F32)] * 2
        self.t1, self.t2 = self.t1b[0], self.t2b[0]
        self.lnden = sb("lnden", [128, G], F32)
        self.rden = sb("rden", [128, G], F32)
        self.sil2 = [sb(f"sil{i}", [128, G], F32) for i in range(2)]
        self.corr = sb("corr", [128, 2 * NFC, 2], F32)
        self.saved = sb("saved", [128, 2 * NFC, 2], F32)
        self.xl = [sb(f"xl{i}", [128, 8], F32) for i in range(2)]
        self.cmat = sb("cmat_s", [128, 768], BF16)
        self.poolc = sb("poolc_s", [128, 64], F32)
        self.par = sb("par_s", [128, DEPTH * NPL], F32)
        self.condb = sb("condb", [128, 16], F32)
        self.scond = sb("scond", [128, 16], BF16)
        self.modL = [sb(f"mod{i}", [128, 2, 48], F32) for i in range(2)]
        self.gs1L = [sb(f"gs1_{i}", [128, 2, 8], F32) for i in range(2)]
        self.gs2L = [sb(f"gs2_{i}", [128, 2, 8], F32) for i in range(2)]
        self.esinkL = [sb(f"esink{i}", [128, 8], F32) for i in range(2)]
        self.tail_a = sb("tail_a", [128, 2 * NFC], F32)
        self.tail_s = sb("tail_s", [128, NFC], F32)
        self.tail_act = sb("tail_act", [128, NFC], BF16)
        self.epsb = sb("epsb", [128, 1], F32)
        self.sbuf_left = nc.sbuf_bytes_remaining
        self.ps = [es.enter_context(nc.psum_tensor(f"ps{i}", [128, 512], F32)) for i in range(8)]
        self.pools = {"mm": [0, 1, 2, 3], "st": [4, 5, 0], "o": [6, 1], "n": [7], "aux": [4, 5, 6, 7], "qk": [4, 5], "pp": [2, 3], "up": [0, 1, 2, 3, 4, 5, 6, 7]}
        self.prr = {k: 0 for k in self.pools}

        sc = self.sc
        sc.dma("pool", self.cmat[:, :], self.cmat_d[:, :], writes=["cmat"])
        sc.dma("sp", self.poolc[:, :], self.poolc_d[:, :], writes=["poolc"])
        sc.dma("sp", self.par[:, :], self.params_d[:, :], writes=["par"])
        sc.dma("sp", self.condb[:, :], self.cond[:, :], writes=["condb"])
        sc.op("dve", lambda e: e.memset(self.kT[0][:, :], 0.0), writes=[("kT", s) for s in range(2 + RING)])
        sc.op("dve", lambda e: e.memset(self.kT[1][:, :], 0.0), writes=[("kT", s) for s in range(2 + RING)])
        sc.op("dve", lambda e: e.memset(self.V[:, :, 0, 64:128], 1.0), writes=[("V", s) for s in range(2 + RING)])
        sc.op("dve", lambda e: e.memset(self.V[:, :, 1, 0:64], 1.0), writes=[("V", s) for s in range(2 + RING)])
        sc.op("act", lambda e: e.activation(out=self.scond[:, :], in_=self.condb[:, :], func=AF.Silu),
              reads=["condb"], writes=["scond"])
        sc.op("dve", lambda e: e.memset(self.epsb[:, :], EPS), writes=["epsb"])

        self.ones_mean = self.cmat[:, 0:128]
        self.bd_mean = self.cmat[:, 128:256]
        self.perm = self.cmat[:, 256:384]
        self.mask_next = self.cmat[:, 384:512]
        self.mask_prev = self.cmat[:, 512:640]
        self.ident = self.cmat[:, 640:768]

        self.l = 0
        for k in range(8):
            self.adaln_part(0, k)
        self.adaln_finish(0)
        for l in range(L):
            self.set_layer(l)
            last = (l == DEPTH - 1)
            self.src_x = self.xT if l == 0 else self.S[(l - 1) % 2]
            self.src_c = self.cxT if l == 0 else self.C[(l - 1) % 2]
            self.dst_x = self.outT if l == L - 1 else self.S[l % 2]
            self.dst_c = self.C[l % 2]
            self.xkey_src = ("X", "in" if l == 0 else (l - 1) % 2)
            self.xkey_dst = ("X", "out" if l == L - 1 else l % 2)
            gc = Grp(True, 0)
            grps = [Grp(False, g) for g in range(NG)]
            self.front_load(gc)
            self.front_load(grps[0])
            self.front_sq(gc)
            self.front_a(gc)
            self.front_b(gc)
            if not last:
                self.back_proj_a(gc, 0)
                self.back_proj_a(gc, 1)
                self.back_rest(gc, None)
                self.ffn(gc)
            self.front_load(grps[1])
            self.front_sq(grps[0])
            self.front_a(grps[0])
            self.front_b(grps[0])
            for g in range(NG):
                Gr = grps[g]
                nxt = grps[g + 1] if g + 1 < NG else None
                if g + 2 < NG:
                    self.front_load(grps[g + 2])
                if nxt is not None:
                    self.front_sq(nxt)
                self.back_proj_a(Gr, 0)
                if nxt is not None:
                    self.front_a(nxt)
                self.back_proj_a(Gr, 1)
                if nxt is not None:
                    self.front_b(nxt)
                self.back_rest(Gr, nxt)
                if l + 1 < L:
                    self.adaln_part(l + 1, g)
                self.ffn(Gr)
            if l + 1 < L:
                self.adaln_finish(l + 1)
        sc.finish()

        prog = sc.prog
        if self.dry:
            es.close()
            return None
        with nc.Block() as block:
            @block.tensor
            def _(e):
                for f in prog["pe"]:
                    f(e)

            @block.scalar
            def _(e):
                for f in prog["act"]:
                    f(e)

            @block.vector
            def _(e):
                for f in prog["dve"]:
                    f(e)

            @block.gpsimd
            def _(e):
                for f in prog["pool"]:
                    f(e)

            @block.sync
            def _(e):
                for f in prog["sp"]:
                    f(e)
        es.close()
        return nc

    def psum(self, pool):
        banks = self.pools[pool]
        i = self.prr[pool]
        self.prr[pool] = (i + 1) % len(banks)
        b = banks[i]
        return self.ps[b], ("ps", b)

    def _issue_upto(self, k):
        k = min(k, len(self.plan) - 1)
        while self.issued <= k:
            idx = self.issued
            kind, l, nm = self.plan[idx]
            i = idx % NSLOT
            slot = self.wslot[i]
            key = ("w", i)
            if kind == "w":
                o, e = POFF[nm]
                src = self.wst[l][o:o + 128 * e].rearrange("(p e) -> p e", e=e)
                self.sc.dma("pool", slot[:, 0:e], src, writes=[key])
            else:
                self.sc.dma("pool", slot[:, :], self.wmod[l][nm, :, :], writes=[key])
            self.issued += 1

    def wget(self, name, kind="w", layer=None):
        ent = (kind, self.l if layer is None else layer, name)
        if self.dry:
            self.plan.append(ent)
            return self.wslot[0], ("w", 0)
        idx = self.pidx
        assert self.plan[idx] == ent, (self.plan[idx], ent)
        self.pidx += 1
        self._issue_upto(idx + AHEAD)
        i = idx % NSLOT
        return self.wslot[i], ("w", i)

    def pcol(self, off, n=1):
        o = self.l * NPL + off
        return self.par[:, o:o + n]

    def mm_group(self, out_ap, pskey, terms, extra_reads=()):
        sc = self.sc
        nt = len(terms)
        tok = None
        for i, (lh, rh, rk) in enumerate(terms):
            st, sp_ = (i == 0), (i == nt - 1)
            tok = sc.op("pe", lambda e, lh=lh, rh=rh, st=st, sp_=sp_: e.matmul(out_ap, lhsT=lh, rhs=rh, start=st, stop=sp_),
                        reads=list(rk) + (list(extra_reads) if i == 0 else []),
                        writes=[pskey] if i == 0 else [], signal=sp_)
        self.sc.lastw[pskey] = tok
        return tok

    def pcol_l(self, l, off, n=1):
        o = l * NPL + off
        return self.par[:, o:o + n]

    def adaln_part(self, l, k):
        sc = self.sc
        p2 = l % 2
        mod, kmod = self.modL[p2], ("mod", p2)
        ps, pk = self.psum("mm")
        tok = None
        rhs_all = self.scond[:, :].rearrange("p (s k) -> p k s", s=2)
        first = True
        for i2 in range(3 * k, 3 * k + 3):
            slot, key = self.wget(i2, kind="m", layer=l)
            wv = slot[:, :].rearrange("p (k f) -> p k f", f=256)
            for cc in range(2):
                jj = 2 * (i2 - 3 * k) + cc
                for kc in range(8):
                    st, sp_ = (kc == 0), (kc == 7)
                    tok = sc.op("pe", lambda e, wv=wv, kc=kc, jj=jj, cc=cc, st=st, sp_=sp_: e.matmul(
                        ps[:, 2 * jj:2 * jj + 2], lhsT=wv[:, kc, cc * 128:(cc + 1) * 128], rhs=rhs_all[:, kc, :], start=st, stop=sp_),
                        reads=[key, "scond"], writes=[pk] if first else [], signal=sp_)
                    first = False
        sc.lastw[pk] = tok
        bm = self.pcol_l(l, 6 * k, 6)
        for s_ in range(2):
            sc.op("dve", lambda e, s_=s_: e.tensor_tensor(out=mod[:, s_, 6 * k:6 * k + 6], in0=ps[:, s_:12:2], in1=bm, op=ALU.add),
                  reads=["par"], writes=[pk, kmod])

    def adaln_finish(self, l):
        sc = self.sc
        p2 = l % 2
        mod, kmod = self.modL[p2], ("mod", p2)
        for (gsb, sco, ngo, nm) in ((self.gs1L[p2], 8, 48, ("gs1", p2)), (self.gs2L[p2], 32, 56, ("gs2", p2))):
            ng = self.pcol_l(l, ngo, 8)
            for s_ in range(2):
                sc.op("dve", lambda e, s_=s_, gsb=gsb, sco=sco, ng=ng: e.scalar_tensor_tensor(
                    out=gsb[:, s_, :], in0=mod[:, s_, sco:sco + 8], scalar=1.0, in1=ng, op0=ALU.add, op1=ALU.mult),
                    reads=[kmod, "par"], writes=[nm])
        esink = self.esinkL[p2]
        sk = self.pcol_l(l, 246, 8)
        sc.op("act", lambda e: e.activation(out=esink[:, :], in_=sk, func=AF.Exp), reads=["par"], writes=[("esink", p2)])

    def set_layer(self, l):
        p2 = l % 2
        self.l = l
        self.mod, self.gs1, self.gs2, self.esink = self.modL[p2], self.gs1L[p2], self.gs2L[p2], self.esinkL[p2]
        self.kmod, self.kgs1, self.kgs2, self.kesink = ("mod", p2), ("gs1", p2), ("gs2", p2), ("esink", p2)

    def norm_mod(self, Gr, gsb, gsname, sh_off, stats_done=False, interleave=False, sq_done=False):
        sc = self.sc
        s, n, si = Gr.slot, Gr.n, Gr.si
        xb, hb = self.xb[Gr.xs], self.hb[s]
        mod, kmod = self.mod, self.kmod
        xks = [("xb", Gr.xs, kc) for kc in range(8)]
        hks = [("hb", s, kc) for kc in range(8)]
        if not stats_done:
            if not sq_done:
                sc.op("act", lambda e: e.activation(out=hb[:, :, 0:n], in_=xb[:, :, 0:n], func=AF.Square), reads=xks, writes=hks)
            ps, pk = self.psum("n")
            self.mm_group(ps[:, 0:n], pk, [(self.ones_mean, hb[:, kc, 0:n], [hks[kc], "cmat"]) for kc in range(8)])
        else:
            ps, pk = self.nstat
        sc.op("act", lambda e: e.activation(out=self.rln[:, 0:n], in_=ps[:, 0:n], func=AF.Ln, bias=self.epsb[:, 0:1], scale=1.0),
              reads=["epsb"], writes=[pk, "rln"])
        sc.op("act", lambda e: e.activation(out=self.rstd[:, 0:n], in_=self.rln[:, 0:n], func=AF.Exp, scale=-0.5),
              reads=["rln"], writes=["rstd"])
        if interleave and not stats_done:
            self.tick()
        def chunk(kc):
            nt = self.ntmp[kc % 2]
            nk = ("ntmp", kc % 2)
            sc.op("dve", lambda e: e.tensor_tensor(out=nt[:, 0:n], in0=xb[:, kc, 0:n], in1=self.rstd[:, 0:n], op=ALU.mult),
                  reads=[xks[kc], "rstd"], writes=[nk])
            sc.op("act", lambda e: e.activation(out=hb[:, kc, 0:n], in_=nt[:, 0:n], func=AF.Identity,
                                                bias=mod[:, si, sh_off + kc:sh_off + kc + 1], scale=gsb[:, si, kc:kc + 1]),
                  reads=[nk, kmod, gsname], writes=[hks[kc]])

        for kc in range(8):
            if interleave:
                self.defer(kc + 1, lambda kc=kc: chunk(kc))
            else:
                chunk(kc)

    def defer(self, delay, fn):
        self.deferred.append([delay, fn])

    def tick(self):
        due = [d for d in self.deferred if d[0] <= 1]
        self.deferred = [[d[0] - 1, d[1]] for d in self.deferred if d[0] > 1]
        for d in due:
            d[1]()

    def flush(self):
        while self.deferred:
            self.tick()

    def qk_post(self, ps, pk, n, gain_off, rope, dst_ap, dst_keys):
        sc = self.sc
        par = self.qkpar
        self.qkpar ^= 1
        qsq, qln, qrs, qn, t1, t2 = self.qsq2[par], self.qln2[par], self.qrs2[par], self.qn2[par], self.t1b[par], self.t2b[par]
        kq, kr, kn = [(nm, par) for nm in ("qsq", "qrs", "qn")]
        kl, k1, k2 = ("qln", 0), ("t1", 0), ("t2", 0)
        gain = self.pcol(gain_off, 1)
        if not isinstance(dst_ap, list):
            dst_ap = [(0, 128, dst_ap)]
        sc.op("act", lambda e: e.activation(out=qsq[:, 0:n], in_=ps[:, 0:n], func=AF.Square), reads=[], writes=[pk, kq])

        def step1():
            pn, pnk = self.psum("n")
            self.mm_group(pn[:, 0:n], pnk, [(self.bd_mean, qsq[:, 0:n], [kq, "cmat"])])
            sc.op("act", lambda e: e.activation(out=qln[:, 0:n], in_=pn[:, 0:n], func=AF.Ln, bias=self.epsb[:, 0:1], scale=1.0),
                  reads=["epsb"], writes=[pnk, kl])
            sc.op("act", lambda e: e.activation(out=qrs[:, 0:n], in_=qln[:, 0:n], func=AF.Exp, scale=-0.5), reads=[kl], writes=[kr])
            if not rope:
                for (p0, p1, dap) in dst_ap:
                    sc.op("dve", lambda e, p0=p0, p1=p1, dap=dap: e.scalar_tensor_tensor(out=dap, in0=ps[p0:p1, 0:n], scalar=gain[p0:p1, :], in1=qrs[p0:p1, 0:n],
                                                                                   op0=ALU.mult, op1=ALU.mult),
                          reads=[kr, "par"], writes=[pk] + dst_keys)
            else:
                sc.op("dve", lambda e: e.scalar_tensor_tensor(out=qn[:, 0:n], in0=ps[:, 0:n], scalar=gain, in1=qrs[:, 0:n], op0=ALU.mult, op1=ALU.mult),
                      reads=[kr, "par"], writes=[pk, kn])

        def step2():
            pr, prk = self.psum("qk")
            self.mm_group(pr[:, 0:n], prk, [(self.perm, qn[:, 0:n], [kn, "cmat"])])
            sc.op("dve", lambda e: e.tensor_tensor(out=t1[:, 0:n], in0=qn[:, 0:n], in1=self.cosb[:, 0:n], op=ALU.mult), reads=[kn, "cosb"], writes=[k1])
            sc.op("dve", lambda e: e.tensor_tensor(out=t2[:, 0:n], in0=pr[:, 0:n], in1=self.sinb[:, 0:n], op=ALU.mult), reads=["sinb"], writes=[prk, k2])
            for (p0, p1, dap) in dst_ap:
                sc.op("dve", lambda e, p0=p0, p1=p1, dap=dap: e.tensor_tensor(out=dap, in0=t1[p0:p1, 0:n], in1=t2[p0:p1, 0:n], op=ALU.add),
                      reads=[k1, k2], writes=dst_keys)

        self.defer(1, step1)
        if rope:
            self.defer(3, step2)

    def front_load(self, Gr):
        sc = self.sc
        n, t0 = Gr.n, Gr.t0
        src = (self.src_c if Gr.is_ctx else self.src_x).rearrange("(k p) t -> p k t", p=128)
        sc.dma("sp", self.xb[Gr.xs][:, :, 0:n], src[:, :, t0:t0 + n],
               reads=[(self.xkey_src, Gr.name, pt_) for pt_ in ("m0", "m1", "m2", "m3", "m4", "m5", "e", "t")], writes=[("xb", Gr.xs, kc) for kc in range(8)])

    def front_sq(self, Gr):
        s, n = Gr.slot, Gr.n
        xb, hb = self.xb[Gr.xs], self.hb[s]
        self.sc.op("act", lambda e: e.activation(out=hb[:, :, 0:n], in_=xb[:, :, 0:n], func=AF.Square),
                   reads=[("xb", Gr.xs, kc) for kc in range(8)], writes=[("hb", s, kc) for kc in range(8)])

    def front_a(self, Gr):
        self.norm_mod(Gr, self.gs1, self.kgs1, 0, interleave=True, sq_done=True)

    def front_b(self, Gr):
        self.flush()
        sc = self.sc
        s, n, t0 = Gr.slot, Gr.n, Gr.t0
        if not Gr.is_ctx:
            sc.dma("sp", self.cosb[:, :], self.cos_d[:, t0:t0 + n], writes=["cosb"])
            sc.dma("sp", self.sinb[:, :], self.sin_d[:, t0:t0 + n], writes=["sinb"])
        hb = self.hb[s]
        hk = lambda kc: ("hb", s, kc)
        w, wk = self.wget("in0")
        wv = w[:, :].rearrange("p (k f) -> p k f", f=256)
        ps, pk = self.psum("mm")
        self.mm_group(ps[:, 0:n], pk, [(wv[:, kc, 0:128], hb[:, kc, 0:n], [wk, hk(kc)]) for kc in range(8)])
        s0 = Gr.kslots[0]
        kdst = [(0, 64, self.kT[0][0:64, s0 * 128:s0 * 128 + n]), (64, 128, self.kT[1][64:128, s0 * 128:s0 * 128 + n])]
        self.qk_post(ps, pk, n, 245, not Gr.is_ctx, kdst, [("kT", sl) for sl in Gr.kslots])
        ps2, pk2 = self.psum("mm")
        tok = None
        for b in range(Gr.nb):
            for kc in range(8):
                st, sp_ = (kc == 0), (kc == 7)
                tok = sc.op("pe", lambda e, b=b, kc=kc, st=st, sp_=sp_: e.matmul(
                    ps2[:, b * 128:(b + 1) * 128], lhsT=hb[:, kc, b * 128:(b + 1) * 128], rhs=wv[:, kc, 128:256], start=st, stop=sp_),
                    reads=[wk, hk(kc)], writes=[pk2] if (b == 0 and kc == 0) else [], signal=sp_)
            self.tick()
        self.flush()
        sc.lastw[pk2] = tok
        psv = ps2[:, 0:n].rearrange("p (b f) -> p b f", f=128)
        vk = [("V", sl) for sl in Gr.kslots]
        sc.op("act", lambda e: e.activation(out=self.V[:, s0:s0 + Gr.nb, 0, 0:64], in_=psv[:, :, 0:64], func=AF.Identity),
              writes=[pk2] + vk)
        sc.op("act", lambda e: e.activation(out=self.V[:, s0:s0 + Gr.nb, 1, 64:128], in_=psv[:, :, 64:128], func=AF.Identity),
              writes=[pk2] + vk)

    def back_proj_a(self, Gr, part):
        sc = self.sc
        s, n = Gr.slot, Gr.n
        hb = self.hb[s]
        hk = lambda kc: ("hb", s, kc)
        for pi in (range(2) if part == 0 else []):
            w, wk = self.wget(f"in{1 + pi}")
            wv = w[:, :].rearrange("p (k f) -> p k f", f=256)
            for cc in range(2):
                cq = pi * 2 + cc
                ps, pk = self.psum("mm")
                self.mm_group(ps[:, 0:n], pk, [(wv[:, kc, cc * 128:(cc + 1) * 128], hb[:, kc, 0:n], [wk, hk(kc)]) for kc in range(8)])
                self.tick()
                self.qk_post(ps, pk, n, 244, not Gr.is_ctx, self.qb[:, cq, 0:n], [("qb", cq)])
        for pi in ([] if part == 0 else range(8)):
            w, wk = self.wget(f"in{5 + pi}")
            wv = w[:, :].rearrange("p (k f) -> p k f", f=256)
            for cc in range(2):
                j = pi * 2 + cc
                ps, pk = self.psum("mm")
                self.mm_group(ps[:, 0:n], pk, [(wv[:, kc, cc * 128:(cc + 1) * 128], hb[:, kc, 0:n], [wk, hk(kc)]) for kc in range(8)])
                sc.op("act", lambda e, ps=ps, j=j: e.activation(out=self.gates[:, j, 0:n], in_=ps[:, 0:n], func=AF.Sigmoid),
                      writes=[pk, ("big", j)])
                self.tick()

    def back_proj_b_steps(self, Gr, nxt):
        sc = self.sc
        s, n = Gr.slot, Gr.n
        hb = self.hb[s]
        hk = lambda kc: ("hb", s, kc)
        pT = self.pT
        st8 = {}

        def init():
            if Gr.first:
                sc.op("dve", lambda e: e.memset(pT[:, :, 0:8], 0.0), writes=["pT"])
            else:
                sc.op("dve", lambda e: e.tensor_copy(out=pT[:, :, 0:8], in_=pT[:, :, n:n + 8]), reads=[], writes=["pT"])
            if nxt is None:
                sc.op("dve", lambda e: e.memset(pT[:, :, 8 + n:16 + n], 0.0), writes=["pT"])
            else:
                st8["psh"] = self.psum("n")

        def chunk(c):
            pi, cc = c // 2, c % 2
            if cc == 0:
                st8["w"] = self.wget(f"in{3 + pi}")
            w, wk = st8["w"]
            wv = w[:, :].rearrange("p (k f) -> p k f", f=256)
            ps, pk = self.psum("pp")
            self.mm_group(ps[:, 0:n], pk, [(wv[:, kc, cc * 128:(cc + 1) * 128], hb[:, kc, 0:n], [wk, hk(kc)]) for kc in range(8)])
            sc.op("dve", lambda e: e.tensor_copy(out=pT[:, c, 8:8 + n], in_=ps[:, 0:n]), writes=[pk, "pT"])
            if nxt is not None:
                psh, pkh = st8["psh"]
                hbn = self.hb[nxt.slot]
                tok = None
                for kc in range(8):
                    st, sp_ = (kc == 0), (kc == 7)
                    tok = sc.op("pe", lambda e, kc=kc, st=st, sp_=sp_: e.matmul(
                        psh[:, c * 8:(c + 1) * 8], lhsT=wv[:, kc, cc * 128:(cc + 1) * 128], rhs=hbn[:, kc, 0:8], start=st, stop=sp_),
                        reads=[wk, ("hb", nxt.slot, kc)], writes=[pkh] if (c == 0 and kc == 0) else [], signal=sp_)
                sc.lastw[pkh] = tok

        def fin():
            if nxt is not None:
                psh, pkh = st8["psh"]
                sc.op("dve", lambda e: e.tensor_copy(out=pT[:, :, 8 + n:16 + n], in_=psh[:, 0:32].rearrange("p (c f) -> p c f", f=8)),
                      writes=[pkh, "pT"])

        return [init] + [(lambda c=c: chunk(c)) for c in range(4)] + [fin]

    def back_rest(self, Gr, nxt):
        self.flush()
        steps = self.back_proj_b_steps(Gr, nxt) + [lambda: self.poolmix(Gr)]
        spacing = 1 if Gr.is_ctx else 4
        for i, st in enumerate(steps):
            self.defer(1 + i * spacing, st)
        self.attention(Gr)
        self.flush()
        self.poolmix_pe(Gr)
        self.merge_out(Gr)

    def attention(self, Gr):
        sc = self.sc
        n, g = Gr.n, Gr.g
        tiles = [(0, 0, n, []), (1, 0, n, [])]
        if not Gr.is_ctx:
            for j in range(4 * g - 1, 4 * g + 5):
                if j < 0 or j >= SEQ // 128:
                    continue
                lo = max(j - 1, 4 * g)
                hi = min(j + 1, 4 * g + 3)
                masks = []
                for i in range(lo, hi + 1):
                    if i == j - 1:
                        masks.append(((i - lo) * 128, self.mask_next))
                    elif i == j + 1:
                        masks.append(((i - lo) * 128, self.mask_prev))
                tiles.append((mslot(j), (lo - 4 * g) * 128, (hi + 1 - 4 * g) * 128, masks))
        units = [(cq, half) for cq in range(4) for half in range(2)]
        esink, kesink = self.esink, self.kesink
        nt = len(tiles)
        seq = [(u, ti) for u in range(len(units)) for ti in range(nt)]
        psS_of = {}

        def emit_S(idx):
            u, ti = seq[idx]
            cq, half = units[u]
            slot, c0, c1, masks = tiles[ti]
            N = c1 - c0
            psS, pkS = self.psum("st")
            nmm = 1 + len(masks)
            tok = sc.op("pe", lambda e: e.matmul(psS[:, 0:N], lhsT=self.kT[half][:, slot * 128:(slot + 1) * 128], rhs=self.qb[:, cq, c0:c1],
                                                 start=True, stop=(nmm == 1)),
                        reads=[("kT", slot), ("qb", cq)], writes=[pkS], signal=(nmm == 1))
            for mi, (co, mk) in enumerate(masks):
                lastm = (mi == len(masks) - 1)
                tok = sc.op("pe", lambda e, co=co, mk=mk, lastm=lastm: e.matmul(psS[:, co:co + 128], lhsT=self.ident, rhs=mk, start=False, stop=lastm),
                            reads=["cmat"], writes=[], signal=lastm)
            sc.lastw[pkS] = tok
            psS_of[idx] = (psS, pkS)

        def emit_norm(u, psO, pkO):
            cq, half = units[u]
            hbp = half * 64
            ob = 64 - hbp
            h = cq + 4 * half
            sc.op("act", lambda e: e.activation(out=self.lnden[ob:ob + 64, 0:n], in_=psO[ob:ob + 64, 0:n], func=AF.Ln,
                                                bias=esink[ob:ob + 64, h:h + 1], scale=1.0),
                  reads=[kesink], writes=[pkO, "lnden"])
            sc.op("act", lambda e: e.activation(out=self.rden[hbp:hbp + 64, 0:n], in_=self.lnden[ob:ob + 64, 0:n], func=AF.Exp, scale=-1.0),
                  reads=["lnden"], writes=["rden"])
            sc.op("dve", lambda e: e.tensor_tensor(out=self.attn[hbp:hbp + 64, cq, 0:n], in0=psO[hbp:hbp + 64, 0:n],
                                                   in1=self.rden[hbp:hbp + 64, 0:n], op=ALU.mult),
                  reads=["rden"], writes=[pkO, ("attn", cq)])

        pending = None
        cur = None
        emit_S(0)
        emit_S(1)
        for idx in range(len(seq)):
            u, ti = seq[idx]
            cq, half = units[u]
            slot, c0, c1, masks = tiles[ti]
            N = c1 - c0
            if idx + 2 < len(seq):
                emit_S(idx + 2)
            if ti == 0:
                cur = self.psum("o")
            psO, pkO = cur
            psS, pkS = psS_of.pop(idx)
            pt = self.PT[self.ptr]
            ptk = ("PT", self.ptr)
            self.ptr = (self.ptr + 1) % len(self.PT)
            sc.op("act", lambda e, pt=pt, psS=psS, N=N: e.activation(out=pt[:, 0:N], in_=psS[:, 0:N], func=AF.Exp, scale=0.125),
                  writes=[pkS, ptk])
            st, sp_ = (ti == 0), (ti == nt - 1)
            tokO = sc.op("pe", lambda e, psO=psO, slot=slot, half=half, pt=pt, N=N, c0=c0, c1=c1, st=st, sp_=sp_: e.matmul(
                psO[:, c0:c1], lhsT=self.V[:, slot, half, :], rhs=pt[:, 0:N], start=st, stop=sp_),
                reads=[("V", slot), ptk], writes=[pkO] if ti == 0 else [], signal=sp_)
            self.tick()
            if ti == 1 and pending is not None:
                emit_norm(*pending)
                pending = None
            if ti == nt - 1:
                sc.lastw[pkO] = tokO
                pending = (u, psO, pkO)
        emit_norm(*pending)

    def poolmix(self, Gr):
        sc = self.sc
        n = Gr.n
        pT = self.pT
        A, B_, C8, S_ = self.f4
        fk = ["f4_0", "f4_1", "f4_2", "f4_3"]

        def add(out_ap, a, b, reads, writes):
            sc.op("dve", lambda e: e.tensor_tensor(out=out_ap, in0=a, in1=b, op=ALU.add), reads=reads, writes=writes)

        for c, w in enumerate((2, 4, 8, 16)):
            P = pT[:, c, :]
            if c == 0:
                add(S_[:, 0:n], P[:, 7:7 + n], P[:, 8:8 + n], ["pT"], [fk[3]])
            elif c == 1:
                add(A[:, 0:n + 2], P[:, 6:8 + n], P[:, 7:9 + n], ["pT"], [fk[0]])
                add(S_[:, 0:n], A[:, 0:n], A[:, 2:n + 2], [fk[0]], [fk[3]])
            elif c == 2:
                add(A[:, 0:n + 6], P[:, 4:10 + n], P[:, 5:11 + n], ["pT"], [fk[0]])
                add(B_[:, 0:n + 4], A[:, 0:n + 4], A[:, 2:n + 6], [fk[0]], [fk[1]])
                add(S_[:, 0:n], B_[:, 0:n], B_[:, 4:n + 4], [fk[1]], [fk[3]])
            else:
                add(A[:, 0:n + 14], P[:, 0:14 + n], P[:, 1:15 + n], ["pT"], [fk[0]])
                add(B_[:, 0:n + 12], A[:, 0:n + 12], A[:, 2:n + 14], [fk[0]], [fk[1]])
                add(C8[:, 0:n + 8], B_[:, 0:n + 8], B_[:, 4:n + 12], [fk[1]], [fk[2]])
                add(S_[:, 0:n], C8[:, 0:n], C8[:, 8:n + 8], [fk[2]], [fk[3]])
            sc.op("dve", lambda e, c=c, w=w, P=P: e.scalar_tensor_tensor(out=self.dT[:, c, 0:n], in0=S_[:, 0:n], scalar=1.0 / w, in1=P[:, 8:8 + n],
                                                                        op0=ALU.mult, op1=ALU.subtract),
                  reads=[fk[3], "pT"], writes=[("dT", c)])
            if Gr.first:
                sc.op("dve", lambda e, c=c: e.tensor_tensor(out=A[:, 0:8], in0=S_[:, 0:8], in1=self.poolc[:, c * 8:(c + 1) * 8], op=ALU.mult),
                      reads=[fk[3], "poolc"], writes=[fk[0]])
                sc.op("dve", lambda e, c=c, P=P: e.tensor_tensor(out=self.dT[:, c, 0:8], in0=A[:, 0:8], in1=P[:, 8:16], op=ALU.subtract),
                      reads=[fk[0], "pT"], writes=[("dT", c)])
            if Gr.last:
                sc.op("dve", lambda e, c=c: e.tensor_tensor(out=A[:, 0:8], in0=S_[:, n - 8:n], in1=self.poolc[:, 32 + c * 8:32 + (c + 1) * 8], op=ALU.mult),
                      reads=[fk[3], "poolc"], writes=[fk[0]])
                sc.op("dve", lambda e, c=c, P=P: e.tensor_tensor(out=self.dT[:, c, n - 8:n], in0=A[:, 0:8], in1=P[:, n:n + 8], op=ALU.subtract),
                      reads=[fk[0], "pT"], writes=[("dT", c)])

    def poolmix_pe(self, Gr):
        sc = self.sc
        n = Gr.n
        w, wk = self.wget("pool")
        wv = w[:, 0:512].rearrange("p (g d) -> p g d", d=128)
        for c in range(4):
            ps, pk = self.psum("mm")
            self.mm_group(ps[:, 0:n], pk, [(wv[:, c, :], self.dT[:, c, 0:n], [wk, ("dT", c)])])
            sc.op("act", lambda e, ps=ps, c=c, psc=self.pcol(64 + c, 1): e.activation(out=self.po[:, c, 0:n], in_=ps[:, 0:n], func=AF.Identity, scale=psc),
                  reads=["par"], writes=[pk, ("po", c)])

    def merge_out(self, Gr):
        sc = self.sc
        s, n = Gr.slot, Gr.n
        si = Gr.si
        xb = self.xb[Gr.xs]
        xks = [("xb", Gr.xs, kc) for kc in range(8)]
        hb = self.hb[s]
        psn, pkn = self.psum("n")

        def stat_mm(c):
            st, sp_ = (c == 0), (c == 7)
            return sc.op("pe", lambda e: e.matmul(psn[:, 0:n], lhsT=self.ones_mean, rhs=hb[:, c, 0:n], start=st, stop=sp_),
                         reads=[("hb", s, c), "cmat"], writes=[pkn] if c == 0 else [], signal=sp_)

        for hf in range(2):
            wa, wak = self.wget(f"bra{hf}")
            wp, wpk = self.wget(f"brp{hf}")
            wav = wa[:, :].rearrange("p (k f) -> p k f", f=512)
            wpv = wp[:, :].rearrange("p (k f) -> p k f", f=512)
            for cc in range(4):
                c = hf * 4 + cc
                psa, pka = self.psum("mm")
                self.mm_group(psa[:, 0:n], pka, [(wav[:, k, cc * 128:(cc + 1) * 128], self.attn[:, k, 0:n], [wak, ("attn", k)]) for k in range(4)])
                psp, pkp = self.psum("mm")
                self.mm_group(psp[:, 0:n], pkp, [(wpv[:, k, cc * 128:(cc + 1) * 128], self.po[:, k, 0:n], [wpk, ("po", k)]) for k in range(4)])
                m1, m1k = self.f4[(c % 2) * 2], f"f4_{(c % 2) * 2}"
                m2, m2k = self.f4[(c % 2) * 2 + 1], f"f4_{(c % 2) * 2 + 1}"
                sc.op("dve", lambda e, psa=psa, c=c, m1=m1: e.tensor_tensor(out=m1[:, 0:n], in0=psa[:, 0:n], in1=self.gates[:, c, 0:n], op=ALU.mult),
                      reads=[("big", c)], writes=[pka, m1k])
                sc.op("dve", lambda e, psp=psp, c=c, m2=m2: e.tensor_tensor(out=m2[:, 0:n], in0=psp[:, 0:n], in1=self.gates[:, 8 + c, 0:n], op=ALU.mult),
                      reads=[("big", 8 + c)], writes=[pkp, m2k])
                sc.op("pool", lambda e, c=c, m1=m1, m2=m2: e.tensor_tensor(out=self.yb[:, c, 0:n], in0=m1[:, 0:n], in1=m2[:, 0:n], op=ALU.add),
                      reads=[m1k, m2k], writes=[("big", 16 + c)])
        for pi in range(4):
            w, wk = self.wget(f"out{pi}")
            wv = w[:, :].rearrange("p (k f) -> p k f", f=256)
            for cc in range(2):
                c = pi * 2 + cc
                ps, pk = self.psum("mm")
                self.mm_group(ps[:, 0:n], pk, [(wv[:, k, cc * 128:(cc + 1) * 128], self.yb[:, k, 0:n], [wk, ("big", 16 + k)]) for k in range(8)])
                if c >= 1:
                    stat_mm(c - 1)
                g1 = self.mod[:, si, 16 + c:17 + c]
                sc.op("dve", lambda e, ps=ps, c=c, g1=g1: e.scalar_tensor_tensor(out=xb[:, c, 0:n], in0=ps[:, 0:n], scalar=g1,
                                                                                in1=xb[:, c, 0:n], op0=ALU.mult, op1=ALU.add),
                      reads=[self.kmod], writes=[pk, xks[c]])
                sc.op("act", lambda e, c=c: e.activation(out=hb[:, c, 0:n], in_=xb[:, c, 0:n], func=AF.Square), reads=[xks[c]], writes=[("hb", s, c)])
        tokn = stat_mm(7)
        sc.lastw[pkn] = tokn
        self.nstat = (psn, pkn)
        xl = self.xl[Gr.g % 2 if not Gr.is_ctx else 0]
        sc.op("dve", lambda e: e.tensor_copy(out=xl[:, :], in_=xb[:, :, n - 1]), reads=xks, writes=[("xl", Gr.g % 2 if not Gr.is_ctx else 0)])

    def ffn(self, Gr):
        sc = self.sc
        s, n, si = Gr.slot, Gr.n, Gr.si
        xb = self.xb[Gr.xs]
        xks = [("xb", Gr.xs, kc) for kc in range(8)]
        hb = self.hb[s]
        hk = lambda kc: ("hb", s, kc)
        self.norm_mod(Gr, self.gs2, self.kgs2, 24, stats_done=True)
        if Gr.first:
            sc.op("dve", lambda e: e.memset(self.saved[:, :, :], 0.0), writes=["saved"])
        w0T, w1T = self.pcol(68, 44), self.pcol(68 + 44, 44)
        corr = self.corr
        sc.op("dve", lambda e: e.tensor_tensor(out=corr[:, :, 1], in0=self.saved[:, :, 1], in1=w0T, op=ALU.mult), reads=["saved", "par"], writes=["corr"])
        sc.op("dve", lambda e: e.tensor_tensor(out=corr[:, :, 0], in0=self.saved[:, :, 0], in1=w0T, op=ALU.mult), reads=["saved", "par"], writes=["corr"])
        sc.op("dve", lambda e: e.tensor_tensor(out=self.tail_a[:, :], in0=self.saved[:, :, 1], in1=w1T, op=ALU.mult), reads=["saved", "par"], writes=["tail_a"])
        sc.op("dve", lambda e: e.tensor_tensor(out=corr[:, :, 0], in0=corr[:, :, 0], in1=self.tail_a[:, :], op=ALU.add), reads=["tail_a"], writes=["corr"])
        bigkeys = [("big", j) for j in range(24)]
        for i in range(NFC):
            w, wk = self.wget(f"up{i}")
            wv = w[:, :].rearrange("p (k f) -> p k f", f=256)
            accs = []
            for ab in range(2):
                ch = i + ab * NFC
                ps, pk = self.psum("up")
                self.mm_group(ps[:, 0:n], pk, [(wv[:, kc, ab * 128:(ab + 1) * 128], hb[:, kc, 0:n], [wk, hk(kc)]) for kc in range(8)])
                acc = self.f4[(i % 2) * 2 + ab]
                ak = f"f4_{(i % 2) * 2 + ab}"
                w0, w1, w2, bb = self.pcol(68 + ch), self.pcol(68 + 44 + ch), self.pcol(68 + 88 + ch), self.pcol(200 + ch)
                sv = self.saved[:, ch, :]
                sc.op("act", lambda e, ps=ps, acc=acc, w2=w2, bb=bb: e.activation(out=acc[:, 0:n], in_=ps[:, 0:n], func=AF.Identity, bias=bb, scale=w2),
                      reads=["par"], writes=[pk, ak])
                sc.op("act", lambda e, ps=ps, sv=sv: e.activation(out=sv[:, 0:2], in_=ps[:, n - 2:n], func=AF.Identity), reads=["corr"], writes=[pk, "saved"])
                sc.op("dve", lambda e, ps=ps, acc=acc, w1=w1: e.scalar_tensor_tensor(out=acc[:, 1:n], in0=ps[:, 0:n - 1], scalar=w1, in1=acc[:, 1:n],
                                                                                    op0=ALU.mult, op1=ALU.add),
                      reads=["par"], writes=[pk, ak])
                sc.op("dve", lambda e, ps=ps, acc=acc, w0=w0: e.scalar_tensor_tensor(out=acc[:, 2:n], in0=ps[:, 0:n - 2], scalar=w0, in1=acc[:, 2:n],
                                                                                    op0=ALU.mult, op1=ALU.add),
                      reads=["par"], writes=[pk, ak])
                sc.op("dve", lambda e, acc=acc, ch=ch: e.tensor_tensor(out=acc[:, 0:2], in0=acc[:, 0:2], in1=corr[:, ch, :], op=ALU.add),
                      reads=["corr"], writes=[ak])
                accs.append((acc, ak))
            (aa, aak), (ab_, abk) = accs
            sil, silk = self.sil2[i % 2], ("sil", i % 2)
            sc.op("act", lambda e, aa=aa, sil=sil: e.activation(out=sil[:, 0:n], in_=aa[:, 0:n], func=AF.Silu), reads=[aak], writes=[silk])
            sc.op("pool", lambda e, ab_=ab_, i=i, sil=sil: e.tensor_tensor(out=self.actT[:, i, 0:n], in0=sil[:, 0:n], in1=ab_[:, 0:n], op=ALU.mult),
                  reads=[silk, abk], writes=bigkeys if i == 0 else [("act", i)])
        actkeys = bigkeys + [("act", i) for i in range(1, NFC)]
        c_first = 1 if Gr.first else 0
        xlp = self.xl[(Gr.g - 1) % 2]
        xlpk = ("xl", (Gr.g - 1) % 2)
        hbf = hb[:, :, :].rearrange("p c t -> p (c t)").bitcast(F32).rearrange("p (c t) -> p c t", t=G)
        if Gr.last:
            self.tail_prep()
            pst, pkt = self.psum("mm")
            tokt = None
        attn_f = self.attn[:, :, :].rearrange("p c t -> p (c t)").bitcast(F32).rearrange("p (c t) -> p c t", t=G)
        dT_f = self.dT[:, :, :].rearrange("p c t -> p (c t)").bitcast(F32).rearrange("p (c t) -> p c t", t=G)
        KA = NFC // 2
        for fp in range(4):
            wd = [self.wget(f"dn{fp}_{q}") for q in range(3)]

            def wv_of(kc):
                w, wk = wd[kc // 8]
                return w[:, 0:(8 if kc < 16 else 6) * 256].rearrange("p (k f) -> p k f", f=256), wk

            if Gr.last:
                for cc in range(2):
                    c = fp * 2 + cc
                    for kc in range(NFC):
                        wv, wk = wv_of(kc)
                        st, sp_ = (kc == 0), (kc == NFC - 1)
                        tokt = sc.op("pe", lambda e, c=c, kc=kc, wv=wv, cc=cc, st=st, sp_=sp_: e.matmul(
                            pst[:, 2 * c:2 * c + 1], lhsT=wv[:, kc % 8, cc * 128:(cc + 1) * 128], rhs=self.tail_act[:, kc:kc + 1], start=st, stop=sp_),
                            reads=[wk, "tail_act"], writes=[pkt] if (c == 0 and kc == 0) else [], signal=sp_)
            banks = [self.psum("aux") for _ in range(2)]
            toks = [None, None]
            for (k0, k1) in ((0, KA), (KA, NFC)):
                for cc in range(2):
                    ps, pk = banks[cc]
                    for kc in range(k0, k1):
                        wv, wk = wv_of(kc)
                        st, sp_ = (kc == 0), (kc == NFC - 1)
                        toks[cc] = sc.op("pe", lambda e, ps=ps, kc=kc, wv=wv, cc=cc, st=st, sp_=sp_: e.matmul(
                            ps[:, 0:n], lhsT=wv[:, kc % 8, cc * 128:(cc + 1) * 128], rhs=self.actT[:, kc, 0:n], start=st, stop=sp_),
                            reads=[wk] + (bigkeys if kc == 0 else [("act", kc)]), writes=[pk] if kc == 0 else [], signal=sp_)
            for cc in range(2):
                c = fp * 2 + cc
                ps, pk = banks[cc]
                sc.lastw[pk] = toks[cc]
                g2 = self.mod[:, si, 40 + c:41 + c]
                if c < 4:
                    xo, xok = self.f4[c][:, 0:n - 1], [f"f4_{c}"]
                elif c < 6:
                    xo, xok = attn_f[:, c - 4, 0:n - 1], [("attn", 2 * (c - 4)), ("attn", 2 * (c - 4) + 1)]
                else:
                    xo, xok = dT_f[:, c - 6, 0:n - 1], [("dT", 2 * (c - 6)), ("dT", 2 * (c - 6) + 1)]
                sc.op("dve", lambda e, ps=ps, c=c, g2=g2, xo=xo: e.scalar_tensor_tensor(out=xo, in0=ps[:, 1:n], scalar=g2, in1=xb[:, c, 0:n - 1],
                                                                                       op0=ALU.mult, op1=ALU.add),
                      reads=[self.kmod, xks[c]], writes=[pk] + xok)
                if not Gr.first:
                    sc.op("dve", lambda e, ps=ps, c=c, g2=g2: e.scalar_tensor_tensor(out=xlp[:, c:c + 1], in0=ps[:, 0:1], scalar=g2, in1=xlp[:, c:c + 1],
                                                                                    op0=ALU.mult, op1=ALU.add),
                          reads=[self.kmod], writes=[pk, xlpk])
        dst = (self.dst_c if Gr.is_ctx else self.dst_x).rearrange("(k p) t -> p k t", p=128)
        t0 = Gr.t0
        dkey = (self.xkey_dst, Gr.name, "t")
        for c in range(4):
            sc.dma("sp", dst[:, c, t0:t0 + n - 1], self.f4[c][:, 0:n - 1], reads=[f"f4_{c}"], writes=[(self.xkey_dst, Gr.name, f"m{c}")])
        sc.dma("sp", dst[:, 4:6, t0:t0 + n - 1], attn_f[:, :, 0:n - 1], reads=[("attn", k) for k in range(4)], writes=[(self.xkey_dst, Gr.name, "m4")])
        sc.dma("sp", dst[:, 6:8, t0:t0 + n - 1], dT_f[:, :, 0:n - 1], reads=[("dT", k) for k in range(4)], writes=[(self.xkey_dst, Gr.name, "m5")])
        if not Gr.first:
            pkey = (self.xkey_dst, str(Gr.g - 1), "e")
            sc.dma("sp", dst[:, :, t0 - 1:t0], xlp[:, :].rearrange("p (k o) -> p k o", o=1), reads=[xlpk], writes=[pkey], slow=True)
        if Gr.last:
            sc.lastw[pkt] = tokt
            self.tail_finish(Gr, dst, dkey, pst, pkt)

    def tail_prep(self):
        sc = self.sc
        w0, w1, bb = self.pcol(68, 44), self.pcol(68 + 44, 44), self.pcol(200, 44)
        ta = self.tail_a
        sc.op("dve", lambda e: e.tensor_tensor(out=ta[:, :], in0=self.saved[:, :, 0], in1=w0, op=ALU.mult), reads=["saved", "par"], writes=["tail_a"])
        sc.op("dve", lambda e: e.tensor_tensor(out=self.tail_s[:, :], in0=self.saved[:, 0:NFC, 1], in1=w1[:, 0:NFC], op=ALU.mult), reads=["saved", "par"], writes=["tail_s"])
        sc.op("dve", lambda e: e.tensor_tensor(out=ta[:, 0:NFC], in0=ta[:, 0:NFC], in1=self.tail_s[:, :], op=ALU.add), reads=["tail_s"], writes=["tail_a"])
        sc.op("dve", lambda e: e.tensor_tensor(out=self.tail_s[:, :], in0=self.saved[:, NFC:2 * NFC, 1], in1=w1[:, NFC:2 * NFC], op=ALU.mult), reads=["saved", "par"], writes=["tail_s"])
        sc.op("dve", lambda e: e.tensor_tensor(out=ta[:, NFC:2 * NFC], in0=ta[:, NFC:2 * NFC], in1=self.tail_s[:, :], op=ALU.add), reads=["tail_s"], writes=["tail_a"])
        sc.op("dve", lambda e: e.tensor_tensor(out=ta[:, :], in0=ta[:, :], in1=bb, op=ALU.add), reads=["par"], writes=["tail_a"])
        sc.op("act", lambda e: e.activation(out=self.tail_s[:, :], in_=ta[:, 0:NFC], func=AF.Silu), reads=["tail_a"], writes=["tail_s"])
        sc.op("dve", lambda e: e.tensor_tensor(out=self.tail_act[:, :], in0=self.tail_s[:, :], in1=ta[:, NFC:2 * NFC], op=ALU.mult),
              reads=["tail_s", "tail_a"], writes=["tail_act"])

    def tail_finish(self, Gr, dst, dkey, ps, pk):
        sc = self.sc
        n, si = Gr.n, Gr.si
        xi = Gr.g % 2 if not Gr.is_ctx else 0
        xl, xlk = self.xl[xi], ("xl", xi)
        g2all = self.mod[:, si, 40:48]
        sc.op("dve", lambda e: e.tensor_tensor(out=self.tail_a[:, 0:8], in0=ps[:, 0:16:2], in1=g2all, op=ALU.mult),
              reads=[self.kmod], writes=[pk, "tail_a"])
        sc.op("dve", lambda e: e.tensor_tensor(out=xl[:, :], in0=xl[:, :], in1=self.tail_a[:, 0:8], op=ALU.add), reads=["tail_a"], writes=[xlk])
        t_last = Gr.t0 + n - 1
        sc.dma("sp", dst[:, :, t_last:t_last + 1], xl[:, :].rearrange("p (k o) -> p k o", o=1), reads=[xlk], writes=[dkey], slow=True)


def build_nc(n_layers=DEPTH, dbg=False):
    dry = Builder(n_layers, dbg, plan=None)
    dry.build()
    return Builder(n_layers, dbg, plan=dry.plan).build()


_CACHE = {}


def prepare_inputs(x, c, ctx, c_ctx, w_mod, b_mod, norm1_g, norm2_g, w_in, q_gain, k_gain, sink,
                   w_pool, pool_scale, w_br_attn, w_br_pool, w_out, w_up, conv_w, conv_b, w_down, n_layers=DEPTH):
    f = lambda a: np.asarray(a, dtype=np.float32)
    x, c, ctx, c_ctx = f(x), f(c), f(ctx), f(c_ctx)
    w_mod, b_mod, norm1_g, norm2_g, w_in = f(w_mod), f(b_mod), f(norm1_g), f(norm2_g), f(w_in)
    q_gain, k_gain, sink, w_pool, pool_scale = f(q_gain), f(k_gain), f(sink), f(w_pool), f(pool_scale)
    w_br_attn, w_br_pool, w_out, w_up, conv_w, conv_b, w_down = f(w_br_attn), f(w_br_pool), f(w_out), f(w_up), f(conv_w), f(conv_b), f(w_down)
    cm, cosT, sinT, pc = build_consts()
    params = build_params(b_mod, norm1_g, norm2_g, pool_scale, conv_w, conv_b, q_gain, k_gain, sink)
    shared = {"params": params, "cmat": cm, "cosT": cosT, "sinT": sinT, "poolc": pc}
    for l in range(n_layers):
        shared[f"wst{l}"] = build_wstream(l, w_in, w_pool, w_br_attn, w_br_pool, w_out, w_up, w_down)
        shared[f"wmod{l}"] = build_wmod_stream(l, w_mod)
    in_maps = []
    for b in range(NCORES):
        m = dict(shared)
        m["xT"] = np.ascontiguousarray(x[b].T)
        m["cxT"] = np.ascontiguousarray(ctx[b].T)
        cond = np.zeros((128, 16), np.float32)
        cond[:, 0:8] = c[b].reshape(8, 128).T
        cond[:, 8:16] = c_ctx.reshape(8, 128).T
        m["cond"] = cond
        in_maps.append(m)
    return in_maps


def kernel(**inputs):
    in_maps = prepare_inputs(**inputs)
    if "nc" not in _CACHE:
        _CACHE["nc"] = build_nc(DEPTH)
    nc = _CACHE["nc"]
    res = run_bass_kernel_spmd(nc, in_maps, core_ids=list(range(NCORES)))
    out = np.stack([np.ascontiguousarray(r["outT"].T) for r in res.results], axis=0)
    return out.astype(np.float32)
```
